# Optimizing a Trainium2 kernel written in Bass

```python
import jax, jax.numpy as jnp
from jax import lax
import numpy as np

D_MODEL = 2048
BATCH = 4
SEQ = 4096
DEPTH = 4

GRID_W = 64
CTX_LEN = 256
D_MIX = D_MODEL
HG_WIDTH = D_MIX // 2
HG_HEADS = 8
HG_DK = HG_WIDTH // HG_HEADS
HG_DV = HG_WIDTH // HG_HEADS
HG_CHUNK = 64
RG_WIDTH = D_MIX - HG_WIDTH
RG_BLOCKS = 8
RG_BW = RG_WIDTH // RG_BLOCKS
RG_CONV = 4
RG_C = 8.0
FFN_CONV = 3
D_FF = 11 * D_MODEL // 4
N_PROJ = 5 * HG_WIDTH + 2 * RG_WIDTH
EPS = 1e-6

kernel_name = "hybrid_hgrn2_rglru_dit_trunk"


def rmsnorm(x, g):
    xf = x.astype(jnp.float32)
    y = xf * lax.rsqrt(jnp.mean(xf * xf, axis=-1, keepdims=True) + EPS)
    return (y * g.astype(jnp.float32)).astype(x.dtype)


def modulate(h, shift, scale):
    return h * (1.0 + scale) + shift


def dwconv(x, w, b, pad_left, pad_right):
    y = lax.conv_general_dilated(x, w[:, None, :].astype(x.dtype), window_strides=(1,),
                                 padding=[(pad_left, pad_right)],
                                 dimension_numbers=('NWC', 'WIO', 'NWC'),
                                 feature_group_count=x.shape[-1])
    return y + b


def grid_to_colmajor(t, rows):
    B, L = t.shape[:2]
    rest = t.shape[2:]
    return t.reshape(B, rows, GRID_W, *rest).swapaxes(1, 2).reshape(B, L, *rest)


def colmajor_to_grid(t, rows):
    B, L = t.shape[:2]
    rest = t.shape[2:]
    return t.reshape(B, GRID_W, rows, *rest).swapaxes(1, 2).reshape(B, L, *rest)


def split_proj(p):
    points = [HG_WIDTH, 2 * HG_WIDTH, 3 * HG_WIDTH, 4 * HG_WIDTH, 5 * HG_WIDTH,
              5 * HG_WIDTH + RG_WIDTH]
    return jnp.split(p, points, axis=-1)


def hgrn2_scan(q, k, v, log_f, s0):
    B, L, H, _ = q.shape
    DV = v.shape[-1]
    n = L // HG_CHUNK

    def chunks(t):
        return t.astype(jnp.float32).reshape(B, n, HG_CHUNK, H, t.shape[-1]).transpose(1, 0, 3, 2, 4)

    mask = jnp.tril(jnp.ones((HG_CHUNK, HG_CHUNK), dtype=bool))[:, :, None]

    def step(S, inp):
        qc, kc, vc, gc = inp
        b = jnp.cumsum(gc, axis=2)
        diff = b[:, :, :, None, :] - b[:, :, None, :, :]
        decay = jnp.where(mask, jnp.exp(jnp.where(mask, diff, 0.0)), 0.0)
        scores = jnp.einsum('bhtd,bhsd,bhtsd->bhts', qc, kc, decay)
        o = (jnp.einsum('bhts,bhse->bhte', scores, vc)
             + jnp.einsum('bhtd,bhde->bhte', qc * jnp.exp(b), S))
        b_last = b[:, :, -1:, :]
        S_new = (jnp.exp(b_last[:, :, 0, :])[..., None] * S
                 + jnp.einsum('bhsd,bhse->bhde', kc * jnp.exp(b_last - b), vc))
        return S_new, o

    S, o = lax.scan(step, s0, (chunks(q), chunks(k), chunks(v), chunks(log_f)))
    o = o.transpose(1, 0, 3, 2, 4).reshape(B, L, H, DV)
    return o.astype(v.dtype), S


def hgrn2_dir(q, z, v, lb, s0):
    B, L, _ = q.shape
    zf = z.astype(jnp.float32)
    log_f = jnp.log(lb + (1.0 - lb) * jax.nn.sigmoid(zf))
    k = (1.0 - lb) * jax.nn.sigmoid(-zf)
    heads = lambda t, d: t.reshape(B, L, HG_HEADS, d)
    return hgrn2_scan(heads(q, HG_DK), heads(k, HG_DK), heads(v, HG_DV), heads(log_f, HG_DK), s0)


def hgrn2_bidirectional(q, zf, zb, v, lb, s0_f, s0_b):
    o_f, s_f = hgrn2_dir(q, zf, v, lb[0], s0_f)
    o_b, s_b = hgrn2_dir(q[:, ::-1], zb[:, ::-1], v[:, ::-1], lb[1], s0_b)
    return o_f + o_b[:, ::-1], s_f, s_b


def hgrn2_readout(o, g, hg_g):
    B, L = o.shape[:2]
    o = rmsnorm(o, hg_g.reshape(HG_HEADS, HG_DV)).reshape(B, L, HG_WIDTH)
    return o * jax.nn.silu(g)


def linear_scan(a, b, h0):
    b = b.at[:, 0].add(a[:, 0] * h0)

    def combine(e1, e2):
        a1, b1 = e1
        a2, b2 = e2
        return a1 * a2, a2 * b1 + b2

    _, h = lax.associative_scan(combine, (a, b), axis=1)
    return h


def rglru_dir(xc, wa, ba, wx, bx, lam, h0):
    B, L, C = xc.shape
    xb = xc.reshape(B, L, RG_BLOCKS, RG_BW)
    r = jax.nn.sigmoid(jnp.einsum('blnc,ncd->blnd', xb, wa).reshape(B, L, C) + ba)
    i = jax.nn.sigmoid(jnp.einsum('blnc,ncd->blnd', xb, wx).reshape(B, L, C) + bx)
    log_a = -RG_C * r.astype(jnp.float32) * jax.nn.softplus(-lam.astype(jnp.float32))
    a = jnp.exp(log_a)
    mult = jnp.sqrt(-jnp.expm1(2.0 * log_a))
    bterm = mult * i.astype(jnp.float32) * xc.astype(jnp.float32)
    h = linear_scan(a, bterm, h0)
    return h.astype(xc.dtype), h[:, -1]


def rglru_bidirectional(xc, rg, h0_f, h0_b):
    wa, ba, wx, bx, lam = rg
    h_f, s_f = rglru_dir(xc, wa[0], ba[0], wx[0], bx[0], lam[0], h0_f)
    h_b, s_b = rglru_dir(xc[:, ::-1], wa[1], ba[1], wx[1], bx[1], lam[1], h0_b)
    return h_f + h_b[:, ::-1], s_f, s_b


def hybrid_mixer(h_lat, h_ctx, rows, w_in, lb, hg_g, rg_cw, rg_cb, rg, w_out, ctx_out):
    B = h_lat.shape[0]
    q_l, zf_l, zb_l, v_l, g_l, rx_l, rgate_l = split_proj(h_lat @ w_in)
    q_c, zf_c, zb_c, v_c, g_c, rx_c, rgate_c = split_proj(h_ctx @ w_in)
    q_l = jax.nn.silu(q_l)
    q_c = jax.nn.silu(q_c)

    zero_s = jnp.zeros((B, HG_HEADS, HG_DK, HG_DV), jnp.float32)
    o_c, s_f, s_b = hgrn2_bidirectional(q_c, zf_c, zb_c, v_c, lb, zero_s, zero_s)
    cm = lambda t: grid_to_colmajor(t, rows)
    o_l, _, _ = hgrn2_bidirectional(cm(q_l), cm(zf_l), cm(zb_l), cm(v_l), lb, s_f, s_b)
    o_l = colmajor_to_grid(o_l, rows)
    hg_l = hgrn2_readout(o_l, g_l, hg_g)

    zero_h = jnp.zeros((B, RG_WIDTH), jnp.float32)
    y_c, h_f, h_b = rglru_bidirectional(dwconv(rx_c, rg_cw, rg_cb, 2, 1), rg, zero_h, zero_h)
    y_l, _, _ = rglru_bidirectional(dwconv(rx_l, rg_cw, rg_cb, 2, 1), rg, h_f, h_b)
    rg_l = y_l * jax.nn.gelu(rgate_l)

    out_l = jnp.concatenate([hg_l, rg_l], axis=-1) @ w_out
    if not ctx_out:
        return out_l, None
    out_c = jnp.concatenate([hgrn2_readout(o_c, g_c, hg_g), y_c * jax.nn.gelu(rgate_c)], axis=-1) @ w_out
    return out_l, out_c


def conv_ffn(h, w_up, cw, cb, w_down):
    u = dwconv(h @ w_up, cw, cb, 1, 1)
    a, v = jnp.split(u, 2, axis=-1)
    return (jax.nn.gelu(a) * v) @ w_down


def setup_inputs(seed: int = 0) -> dict:
    key = jax.random.key(seed)
    ks = jax.random.split(key, 24)
    f32 = jnp.float32
    nrm = lambda k, shape, s: jax.random.normal(k, shape, f32) * s
    a0 = jax.random.uniform(ks[16], (DEPTH, 2, RG_WIDTH), f32, 0.9, 0.999)
    return {
        "x": nrm(ks[0], (BATCH, SEQ, D_MODEL), 1.0),
        "c": nrm(ks[1], (BATCH, D_MODEL), 1.0),
        "ctx": nrm(ks[2], (BATCH, CTX_LEN, D_MODEL), 1.0),
        "c_ctx": nrm(ks[3], (D_MODEL,), 1.0),
        "w_mod": nrm(ks[4], (DEPTH, D_MODEL, 6 * D_MODEL), 0.5 * D_MODEL ** -0.5),
        "b_mod": nrm(ks[5], (DEPTH, 6 * D_MODEL), 0.02),
        "norm_g": 1.0 + nrm(ks[6], (DEPTH, 4, D_MODEL), 0.05),
        "w_in": nrm(ks[7], (DEPTH, D_MODEL, N_PROJ), D_MODEL ** -0.5),
        "hg_lower_bounds": nrm(ks[8], (DEPTH, 2, HG_WIDTH), 0.1),
        "hg_norm_g": 1.0 + nrm(ks[9], (DEPTH, HG_WIDTH), 0.05),
        "rg_conv_w": nrm(ks[10], (DEPTH, RG_CONV, RG_WIDTH), RG_CONV ** -0.5),
        "rg_conv_b": nrm(ks[11], (DEPTH, RG_WIDTH), 0.02),
        "rg_wa": nrm(ks[12], (DEPTH, 2, RG_BLOCKS, RG_BW, RG_BW), RG_BW ** -0.5),
        "rg_ba": nrm(ks[13], (DEPTH, 2, RG_WIDTH), 0.1),
        "rg_wx": nrm(ks[14], (DEPTH, 2, RG_BLOCKS, RG_BW, RG_BW), RG_BW ** -0.5),
        "rg_bx": nrm(ks[15], (DEPTH, 2, RG_WIDTH), 0.1),
        "rg_lambda": jnp.log(a0) - jnp.log1p(-a0),
        "w_out": nrm(ks[17], (DEPTH, D_MIX, D_MODEL), D_MIX ** -0.5),
        "ffn_w_up": nrm(ks[18], (DEPTH, D_MODEL, 2 * D_FF), D_MODEL ** -0.5),
        "ffn_conv_w": nrm(ks[19], (DEPTH, FFN_CONV, 2 * D_FF), FFN_CONV ** -0.5),
        "ffn_conv_b": nrm(ks[20], (DEPTH, 2 * D_FF), 0.02),
        "ffn_w_down": nrm(ks[21], (DEPTH, D_FF, D_MODEL), D_FF ** -0.5),
    }


def reference(x, c, ctx, c_ctx, w_mod, b_mod, norm_g, w_in, hg_lower_bounds, hg_norm_g,
              rg_conv_w, rg_conv_b, rg_wa, rg_ba, rg_wx, rg_bx, rg_lambda, w_out,
              ffn_w_up, ffn_conv_w, ffn_conv_b, ffn_w_down):
    rows = x.shape[1] // GRID_W
    lb_all = jnp.cumsum(jax.nn.softmax(hg_lower_bounds.astype(jnp.float32), axis=0), axis=0)
    lb_all = lb_all - lb_all[0]
    for l in range(DEPTH):
        ctx_out = l < DEPTH - 1
        mod_l = jax.nn.silu(c) @ w_mod[l] + b_mod[l]
        mod_c = jax.nn.silu(c_ctx) @ w_mod[l] + b_mod[l]
        sh1, sc1, gt1, sh2, sc2, gt2 = jnp.split(mod_l[:, None, :], 6, axis=-1)
        csh1, csc1, cgt1, csh2, csc2, cgt2 = jnp.split(mod_c, 6, axis=-1)

        h_l = modulate(rmsnorm(x, norm_g[l, 0]), sh1, sc1)
        h_c = modulate(rmsnorm(ctx, norm_g[l, 0]), csh1, csc1)
        rg = (rg_wa[l], rg_ba[l], rg_wx[l], rg_bx[l], rg_lambda[l])
        mix_l, mix_c = hybrid_mixer(h_l, h_c, rows, w_in[l], lb_all[l], hg_norm_g[l],
                                    rg_conv_w[l], rg_conv_b[l], rg, w_out[l], ctx_out)
        x = x + gt1 * rmsnorm(mix_l, norm_g[l, 1])

        f_l = conv_ffn(modulate(rmsnorm(x, norm_g[l, 2]), sh2, sc2),
                       ffn_w_up[l], ffn_conv_w[l], ffn_conv_b[l], ffn_w_down[l])
        x = x + gt2 * rmsnorm(f_l, norm_g[l, 3])

        if ctx_out:
            ctx = ctx + cgt1 * rmsnorm(mix_c, norm_g[l, 1])
            f_c = conv_ffn(modulate(rmsnorm(ctx, norm_g[l, 2]), csh2, csc2),
                           ffn_w_up[l], ffn_conv_w[l], ffn_conv_b[l], ffn_w_down[l])
            ctx = ctx + cgt2 * rmsnorm(f_c, norm_g[l, 3])
    return x
```

```python
import contextlib
import numpy as np
import concourse.bass as bass
import concourse.mybir as mybir
from concourse.bass_utils import run_bass_kernel_spmd

F32 = mybir.dt.float32
BF16 = mybir.dt.bfloat16
I32 = mybir.dt.int32
AF = mybir.ActivationFunctionType
ALU = mybir.AluOpType

D = 2048
TL = 4096
TC = 256
T = TL + TC
NL = 4
NCH = 68
DFF = 5632
EPS = 1e-6
SAME_ENGINE_SYNC = True
N_DMA_SEMS = 8
import os as _os
SEM_RESET_AT = int(_os.environ.get('MK_RESET', '1000000000'))

VO = {}
_o = 0
for _n, _sz in [("b_mod", 4 * 96), ("norm_g", 4 * 4 * 16), ("lb", 4 * 2 * 8), ("hgg", 4 * 8),
                ("rcw", 4 * 4 * 8), ("rcb", 4 * 8), ("rba", 4 * 2 * 8), ("rbx", 4 * 2 * 8), ("rlam", 4 * 2 * 8),
                ("fcw", 4 * 3 * 88), ("fcb", 4 * 88)]:
    VO[_n] = _o
    _o += _sz
NV = _o


_UC = [0]


def _u():
    _UC[0] += 1
    return "t%d_" % _UC[0]


_ALL_BUFS = []


class Buf:
    __slots__ = ("name", "w", "r")

    def __init__(self, name=""):
        self.name = name
        self.w = {}
        self.r = {}
        _ALL_BUFS.append(self)


class Sched:
    ENGS = ("pe", "dve", "act", "pool", "sp")

    def __init__(self, nc, sems, dma_sems):
        self.nc = nc
        self.sem = dict(sems)
        for i, s in enumerate(dma_sems):
            self.sem[("dma", i)] = s
        self.n_dma = len(dma_sems)
        self.prog = {e: [] for e in self.ENGS}
        self.cnt = {k: 0 for k in self.sem}
        self.known = {e: {} for e in self.ENGS}
        self.dma_rr = 0
        self.ninst = 0

    def _waits(self, e, reads, writes, extra=()):
        need = {}
        for b in reads:
            for k, v in b.w.items():
                if need.get(k, 0) < v:
                    need[k] = v
        for b in writes:
            for k, v in b.w.items():
                if need.get(k, 0) < v:
                    need[k] = v
            for k, v in b.r.items():
                if need.get(k, 0) < v:
                    need[k] = v
        for k, v in extra:
            if need.get(k, 0) < v:
                need[k] = v
        out = []
        kn = self.known[e]
        for k, v in need.items():
            if k == e and (not SAME_ENGINE_SYNC or e == "pe"):
                continue
            if kn.get(k, 0) >= v:
                continue
            kn[k] = v
            out.append((k, v))
        return out

    def op(self, e, fn, reads=(), writes=()):
        waits = self._waits(e, reads, writes)
        self.cnt[e] += 1
        t = self.cnt[e]
        self.prog[e].append((waits, fn, e, 1))
        for b in reads:
            b.r[e] = t
        for b in writes:
            b.w = {e: t}
            b.r = {}
        self.ninst += 1

    def dma(self, q, fn, reads=(), writes=()):
        i = self.dma_rr
        self.dma_rr = (i + 1) % self.n_dma
        k = ("dma", i)
        extra = [(k, self.cnt[k])] if self.cnt[k] > 0 else []
        waits = self._waits(q, reads, writes, extra)
        self.cnt[k] += 16
        t = self.cnt[k]
        self.prog[q].append((waits, fn, k, 16))
        for b in reads:
            b.r[k] = t
        for b in writes:
            b.w = {k: t}
            b.r = {}
        self.ninst += 1

    def reset_counts(self):
        for k in self.cnt:
            self.cnt[k] = 0
        self.known = {e: {} for e in self.ENGS}
        for b in _ALL_BUFS:
            b.w = {}
            b.r = {}

    def barrier(self):
        for e in self.ENGS:
            waits = []
            kn = self.known[e]
            for k, v in self.cnt.items():
                if v > 0 and k != e and kn.get(k, 0) < v:
                    kn[k] = v
                    waits.append((k, v))
            if e != "pe" and self.cnt[e] > 0 and kn.get(e, 0) < self.cnt[e]:
                kn[e] = self.cnt[e]
                waits.append((e, self.cnt[e]))
            if waits:
                self.prog[e].append((waits, None, None, 0))

    def emit(self, block):
        engmap = {"pe": block.tensor, "dve": block.vector, "act": block.scalar,
                  "pool": block.gpsimd, "sp": block.sync}
        sem = self.sem
        for e in self.ENGS:
            prog = self.prog[e]
            if not prog:
                continue

            def body(eng, prog=prog):
                for waits, fn, k, inc in prog:
                    for wk, wv in waits:
                        eng.wait_ge(sem[wk], wv)
                    if fn is not None:
                        fn(eng).then_inc(sem[k], inc)
            engmap[e](body)
            self.prog[e] = []


class Ctx:
    pass


def build(nlayers=NL, taps=()):
    nc = bass.Bass("TRN2", target_bir_lowering=False)
    g = Ctx()
    g.nc = nc
    dt_in = lambda n, s: nc.dram_tensor(n, s, F32, kind="ExternalInput").ap()
    g.res0 = dt_in("res0", [D, T])
    g.cvec = dt_in("cvec", [128, 16, 2])
    g.vec = dt_in("vec", [128, NV])
    g.rgw = dt_in("rgw", [NL, 128, 4096])
    g.w_mod = dt_in("w_mod", [NL, D, 6 * D])
    g.w_hg = dt_in("w_hg", [NL, 8, D, 640])
    g.w_rg = dt_in("w_rg", [NL, 8, D, 256])
    g.w_out = dt_in("w_out", [NL, D, D])
    g.w_up = dt_in("w_up", [NL, 44, D, 256])
    g.w_dn = dt_in("w_dn", [NL, 16, DFF, 128])
    g.y = nc.dram_tensor("y", [D, TL], F32, kind="ExternalOutput").ap()
    g.res = nc.dram_tensor("res", [D, T], F32, kind="Internal").ap()
    g.hT = nc.dram_tensor("hT", [D, T], BF16, kind="Internal").ap()
    g.mT = nc.dram_tensor("mT", [D, T], BF16, kind="Internal").ap()
    g.h2T = nc.dram_tensor("h2T", [D, T], BF16, kind="Internal").ap()
    g.taps = {}
    for name, shape, dt in taps:
        g.taps[name] = nc.dram_tensor("tap_" + name, shape, dt, kind="ExternalOutput").ap()
    g.B_res_t = [Buf("res%d" % i) for i in range(17)]
    g.B_hT_t = [Buf("hT%d" % i) for i in range(17)]
    g.B_h2T_t = [Buf("h2T%d" % i) for i in range(17)]
    g.B_mT = Buf("mT")
    g.B_y = Buf("y")
    g.B_tap = Buf("tap")

    with contextlib.ExitStack() as st:
        def sb(name, shape, dt):
            return st.enter_context(nc.sbuf_tensor(_u() + name, shape, dt))
        g.vecs = sb("vecs", [128, NV], F32)
        g.modv = sb("modv", [128, NL, 96, 2], F32)
        g.A1 = sb("A1", [128, NL, 16, 2], F32)
        g.G1 = sb("G1", [128, NL, 16, 2], F32)
        g.A2 = sb("A2", [128, NL, 16, 2], F32)
        g.G2 = sb("G2", [128, NL, 16, 2], F32)
        g.lbv = sb("lbv", [128, NL, 16], F32)
        g.omlb = sb("omlb", [128, NL, 16], F32)
        g.nsp8 = sb("nsp8", [128, NL, 16], F32)
        g.ones_bf = sb("ones_bf", [128, 128], BF16)
        g.ident_bf = sb("ident_bf", [128, 128], BF16)
        g.epsb = sb("epsb", [128, 1], F32)
        g.scv = sb("scv", [128, 16, 2], F32)
        g.B_const = Buf("const")
        g.ps = [st.enter_context(nc.psum_tensor("ps%d" % i, [128, 512], F32)) for i in range(7)]
        g.ps_bf = st.enter_context(nc.psum_tensor("ps_bf", [128, 1024], BF16))
        g.B_ps = [Buf("ps%d" % i) for i in range(7)]
        g.B_psbf = Buf("psbf")
        sems = {e: st.enter_context(nc.semaphore("s_" + e)) for e in Sched.ENGS}
        dsems = [st.enter_context(nc.semaphore("d%d" % i)) for i in range(N_DMA_SEMS)]
        S = Sched(nc, sems, dsems)
        g.S = S
        g.blk = None

        def open_block():
            cm = nc.Block()
            g.blk = (cm, cm.__enter__())

        def close_block():
            g.blk[0].__exit__(None, None, None)
            g.blk = None

        def sem_reset():
            close_block()
            with nc.Block() as b2:
                b2.tensor(lambda eng: eng.sem_clear(S.sem["pe"]))
                b2.vector(lambda eng: eng.sem_clear(S.sem["dve"]))
                b2.scalar(lambda eng: eng.sem_clear(S.sem["act"]))
                b2.gpsimd(lambda eng: eng.sem_clear(S.sem["pool"]))

                def spclr(eng):
                    eng.sem_clear(S.sem["sp"])
                    for i in range(S.n_dma):
                        eng.sem_clear(S.sem[("dma", i)])
                b2.sync(spclr)
            S.reset_counts()
            open_block()

        def flush():
            S.barrier()
            S.emit(g.blk[1])
            if max(S.cnt.values()) > SEM_RESET_AT:
                sem_reset()
        g.flush = flush
        open_block()
        try:
            import os
            stop = os.environ.get("MK_STOP", "")
            phase_setup(g)
            flush()
            if stop == "setup":
                return nc
            phase_mod(g, nlayers)
            flush()
            if stop == "mod":
                return nc
            phase_norm1_l0(g)
            flush()
            if "h1" in g.taps:
                S.dma("sp", lambda e: e.dma_start(out=g.taps["h1"], in_=g.hT), reads=g.B_hT_t, writes=[g.B_tap])
                flush()
            if stop == "n1":
                return nc
            for l in range(nlayers):
                for hd in range(8):
                    if os.environ.get("MK_SKIPMIX"):
                        break
                    phase_hg(g, l, hd)
                    flush()
                    if stop == "hg0":
                        return nc
                if stop == "hg":
                    return nc
                if not os.environ.get("MK_SKIPMIX"):
                    phase_rg(g, l)
                flush()
                if "m" in g.taps and l == 0:
                    S.dma("sp", lambda e: e.dma_start(out=g.taps["m"], in_=g.mT), reads=[g.B_mT], writes=[g.B_tap])
                    flush()
                if stop == "rg":
                    return nc
                if not os.environ.get("MK_SKIPA"):
                    phase_a(g, l, last=(l == NL - 1))
                flush()
                if stop == "a":
                    return nc
                phase_b(g, l, last=(l == NL - 1), final=(l == nlayers - 1))
                flush()
        finally:
            if g.blk is not None:
                close_block()
    return nc


def V(g, name, *idx_shape):
    return g.vecs[:, VO[name]:]


def phase_setup(g):
    S, nc = g.S, g.nc
    Bc = g.B_const
    S.dma("sp", lambda e: e.dma_start(out=g.vecs[:], in_=g.vec), writes=[Bc])
    S.dma("sp", lambda e: e.dma_start(out=g.scv[:], in_=g.cvec), writes=[Bc])
    S.op("pool", lambda e: e.memset(g.ones_bf[:], 1.0), writes=[Bc])
    S.op("pool", lambda e: e.memset(g.epsb[:], EPS), writes=[Bc])
    with g.nc.sbuf_tensor(_u() + "identf", [128, 128], F32) as identf:
        S.op("pool", lambda e: e.memset(identf[:], 1.0), writes=[Bc])
        S.op("pool", lambda e: e.affine_select(out=identf[:], in_=identf[:], pattern=[[1, 128]],
                                                compare_op=ALU.is_equal, fill=0.0, base=0, channel_multiplier=-1),
             reads=[Bc], writes=[Bc])
        S.op("pool", lambda e: e.tensor_copy(out=g.ident_bf[:], in_=identf[:]), reads=[Bc], writes=[Bc])
        S.op("act", lambda e: e.activation(out=g.scv[:], in_=g.scv[:], func=AF.Silu), reads=[Bc], writes=[Bc])
        lbraw = g.vecs[:, VO["lb"]:VO["lb"] + 64].rearrange("p (l x) -> p l x", l=NL)
        with g.nc.sbuf_tensor(_u() + "lbe", [128, NL, 16], F32) as lbe, g.nc.sbuf_tensor(_u() + "lbs", [128, 16], F32) as lbs:
            S.op("act", lambda e: e.activation(out=lbe[:], in_=lbraw, func=AF.Exp), reads=[Bc], writes=[Bc])
            S.op("dve", lambda e: e.tensor_tensor(out=lbs[:], in0=lbe[:, 0, :], in1=lbe[:, 1, :], op=ALU.add), reads=[Bc], writes=[Bc])
            S.op("dve", lambda e: e.tensor_tensor(out=lbs[:], in0=lbs[:], in1=lbe[:, 2, :], op=ALU.add), reads=[Bc], writes=[Bc])
            S.op("dve", lambda e: e.tensor_tensor(out=lbs[:], in0=lbs[:], in1=lbe[:, 3, :], op=ALU.add), reads=[Bc], writes=[Bc])
            S.op("dve", lambda e: e.reciprocal(out=lbs[:], in_=lbs[:]), reads=[Bc], writes=[Bc])
            S.op("dve", lambda e: e.memset(g.lbv[:, 0, :], 0.0), writes=[Bc])
            for l in range(1, NL):
                S.op("dve", lambda e, l=l: e.tensor_tensor(out=lbe[:, l, :], in0=lbe[:, l, :], in1=lbs[:], op=ALU.mult), reads=[Bc], writes=[Bc])
                S.op("dve", lambda e, l=l: e.tensor_tensor(out=g.lbv[:, l, :], in0=g.lbv[:, l - 1, :], in1=lbe[:, l, :], op=ALU.add), reads=[Bc], writes=[Bc])
            S.op("dve", lambda e: e.tensor_scalar(out=g.omlb[:], in0=g.lbv[:], scalar1=-1.0, scalar2=1.0, op0=ALU.mult, op1=ALU.add), reads=[Bc], writes=[Bc])
            lam = g.vecs[:, VO["rlam"]:VO["rlam"] + 64].rearrange("p (l x) -> p l x", l=NL)
            S.op("act", lambda e: e.activation(out=g.nsp8[:], in_=lam, func=AF.Exp, scale=-1.0), reads=[Bc], writes=[Bc])
            S.op("act", lambda e: e.activation(out=g.nsp8[:], in_=g.nsp8[:], func=AF.Ln, bias=1.0, scale=1.0), reads=[Bc], writes=[Bc])
            S.op("dve", lambda e: e.tensor_scalar(out=g.nsp8[:], in0=g.nsp8[:], scalar1=-8.0, scalar2=None, op0=ALU.mult), reads=[Bc], writes=[Bc])
            g.flush()


def phase_mod(g, nlayers):
    S, nc = g.S, g.nc
    Bc = g.B_const
    with contextlib.ExitStack() as st:
        wt = [st.enter_context(nc.sbuf_tensor(_u() + "wmod%d" % i, [128, 16, 512], F32)) for i in range(2)]
        Bw = [Buf("wmod0"), Buf("wmod1")]
        Bp = g.B_ps[0]
        it = 0
        for l in range(nlayers):
            for cb in range(24):
                j = it % 2
                it += 1
                src = g.w_mod[l, :, cb * 512:(cb + 1) * 512].rearrange("(kc p) n -> p kc n", p=128)
                for half in range(2):
                    S.dma("sp", lambda e, j=j, src=src, half=half: e.dma_start(out=wt[j][:, half * 8:(half + 1) * 8, :], in_=src[:, half * 8:(half + 1) * 8, :]), writes=[Bw[j]])
                for mi in range(4):
                    m = cb * 4 + mi
                    for kc in range(16):
                        S.op("pe", lambda e, j=j, mi=mi, kc=kc, m=m: e.matmul(
                            g.ps[0][:, 2 * m:2 * m + 2], lhsT=wt[j][:, kc, mi * 128:(mi + 1) * 128], rhs=g.scv[:, kc, :],
                            start=(kc == 0), stop=(kc == 15)), reads=[Bw[j], Bc], writes=[Bp])
            bm = g.vecs[:, VO["b_mod"] + l * 96: VO["b_mod"] + (l + 1) * 96]
            S.op("dve", lambda e, l=l, bm=bm: e.tensor_tensor(
                out=g.modv[:, l, :, :], in0=g.ps[0][:, 0:192].rearrange("p (m w) -> p m w", w=2),
                in1=bm.unsqueeze(2).to_broadcast([128, 96, 2]), op=ALU.add), reads=[Bp, Bc], writes=[Bc])
            ng = g.vecs[:, VO["norm_g"] + l * 64: VO["norm_g"] + (l + 1) * 64].rearrange("p (j k) -> p j k", j=4)

            def gb(j):
                return ng[:, j, :].unsqueeze(2).to_broadcast([128, 16, 2])
            mv = g.modv[:, l, :, :]
            S.op("dve", lambda e, l=l, mv=mv, gb=gb: e.scalar_tensor_tensor(out=g.A1[:, l], in0=mv[:, 16:32, :], scalar=1.0, in1=gb(0), op0=ALU.add, op1=ALU.mult), reads=[Bc], writes=[Bc])
            S.op("dve", lambda e, l=l, mv=mv, gb=gb: e.tensor_tensor(out=g.G1[:, l], in0=mv[:, 32:48, :], in1=gb(1), op=ALU.mult), reads=[Bc], writes=[Bc])
            S.op("dve", lambda e, l=l, mv=mv, gb=gb: e.scalar_tensor_tensor(out=g.A2[:, l], in0=mv[:, 64:80, :], scalar=1.0, in1=gb(2), op0=ALU.add, op1=ALU.mult), reads=[Bc], writes=[Bc])
            S.op("dve", lambda e, l=l, mv=mv, gb=gb: e.tensor_tensor(out=g.G2[:, l], in0=mv[:, 80:96, :], in1=gb(3), op=ALU.mult), reads=[Bc], writes=[Bc])
            g.flush()


def tb(lst, lo, hi):
    return lst[lo // 256:(hi + 255) // 256]


def tok_tiles(nt):
    out = [(t0, nt, 0) for t0 in range(0, TL, nt)]
    out += [(TL + t0, min(nt, TC), 1) for t0 in range(0, TC, nt)]
    return out


def emit_norm_mod(g, xt, Bx, n, A, sh, sq, Bsq, tmp, Btmp, rstd, Brstd, hout, Bh, psb):
    S = g.S
    Bc = g.B_const
    Bp = g.B_ps[psb]
    ps = g.ps[psb]
    for kc in range(16):
        S.op("act", lambda e, kc=kc: e.activation(out=sq[:, kc, :n], in_=xt[:, kc, :n], func=AF.Square), reads=[Bx], writes=[Bsq])
    for kc in range(16):
        S.op("pe", lambda e, kc=kc: e.matmul(ps[:, :n], lhsT=g.ones_bf[:], rhs=sq[:, kc, :n], start=(kc == 0), stop=(kc == 15)), reads=[Bsq, Bc], writes=[Bp])
    S.op("act", lambda e: e.activation(out=rstd[:, :n], in_=ps[:, :n], func=AF.Sqrt, bias=g.epsb[:, 0:1], scale=1.0 / D), reads=[Bp, Bc], writes=[Brstd])
    S.op("dve", lambda e: e.reciprocal(out=rstd[:, :n], in_=rstd[:, :n]), reads=[Brstd], writes=[Brstd])
    for kc in range(16):
        S.op("dve", lambda e, kc=kc: e.scalar_tensor_tensor(out=tmp[:, :n], in0=xt[:, kc, :n], scalar=A[:, kc:kc + 1], in1=rstd[:, :n], op0=ALU.mult, op1=ALU.mult), reads=[Bx, Brstd, Bc], writes=[Btmp])
        S.op("act", lambda e, kc=kc: e.activation(out=hout[:, kc, :n], in_=tmp[:, :n], func=AF.Identity, bias=sh[:, kc:kc + 1], scale=1.0), reads=[Btmp, Bc], writes=[Bh])


def phase_norm1_l0(g):
    S, nc = g.S, g.nc
    NT = 256
    with contextlib.ExitStack() as st:
        xt = [st.enter_context(nc.sbuf_tensor(_u() + "n1x%d" % i, [128, 16, NT], F32)) for i in range(2)]
        ht = [st.enter_context(nc.sbuf_tensor(_u() + "n1h%d" % i, [128, 16, NT], BF16)) for i in range(2)]
        sq = st.enter_context(nc.sbuf_tensor(_u() + "n1sq", [128, 16, NT], BF16))
        tmp = st.enter_context(nc.sbuf_tensor(_u() + "n1tmp", [128, NT], F32))
        rstd = st.enter_context(nc.sbuf_tensor(_u() + "n1rstd", [128, NT], F32))
        Bx = [Buf(), Buf()]
        Bh = [Buf(), Buf()]
        Bsq, Btmp, Brstd = Buf(), Buf(), Buf()
        for i, (t0, n, which) in enumerate(tok_tiles(NT)):
            j = i % 2
            src = g.res0[:, t0:t0 + n].rearrange("(kc p) t -> p kc t", p=128)
            S.dma("sp", lambda e, j=j, src=src, n=n: e.dma_start(out=xt[j][:, :, :n], in_=src), writes=[Bx[j]])
            dst = g.res[:, t0:t0 + n].rearrange("(kc p) t -> p kc t", p=128)
            S.dma("sp", lambda e, j=j, dst=dst, n=n: e.dma_start(out=dst, in_=xt[j][:, :, :n]), reads=[Bx[j]], writes=tb(g.B_res_t, t0, t0 + n))
            A = g.A1[:, 0, :, which]
            sh = g.modv[:, 0, 0:16, which]
            emit_norm_mod(g, xt[j], Bx[j], n, A, sh, sq, Bsq, tmp, Btmp, rstd, Brstd, ht[j], Bh[j], 1 + j)
            dsth = g.hT[:, t0:t0 + n].rearrange("(kc p) t -> p kc t", p=128)
            S.dma("sp", lambda e, j=j, dsth=dsth, n=n: e.dma_start(out=dsth, in_=ht[j][:, :, :n]), reads=[Bh[j]], writes=tb(g.B_hT_t, t0, t0 + n))
        g.flush()


def hg_pos_view(buf3, i):
    return buf3[:, 4:68, 8 * i:8 * i + 8].rearrange("p c r -> p r c")


def phase_hg(g, l, hd):
    S, nc = g.S, g.nc
    Bc = g.B_const
    NT = 512
    tiles = tok_tiles(NT)
    lbf = [g.lbv[:, l, d * 8 + hd:d * 8 + hd + 1] for d in range(2)]
    omf = [g.omlb[:, l, d * 8 + hd:d * 8 + hd + 1] for d in range(2)]
    with contextlib.ExitStack() as st0:
        def sb0(name, shape, dt):
            return st0.enter_context(nc.sbuf_tensor(_u() + name, shape, dt))
        qs = sb0("hg_qs", [128, NCH, 64], BF16)
        sg = sb0("hg_sg", [128, NCH, 64], BF16)
        vT = sb0("hg_vT", [128, NCH, 64], BF16)
        kk = [sb0("hg_k%d" % d, [128, NCH, 64], BF16) for d in range(2)]
        qt = [sb0("hg_qt%d" % d, [128, NCH, 64], BF16) for d in range(2)]
        sm = sb0("hg_sm", [128, 2, 5, NCH], F32)
        B_qs, B_sg, B_vT = Buf(), Buf(), Buf()
        B_k = [Buf(), Buf()]
        B_qt = [Buf(), Buf()]
        B_sm = Buf()
        with contextlib.ExitStack() as st1:
            def sb1(name, shape, dt):
                return st1.enter_context(nc.sbuf_tensor(_u() + name, shape, dt))
            lf = [sb1("hg_lf%d" % d, [128, NCH, 65], F32) for d in range(2)]
            msk = sb1("hg_msk", [128, NCH, 65], BF16)
            B_lf = [Buf(), Buf()]
            B_msk = Buf()
            S.op("pool", lambda e: e.memset(msk[:], 1.0), writes=[B_msk])
            S.op("pool", lambda e: e.memset(msk[:, :, 0:1], 0.0), writes=[B_msk])
            for d in range(2):
                S.op("pool", lambda e, d=d: e.memset(lf[d][:, :, 0:1], 0.0), writes=[B_lf[d]])
            with contextlib.ExitStack() as st2:
                wt = st2.enter_context(nc.sbuf_tensor(_u() + "hg_w", [128, 16, 640], BF16))
                hts = [st2.enter_context(nc.sbuf_tensor(_u() + "hg_h%d" % i, [128, 16, NT], BF16)) for i in range(2)]
                sgt = st2.enter_context(nc.sbuf_tensor(_u() + "hg_sgt", [128, NT], F32))
                B_w, B_h, B_sgt = Buf(), [Buf(), Buf()], Buf()
                src = g.w_hg[l, hd].rearrange("(kc p) n -> p kc n", p=128)
                for q4 in range(4):
                    S.dma("pool", lambda e, q4=q4: e.dma_start(out=wt[:, q4 * 4:(q4 + 1) * 4, :], in_=src[:, q4 * 4:(q4 + 1) * 4, :]), writes=[B_w])
                for i, (t0, n, which) in enumerate(tiles):
                    j = i % 2
                    hsrc = g.hT[:, t0:t0 + n].rearrange("(kc p) t -> p kc t", p=128)
                    S.dma("sp", lambda e, j=j, hsrc=hsrc, n=n: e.dma_start(out=hts[j][:, :, :n], in_=hsrc), reads=tb(g.B_hT_t, t0, t0 + n), writes=[B_h[j]])

                    def dstv(buf, w0=0):
                        if which == 0:
                            return hg_pos_view(buf[:, :, w0:w0 + 64], i)
                        return buf[:, 0:4, w0:w0 + 64]

                    def srcv(ap):
                        if which == 0:
                            return ap.rearrange("p (r c) -> p r c", c=64)
                        return ap.rearrange("p (c r) -> p c r", r=64)
                    for blk in range(5):
                        pb = blk % 5
                        ps, Bp = g.ps[pb], g.B_ps[pb]
                        for kc in range(16):
                            S.op("pe", lambda e, ps=ps, kc=kc, blk=blk, j=j, n=n: e.matmul(
                                ps[:, :n], lhsT=wt[:, kc, blk * 128:(blk + 1) * 128], rhs=hts[j][:, kc, :n],
                                start=(kc == 0), stop=(kc == 15)), reads=[B_w, B_h[j]], writes=[Bp])
                        pv = srcv(ps[:, :n])
                        if blk == 0:
                            S.op("act", lambda e, pv=pv, o=dstv(qs): e.activation(out=o, in_=pv, func=AF.Silu), reads=[Bp], writes=[B_qs])
                        elif blk in (1, 2):
                            d = blk - 1
                            S.op("act", lambda e, ps=ps, n=n: e.activation(out=sgt[:, :n], in_=ps[:, :n], func=AF.Sigmoid), reads=[Bp], writes=[B_sgt])
                            S.op("dve", lambda e, d=d, sv=srcv(sgt[:, :n]), o=dstv(kk[d]): e.tensor_scalar(
                                out=o, in0=sv, scalar1=omf[d], scalar2=-1.0, op0=ALU.mult, op1=ALU.mult), reads=[B_sgt, Bc], writes=[B_k[d]])
                            S.op("dve", lambda e, d=d, o=dstv(kk[d]): e.tensor_scalar(
                                out=o, in0=o, scalar1=omf[d], scalar2=None, op0=ALU.add), reads=[B_k[d], Bc], writes=[B_k[d]])
                            S.op("act", lambda e, d=d, sv=srcv(sgt[:, :n]), o=dstv(lf[d], 1): e.activation(
                                out=o, in_=sv, func=AF.Ln, bias=lbf[d], scale=omf[d]), reads=[B_sgt, Bc], writes=[B_lf[d]])
                        elif blk == 3:
                            S.op("act", lambda e, pv=pv, o=dstv(vT): e.activation(out=o, in_=pv, func=AF.Copy), reads=[Bp], writes=[B_vT])
                        else:
                            S.op("act", lambda e, pv=pv, o=dstv(sg): e.activation(out=o, in_=pv, func=AF.Silu), reads=[Bp], writes=[B_sg])
                g.flush()
            with nc.sbuf_tensor(_u() + "hg_tmpE", [128, NCH, 64], F32) as tmpE:
                B_tE = Buf()
                for d in range(2):
                    lff = lf[d][:].rearrange("p c w -> p (c w)")
                    mf = msk[:].rearrange("p c w -> p (c w)")
                    S.op("dve", lambda e, lff=lff, mf=mf: e.tensor_tensor_scan(out=lff, data0=mf, data1=lff, initial=0.0, op0=ALU.mult, op1=ALU.add), reads=[B_lf[d], B_msk], writes=[B_lf[d]])
                    if d == 0:
                        Vw = lf[d][:, :, 1:65]
                        refcol = lf[d][:, :, 32]
                    else:
                        Vw = lf[d][:, :, 0:64]
                        refcol = lf[d][:, :, 32]
                    totcol = lf[d][:, :, 64]
                    sref, stot, sr, ss1, sa = [sm[:, d, x, :] for x in range(5)]
                    S.op("dve", lambda e, sref=sref, refcol=refcol: e.tensor_copy(out=sref, in_=refcol), reads=[B_lf[d]], writes=[B_sm])
                    S.op("dve", lambda e, stot=stot, totcol=totcol: e.tensor_copy(out=stot, in_=totcol), reads=[B_lf[d]], writes=[B_sm])
                    S.op("act", lambda e, sa=sa, stot=stot: e.activation(out=sa, in_=stot, func=AF.Exp), reads=[B_sm], writes=[B_sm])
                    e_ref, e_tr = (sr, ss1) if d == 0 else (ss1, sr)
                    S.op("act", lambda e, e_ref=e_ref, sref=sref: e.activation(out=e_ref, in_=sref, func=AF.Exp), reads=[B_sm], writes=[B_sm])
                    S.op("dve", lambda e, e_tr=e_tr, stot=stot, sref=sref: e.tensor_tensor(out=e_tr, in0=stot, in1=sref, op=ALU.subtract), reads=[B_sm], writes=[B_sm])
                    S.op("act", lambda e, e_tr=e_tr: e.activation(out=e_tr, in_=e_tr, func=AF.Exp), reads=[B_sm], writes=[B_sm])
                    S.op("dve", lambda e, Vw=Vw, sref=sref: e.tensor_tensor(out=Vw, in0=Vw, in1=sref.unsqueeze(2).to_broadcast([128, NCH, 64]), op=ALU.subtract), reads=[B_lf[d], B_sm], writes=[B_lf[d]])
                    sq_, sk_ = (1.0, -1.0) if d == 0 else (-1.0, 1.0)
                    S.op("act", lambda e, Vw=Vw, sq_=sq_: e.activation(out=tmpE[:], in_=Vw, func=AF.Exp, scale=sq_), reads=[B_lf[d]], writes=[B_tE])
                    S.op("dve", lambda e, d=d: e.tensor_tensor(out=qt[d][:], in0=qs[:], in1=tmpE[:], op=ALU.mult), reads=[B_qs, B_tE], writes=[B_qt[d]])
                    S.op("act", lambda e, Vw=Vw, sk_=sk_: e.activation(out=tmpE[:], in_=Vw, func=AF.Exp, scale=sk_), reads=[B_lf[d]], writes=[B_tE])
                    S.op("dve", lambda e, d=d: e.tensor_tensor(out=kk[d][:], in0=kk[d][:], in1=tmpE[:], op=ALU.mult), reads=[B_k[d], B_tE], writes=[B_k[d]])
                g.flush()
        with contextlib.ExitStack() as st1:
            def sb1(name, shape, dt):
                return st1.enter_context(nc.sbuf_tensor(_u() + name, shape, dt))
            v_tm = sb1("hg_vtm", [64, NCH, 128], BF16)
            k_tm = sb1("hg_ktm", [64, NCH, 128], BF16)
            o_sb = sb1("hg_o", [128, NCH, 64], F32)
            mk = [sb1("hg_mask%d" % d, [64, 8, 64], I32) for d in range(2)]
            scb = [sb1("hg_scb%d" % i, [64, 8, 64], BF16) for i in range(2)]
            Ur = sb1("hg_U", [128, 8, 128], F32)
            Sst = [sb1("hg_S%d" % i, [128, 128], F32) for i in range(2)]
            Spr = sb1("hg_Sp", [128, 4, 128], BF16)
            B_vtm, B_ktm, B_o = Buf(), Buf(), Buf()
            B_mk = Buf()
            B_scb = [Buf(), Buf()]
            B_U = [Buf() for _ in range(8)]
            B_S = [Buf(), Buf()]
            B_Sp = [Buf() for _ in range(4)]
            S.op("pool", lambda e: e.iota(mk[0][:], pattern=[[0, 8], [1, 64]], base=0, channel_multiplier=-1), writes=[B_mk])
            S.op("pool", lambda e: e.iota(mk[1][:], pattern=[[0, 8], [-1, 64]], base=0, channel_multiplier=1), writes=[B_mk])
            for d in range(2):
                S.op("dve", lambda e, d=d: e.tensor_single_scalar(out=mk[d][:], in_=mk[d][:], scalar=0, op=ALU.is_ge), reads=[B_mk], writes=[B_mk])
            for i in range(2):
                S.op("pool", lambda e, i=i: e.memset(scb[i][:], 0.0), writes=[B_scb[i]])

            def transpose_all(src, Bsrc, dst, Bdst):
                for c0 in range(0, NCH, 8):
                    ncg = min(8, NCH - c0)
                    for cc in range(ncg):
                        S.op("pe", lambda e, c0=c0, cc=cc: e.transpose(g.ps_bf[0:64, cc * 128:(cc + 1) * 128], src[:, c0 + cc, :], g.ident_bf[:]),
                             reads=[Bsrc, Bc], writes=[g.B_psbf])
                    S.op("act", lambda e, c0=c0, ncg=ncg: e.activation(out=dst[:, c0:c0 + ncg, :], in_=g.ps_bf[0:64, 0:ncg * 128].rearrange("p (c x) -> p c x", x=128), func=AF.Copy),
                         reads=[g.B_psbf], writes=[Bdst])
            transpose_all(vT, B_vT, v_tm, B_vtm)
            fwd_chain = list(range(NCH))
            bwd_chain = [3, 2, 1, 0] + list(range(67, 3, -1))
            for pas, d in enumerate((1, 0)):
                chain = bwd_chain if d == 1 else fwd_chain
                transpose_all(kk[d], B_k[d], k_tm, B_ktm)
                for i2 in range(2):
                    S.op("pool", lambda e, i2=i2: e.memset(scb[i2][:], 0.0), writes=[B_scb[i2]])
                sr, ss1, sa = sm[:, d, 2, :], sm[:, d, 3, :], sm[:, d, 4, :]
                groups = [chain[0:4]] + [chain[4 + 8 * i:12 + 8 * i] for i in range(8)]
                step = 0
                prev_sp = None
                for gi, grp in enumerate(groups):
                    cmin = min(grp)
                    ng = len(grp)
                    sj = gi % 2
                    for c in grp:
                        s = c - cmin
                        S.op("pe", lambda e, c=c, s=s, d=d: e.matmul(g.ps[5][0:64, s * 64:(s + 1) * 64], lhsT=kk[d][:, c, :], rhs=qt[d][:, c, :], start=True, stop=True),
                             reads=[B_k[d], B_qt[d]], writes=[g.B_ps[5]])
                    S.op("dve", lambda e, sj=sj, ng=ng, d=d: e.copy_predicated(out=scb[sj][:, 0:ng, :], mask=mk[d][:, 0:ng, :], data=g.ps[5][0:64, 0:ng * 64].rearrange("p (c t) -> p c t", t=64)),
                         reads=[g.B_ps[5], B_mk], writes=[B_scb[sj]])
                    for c in grp:
                        s = c - cmin
                        pb = 3 + (s // 4)
                        S.op("pe", lambda e, c=c, s=s, pb=pb: e.matmul(g.ps[pb][:, (s % 4) * 128:(s % 4 + 1) * 128], lhsT=k_tm[:, c, :], rhs=v_tm[:, c, :], start=True, stop=True),
                             reads=[B_ktm, B_vtm], writes=[g.B_ps[pb]])
                    for c in grp:
                        s = c - cmin
                        pb = 3 + (s // 4)
                        S.op("act", lambda e, c=c, s=s, pb=pb, ss1=ss1: e.activation(out=Ur[:, s, :], in_=g.ps[pb][:, (s % 4) * 128:(s % 4 + 1) * 128], func=AF.Identity, scale=ss1[:, c:c + 1]),
                             reads=[g.B_ps[pb], B_sm], writes=[B_U[s]])
                    for c in grp:
                        s = c - cmin
                        ov = g.ps[6][:, s * 64:(s + 1) * 64]
                        S.op("pe", lambda e, c=c, s=s, sj=sj, ov=ov, last=(prev_sp is None): e.matmul(ov, lhsT=v_tm[:, c, :], rhs=scb[sj][:, s, :], start=True, stop=last),
                             reads=[B_vtm, B_scb[sj]], writes=[g.B_ps[6]])
                        if prev_sp is not None:
                            S.op("pe", lambda e, c=c, ov=ov, psp=prev_sp, d=d: e.matmul(ov, lhsT=Spr[:, psp, :], rhs=qt[d][:, c, :], start=False, stop=True),
                                 reads=[B_Sp[prev_sp], B_qt[d]], writes=[g.B_ps[6]])
                        sn, so = step % 2, (step + 1) % 2
                        if step == 0:
                            S.op("dve", lambda e, s=s, sn=sn: e.tensor_copy(out=Sst[sn][:], in_=Ur[:, s, :]), reads=[B_U[s]], writes=[B_S[sn]])
                        else:
                            S.op("dve", lambda e, s=s, sn=sn, so=so, c=c, sa=sa: e.scalar_tensor_tensor(out=Sst[sn][:], in0=Sst[so][:], scalar=sa[:, c:c + 1], in1=Ur[:, s, :], op0=ALU.mult, op1=ALU.add),
                                 reads=[B_S[so], B_U[s], B_sm], writes=[B_S[sn]])
                        if step + 1 < NCH:
                            cn = chain[step + 1]
                            spi = step % 4
                            S.op("act", lambda e, sn=sn, spi=spi, cn=cn, sr=sr: e.activation(out=Spr[:, spi, :], in_=Sst[sn][:], func=AF.Identity, scale=sr[:, cn:cn + 1]),
                                 reads=[B_S[sn], B_sm], writes=[B_Sp[spi]])
                            prev_sp = spi
                        step += 1
                    osrc = g.ps[6][:, 0:ng * 64].rearrange("p (c t) -> p c t", t=64)
                    odst = o_sb[:, cmin:cmin + ng, :]
                    if pas == 0:
                        S.op("act", lambda e, osrc=osrc, odst=odst: e.activation(out=odst, in_=osrc, func=AF.Copy), reads=[g.B_ps[6]], writes=[B_o])
                    else:
                        S.op("dve", lambda e, osrc=osrc, odst=odst: e.tensor_tensor(out=odst, in0=osrc, in1=odst, op=ALU.add), reads=[g.B_ps[6], B_o], writes=[B_o])
            with contextlib.ExitStack() as st2:
                osq = st2.enter_context(nc.sbuf_tensor(_u() + "hg_osq", [128, 512], BF16))
                rs = st2.enter_context(nc.sbuf_tensor(_u() + "hg_rs", [128, 512], F32))
                tmp = st2.enter_context(nc.sbuf_tensor(_u() + "hg_tmp", [128, NCH, 64], F32))
                outr = st2.enter_context(nc.sbuf_tensor(_u() + "hg_outr", [128, T], BF16))
                B_osq, B_rs, B_tmp, B_outr = Buf(), Buf(), Buf(), Buf()
                of = o_sb[:].rearrange("p c t -> p (c t)")
                tf = tmp[:].rearrange("p c t -> p (c t)")
                hgg = g.vecs[:, VO["hgg"] + l * 8 + hd: VO["hgg"] + l * 8 + hd + 1]
                for t0 in range(0, T, 512):
                    n = min(512, T - t0)
                    S.op("act", lambda e, t0=t0, n=n: e.activation(out=osq[:, :n], in_=of[:, t0:t0 + n], func=AF.Square), reads=[B_o], writes=[B_osq])
                    S.op("pe", lambda e, n=n: e.matmul(g.ps[0][:, :n], lhsT=g.ones_bf[:], rhs=osq[:, :n], start=True, stop=True), reads=[B_osq, Bc], writes=[g.B_ps[0]])
                    S.op("act", lambda e, n=n: e.activation(out=rs[:, :n], in_=g.ps[0][:, :n], func=AF.Sqrt, bias=g.epsb[:, 0:1], scale=1.0 / 128), reads=[g.B_ps[0], Bc], writes=[B_rs])
                    S.op("dve", lambda e, n=n: e.reciprocal(out=rs[:, :n], in_=rs[:, :n]), reads=[B_rs], writes=[B_rs])
                    S.op("dve", lambda e, t0=t0, n=n: e.scalar_tensor_tensor(out=tf[:, t0:t0 + n], in0=of[:, t0:t0 + n], scalar=hgg, in1=rs[:, :n], op0=ALU.mult, op1=ALU.mult), reads=[B_o, B_rs, Bc], writes=[B_tmp])
                S.op("dve", lambda e: e.tensor_tensor(out=outr[:, TL:T].rearrange("p (c t) -> p c t", t=64), in0=tmp[:, 0:4, :], in1=sg[:, 0:4, :], op=ALU.mult), reads=[B_tmp, B_sg], writes=[B_outr])
                S.op("dve", lambda e: e.tensor_tensor(out=outr[:, 0:TL].rearrange("p (r c) -> p c r", c=64), in0=tmp[:, 4:68, :], in1=sg[:, 4:68, :], op=ALU.mult), reads=[B_tmp, B_sg], writes=[B_outr])
                S.dma("sp", lambda e: e.dma_start(out=g.mT[hd * 128:(hd + 1) * 128, :], in_=outr[:]), reads=[B_outr], writes=[g.B_mT])
                if "o" in g.taps and l == 0:
                    S.dma("sp", lambda e: e.dma_start(out=g.taps["o"][hd], in_=o_sb[:].rearrange("p c t -> p (c t)")), reads=[B_o], writes=[g.B_tap])
                g.flush()


def phase_rg(g, l):
    S, nc = g.S, g.nc
    Bc = g.B_const
    NT = 512
    tiles = tok_tiles(NT)
    with contextlib.ExitStack() as st0:
        def sb0(name, shape, dt):
            return st0.enter_context(nc.sbuf_tensor(_u() + name, shape, dt))
        gw = sb0("rg_gw", [128, 2, 2, 8, 128], BF16)
        B_gw = Buf()
        gwf = gw[:].rearrange("p a b n d -> p (a b n d)")
        for q4 in range(4):
            S.dma("pool", lambda e, q4=q4: e.dma_start(out=gwf[:, q4 * 1024:(q4 + 1) * 1024], in_=g.rgw[l, :, q4 * 1024:(q4 + 1) * 1024]), writes=[B_gw])
        wts = [sb0("rg_w%d" % i, [128, 16, 256], BF16) for i in range(2)]
        hts = [sb0("rg_h%d" % i, [128, 16, NT], BF16) for i in range(2)]
        rxl = sb0("rg_rxl", [128, TL + 3], F32)
        rxc = sb0("rg_rxc", [128, TC + 3], F32)
        gg = sb0("rg_gg", [128, T], BF16)
        xc = sb0("rg_xc", [128, T], F32)
        xcb = sb0("rg_xcb", [128, T], BF16)
        av = sb0("rg_a", [128, T], F32)
        bt = sb0("rg_bt", [128, T], F32)
        hs = [sb0("rg_hs%d" % d, [128, T], F32) for d in range(2)]
        t1 = sb0("rg_t1", [128, NT], F32)
        t2 = sb0("rg_t2", [128, NT], F32)
        t3 = sb0("rg_t3", [128, NT], F32)
        outr = sb0("rg_outr", [128, T], BF16)
        B_w, B_h = [Buf(), Buf()], [Buf(), Buf()]
        B_rx, B_gg, B_xc, B_xcb, B_a, B_bt = Buf(), Buf(), Buf(), Buf(), Buf(), Buf()
        B_hs = [Buf(), Buf()]
        B_t1, B_t2, B_t3, B_outr = Buf(), Buf(), Buf(), Buf()
        for n_ in range(8):
            wj = n_ % 2
            src = g.w_rg[l, n_].rearrange("(kc p) n -> p kc n", p=128)
            for q2 in range(2):
                S.dma("pool", lambda e, wj=wj, src=src, q2=q2: e.dma_start(out=wts[wj][:, q2 * 8:(q2 + 1) * 8, :], in_=src[:, q2 * 8:(q2 + 1) * 8, :]), writes=[B_w[wj]])
            S.op("pool", lambda e: e.memset(rxl[:, 0:2], 0.0), writes=[B_rx])
            S.op("pool", lambda e: e.memset(rxl[:, TL + 2:TL + 3], 0.0), writes=[B_rx])
            S.op("pool", lambda e: e.memset(rxc[:, 0:2], 0.0), writes=[B_rx])
            S.op("pool", lambda e: e.memset(rxc[:, TC + 2:TC + 3], 0.0), writes=[B_rx])
            for i, (t0, n, which) in enumerate(tiles):
                j = i % 2
                hsrc = g.hT[:, t0:t0 + n].rearrange("(kc p) t -> p kc t", p=128)
                S.dma("sp", lambda e, j=j, hsrc=hsrc, n=n: e.dma_start(out=hts[j][:, :, :n], in_=hsrc), reads=tb(g.B_hT_t, t0, t0 + n), writes=[B_h[j]])
                for blk in range(2):
                    ps, Bp = g.ps[blk], g.B_ps[blk]
                    for kc in range(16):
                        S.op("pe", lambda e, ps=ps, kc=kc, blk=blk, j=j, n=n, wj=wj: e.matmul(
                            ps[:, :n], lhsT=wts[wj][:, kc, blk * 128:(blk + 1) * 128], rhs=hts[j][:, kc, :n],
                            start=(kc == 0), stop=(kc == 15)), reads=[B_w[wj], B_h[j]], writes=[Bp])
                    if blk == 0:
                        dst = rxl[:, 2 + t0:2 + t0 + n] if which == 0 else rxc[:, 2 + t0 - TL:2 + t0 - TL + n]
                        S.op("act", lambda e, ps=ps, n=n, dst=dst: e.activation(out=dst, in_=ps[:, :n], func=AF.Copy), reads=[Bp], writes=[B_rx])
                    else:
                        S.op("act", lambda e, ps=ps, n=n, t0=t0: e.activation(out=gg[:, t0:t0 + n], in_=ps[:, :n], func=AF.Gelu), reads=[Bp], writes=[B_gg])
            cw = [g.vecs[:, VO["rcw"] + l * 32 + k * 8 + n_: VO["rcw"] + l * 32 + k * 8 + n_ + 1] for k in range(4)]
            cb = g.vecs[:, VO["rcb"] + l * 8 + n_: VO["rcb"] + l * 8 + n_ + 1]
            for (rx, o0, nn) in ((rxl, 0, TL), (rxc, TL, TC)):
                S.op("dve", lambda e, rx=rx, o0=o0, nn=nn, c0_=cw[0], cb=cb: e.tensor_scalar(out=xc[:, o0:o0 + nn], in0=rx[:, 0:nn], scalar1=c0_, scalar2=cb, op0=ALU.mult, op1=ALU.add), reads=[B_rx, Bc], writes=[B_xc])
                for k in range(1, 4):
                    S.op("dve", lambda e, rx=rx, o0=o0, nn=nn, k=k, ck=cw[k]: e.scalar_tensor_tensor(out=xc[:, o0:o0 + nn], in0=rx[:, k:k + nn], scalar=ck, in1=xc[:, o0:o0 + nn], op0=ALU.mult, op1=ALU.add), reads=[B_rx, B_xc, Bc], writes=[B_xc])
            S.op("act", lambda e: e.activation(out=xcb[:], in_=xc[:], func=AF.Copy), reads=[B_xc], writes=[B_xcb])
            for d in range(2):
                ba = g.vecs[:, VO["rba"] + l * 16 + d * 8 + n_: VO["rba"] + l * 16 + d * 8 + n_ + 1]
                bx = g.vecs[:, VO["rbx"] + l * 16 + d * 8 + n_: VO["rbx"] + l * 16 + d * 8 + n_ + 1]
                nsp = g.nsp8[:, l, d * 8 + n_: d * 8 + n_ + 1]
                for (t0, n, which) in tiles:
                    S.op("pe", lambda e, d=d, t0=t0, n=n, n_=n_: e.matmul(g.ps[2][:, :n], lhsT=gw[:, d, 0, n_, :], rhs=xcb[:, t0:t0 + n], start=True, stop=True), reads=[B_gw, B_xcb], writes=[g.B_ps[2]])
                    S.op("pe", lambda e, d=d, t0=t0, n=n, n_=n_: e.matmul(g.ps[3][:, :n], lhsT=gw[:, d, 1, n_, :], rhs=xcb[:, t0:t0 + n], start=True, stop=True), reads=[B_gw, B_xcb], writes=[g.B_ps[3]])
                    S.op("act", lambda e, n=n, ba=ba: e.activation(out=t1[:, :n], in_=g.ps[2][:, :n], func=AF.Sigmoid, bias=ba, scale=1.0), reads=[g.B_ps[2], Bc], writes=[B_t1])
                    S.op("act", lambda e, n=n, t0=t0, nsp=nsp: e.activation(out=av[:, t0:t0 + n], in_=t1[:, :n], func=AF.Exp, scale=nsp), reads=[B_t1, Bc], writes=[B_a])
                    S.op("act", lambda e, n=n, t0=t0: e.activation(out=t2[:, :n], in_=av[:, t0:t0 + n], func=AF.Square), reads=[B_a], writes=[B_t2])
                    S.op("act", lambda e, n=n: e.activation(out=t2[:, :n], in_=t2[:, :n], func=AF.Sqrt, bias=1.0, scale=-1.0), reads=[B_t2], writes=[B_t2])
                    S.op("act", lambda e, n=n, bx=bx: e.activation(out=t3[:, :n], in_=g.ps[3][:, :n], func=AF.Sigmoid, bias=bx, scale=1.0), reads=[g.B_ps[3], Bc], writes=[B_t3])
                    S.op("dve", lambda e, n=n: e.tensor_tensor(out=t2[:, :n], in0=t2[:, :n], in1=t3[:, :n], op=ALU.mult), reads=[B_t2, B_t3], writes=[B_t2])
                    S.op("dve", lambda e, n=n, t0=t0: e.tensor_tensor(out=bt[:, t0:t0 + n], in0=t2[:, :n], in1=xc[:, t0:t0 + n], op=ALU.mult), reads=[B_t2, B_xc], writes=[B_bt])
                if d == 0:
                    S.op("dve", lambda e: e.tensor_tensor_scan(out=hs[0][:, TL:T], data0=av[:, TL:T], data1=bt[:, TL:T], initial=0.0, op0=ALU.mult, op1=ALU.add), reads=[B_a, B_bt], writes=[B_hs[0]])
                    S.op("dve", lambda e: e.tensor_tensor_scan(out=hs[0][:, 0:TL], data0=av[:, 0:TL], data1=bt[:, 0:TL], initial=hs[0][:, T - 1:T], op0=ALU.mult, op1=ALU.add), reads=[B_a, B_bt, B_hs[0]], writes=[B_hs[0]])
                else:
                    S.op("dve", lambda e: e.tensor_tensor_scan(out=hs[1][:, TL:T][:, ::-1], data0=av[:, TL:T][:, ::-1], data1=bt[:, TL:T][:, ::-1], initial=0.0, op0=ALU.mult, op1=ALU.add), reads=[B_a, B_bt], writes=[B_hs[1]])
                    S.op("dve", lambda e: e.tensor_tensor_scan(out=hs[1][:, 0:TL][:, ::-1], data0=av[:, 0:TL][:, ::-1], data1=bt[:, 0:TL][:, ::-1], initial=hs[1][:, TL:TL + 1], op0=ALU.mult, op1=ALU.add), reads=[B_a, B_bt, B_hs[1]], writes=[B_hs[1]])
            S.op("dve", lambda e: e.tensor_tensor(out=hs[0][:], in0=hs[0][:], in1=hs[1][:], op=ALU.add), reads=[B_hs[0], B_hs[1]], writes=[B_hs[0]])
            S.op("dve", lambda e: e.tensor_tensor(out=outr[:], in0=hs[0][:], in1=gg[:], op=ALU.mult), reads=[B_hs[0], B_gg], writes=[B_outr])
            S.dma("sp", lambda e, n_=n_: e.dma_start(out=g.mT[1024 + n_ * 128:1024 + (n_ + 1) * 128, :], in_=outr[:]), reads=[B_outr], writes=[g.B_mT])
        g.flush()


def phase_a(g, l, last):
    S, nc = g.S, g.nc
    Bc = g.B_const
    NT = 256
    with contextlib.ExitStack() as st:
        def sb(name, shape, dt):
            return st.enter_context(nc.sbuf_tensor(_u() + name, shape, dt))
        wo = sb("a_wo", [128, 16, D], BF16)
        mts = [sb("a_m%d" % i, [128, 16, NT], BF16) for i in range(2)]
        xts = [sb("a_x%d" % i, [128, 16, NT], F32) for i in range(2)]
        hto = [sb("a_h%d" % i, [128, 16, NT], BF16) for i in range(2)]
        mix = sb("a_mix", [128, 16, NT], F32)
        sq = sb("a_sq", [128, 16, NT], BF16)
        tmp = sb("a_tmp", [128, NT], F32)
        rstd = sb("a_rstd", [128, NT], F32)
        B_wo, B_m, B_x, B_h = Buf(), [Buf(), Buf()], [Buf(), Buf()], [Buf(), Buf()]
        B_mix, B_sq, B_tmp, B_rstd = Buf(), Buf(), Buf(), Buf()
        src = g.w_out[l].rearrange("(kc p) n -> p kc n", p=128)
        for kc4 in range(0, 16, 4):
            for hh in range(4):
                S.dma("pool", lambda e, kc4=kc4, hh=hh: e.dma_start(out=wo[:, kc4:kc4 + 4, hh * 512:(hh + 1) * 512], in_=src[:, kc4:kc4 + 4, hh * 512:(hh + 1) * 512]), writes=[B_wo])
        tiles = tok_tiles(NT)
        if last:
            tiles = [t for t in tiles if t[2] == 0]
        import os
        AB = int(os.environ.get("MK_AB", "99"))
        if AB < 99:
            tiles = tiles[:1]
        for i, (t0, n, which) in enumerate(tiles):
            j = i % 2
            msrc = g.mT[:, t0:t0 + n].rearrange("(kc p) t -> p kc t", p=128)
            xsrc = g.res[:, t0:t0 + n].rearrange("(kc p) t -> p kc t", p=128)
            S.dma("sp", lambda e, j=j, msrc=msrc: e.dma_start(out=mts[j][:], in_=msrc), reads=[g.B_mT], writes=[B_m[j]])
            S.dma("sp", lambda e, j=j, xsrc=xsrc: e.dma_start(out=xts[j][:], in_=xsrc), reads=tb(g.B_res_t, t0, t0 + n), writes=[B_x[j]])
            if AB < 1:
                continue
            for ob in range(16):
                pb = ob % 4
                ps, Bp = g.ps[pb], g.B_ps[pb]
                for kc in range(16):
                    S.op("pe", lambda e, ps=ps, kc=kc, ob=ob, j=j: e.matmul(ps[:, :NT], lhsT=wo[:, kc, ob * 128:(ob + 1) * 128], rhs=mts[j][:, kc, :],
                                                                             start=(kc == 0), stop=(kc == 15)), reads=[B_wo, B_m[j]], writes=[Bp])
                S.op("dve", lambda e, ps=ps, ob=ob: e.tensor_copy(out=mix[:, ob, :], in_=ps[:, :NT]), reads=[Bp], writes=[B_mix])
                S.op("act", lambda e, ob=ob: e.activation(out=sq[:, ob, :], in_=mix[:, ob, :], func=AF.Square), reads=[B_mix], writes=[B_sq])
            if AB < 2:
                continue
            ps, Bp = g.ps[4], g.B_ps[4]
            for kc in range(16):
                S.op("pe", lambda e, kc=kc, ps=ps: e.matmul(ps[:, :NT], lhsT=g.ones_bf[:], rhs=sq[:, kc, :], start=(kc == 0), stop=(kc == 15)), reads=[B_sq, Bc], writes=[Bp])
            S.op("act", lambda e, ps=ps: e.activation(out=rstd[:], in_=ps[:, :NT], func=AF.Sqrt, bias=g.epsb[:, 0:1], scale=1.0 / D), reads=[Bp, Bc], writes=[B_rstd])
            S.op("dve", lambda e: e.reciprocal(out=rstd[:], in_=rstd[:]), reads=[B_rstd], writes=[B_rstd])
            if AB < 3:
                continue
            G = g.G1[:, l, :, which]
            for kc in range(16):
                S.op("dve", lambda e, kc=kc, gk=G[:, kc:kc + 1]: e.scalar_tensor_tensor(out=tmp[:], in0=mix[:, kc, :], scalar=gk, in1=rstd[:], op0=ALU.mult, op1=ALU.mult), reads=[B_mix, B_rstd, Bc], writes=[B_tmp])
                S.op("dve", lambda e, kc=kc, j=j: e.tensor_tensor(out=xts[j][:, kc, :], in0=xts[j][:, kc, :], in1=tmp[:], op=ALU.add), reads=[B_tmp, B_x[j]], writes=[B_x[j]])
            if AB < 4:
                continue
            xdst = g.res[:, t0:t0 + n].rearrange("(kc p) t -> p kc t", p=128)
            S.dma("sp", lambda e, j=j, xdst=xdst: e.dma_start(out=xdst, in_=xts[j][:]), reads=[B_x[j]], writes=tb(g.B_res_t, t0, t0 + n))
            if AB < 5:
                continue
            emit_norm_mod(g, xts[j], B_x[j], NT, g.A2[:, l, :, which], g.modv[:, l, 48:64, which], sq, B_sq, tmp, B_tmp, rstd, B_rstd, hto[j], B_h[j], 5)
            hdst = g.h2T[:, t0:t0 + n].rearrange("(kc p) t -> p kc t", p=128)
            S.dma("sp", lambda e, j=j, hdst=hdst: e.dma_start(out=hdst, in_=hto[j][:]), reads=[B_h[j]], writes=tb(g.B_h2T_t, t0, t0 + n))
            if "xa" in g.taps and l == 0:
                S.dma("sp", lambda e, j=j, t0=t0, n=n: e.dma_start(out=g.taps["xa"][:, t0:t0 + n].rearrange("(kc p) t -> p kc t", p=128), in_=xts[j][:]), reads=[B_x[j]], writes=[g.B_tap])
        g.flush()


def phase_b(g, l, last, final):
    S, nc = g.S, g.nc
    Bc = g.B_const
    TS = 512
    NS = 256
    with contextlib.ExitStack() as st:
        def sb(name, shape, dt):
            return st.enter_context(nc.sbuf_tensor(_u() + name, shape, dt))
        h2 = sb("b_h2", [128, 16, TS + 2], BF16)
        ge = sb("b_ge", [128, 44, TS], BF16)
        wup = [sb("b_wu%d" % i, [128, 16, 256], BF16) for i in range(2)]
        wdn = [sb("b_wd%d" % i, [128, 44, 128], BF16) for i in range(2)]
        fl = sb("b_fl", [128, 16, TS], F32)
        sq = sb("b_sq", [128, 16, NS], BF16)
        xt = sb("b_x", [128, 16, NS], F32)
        hn = sb("b_hn", [128, 16, NS], BF16)
        ca = sb("b_ca", [128, NS], F32)
        cv = sb("b_cv", [128, NS], F32)
        tmp = sb("b_tmp", [128, NS], F32)
        rstd = sb("b_rstd", [128, NS], F32)
        B_h2, B_ge, B_wu, B_wd = Buf(), Buf(), [Buf(), Buf()], [Buf(), Buf()]
        B_fl, B_sq, B_x, B_hn, B_ca, B_cv, B_tmp, B_rstd = Buf(), Buf(), Buf(), Buf(), Buf(), Buf(), Buf(), Buf()
        stiles = [(t0, TS, 0) for t0 in range(0, TL, TS)]
        if not last:
            stiles.append((TL, TC, 1))
        wi = 0
        di = 0
        import os
        if os.environ.get("MK_NST"):
            stiles = stiles[:int(os.environ["MK_NST"])]
        for (t0, n, which) in stiles:
            seq0 = 0 if which == 0 else TL
            seq1 = TL if which == 0 else T
            lo = max(t0 - 1, seq0)
            hi = min(t0 + n + 1, seq1)
            if lo == t0:
                S.op("pool", lambda e: e.memset(h2[:, :, 0:1], 0.0), writes=[B_h2])
            if hi == t0 + n:
                S.op("pool", lambda e, n=n: e.memset(h2[:, :, n + 1:n + 2], 0.0), writes=[B_h2])
            hsrc = g.h2T[:, lo:hi].rearrange("(kc p) t -> p kc t", p=128)
            S.dma("sp", lambda e, hsrc=hsrc, lo=lo, hi=hi, t0=t0: e.dma_start(out=h2[:, :, lo - (t0 - 1):hi - (t0 - 1)], in_=hsrc), reads=tb(g.B_h2T_t, lo, hi), writes=[B_h2])
            nsub = n // NS
            import os
            BB = int(os.environ.get("MK_BB", "99"))
            if BB < 1:
                g.flush()
                continue
            for jf in range(44):
                wj = wi % 2
                wi += 1
                src = g.w_up[l, jf].rearrange("(kc p) n -> p kc n", p=128)
                for q2 in range(2):
                    S.dma("pool", lambda e, wj=wj, src=src, q2=q2: e.dma_start(out=wup[wj][:, q2 * 8:(q2 + 1) * 8, :], in_=src[:, q2 * 8:(q2 + 1) * 8, :]), writes=[B_wu[wj]])
                fw = [[g.vecs[:, VO["fcw"] + l * 264 + k * 88 + half * 44 + jf: VO["fcw"] + l * 264 + k * 88 + half * 44 + jf + 1] for k in range(3)] for half in range(2)]
                fb = [g.vecs[:, VO["fcb"] + l * 88 + half * 44 + jf: VO["fcb"] + l * 88 + half * 44 + jf + 1] for half in range(2)]
                for s in range(nsub):
                    c0 = s * NS
                    for half in range(2):
                        pb = (s * 2 + half) % 4
                        ps, Bp = g.ps[pb], g.B_ps[pb]
                        HH = int(os.environ.get("MK_HH", "2"))
                        for kc in range(16):
                            S.op("pe", lambda e, ps=ps, kc=kc, wj=wj, half=half, c0=c0: e.matmul(
                                ps[:, :NS + HH], lhsT=wup[wj][:, kc, half * 128:(half + 1) * 128], rhs=h2[:, kc, c0:c0 + NS + HH],
                                start=(kc == 0), stop=(kc == 15)), reads=[B_wu[wj], B_h2], writes=[Bp])
                        dst, Bd = (ca, B_ca) if half == 0 else (cv, B_cv)
                        if os.environ.get("MK_EE", "") == "noact":
                            continue
                        S.op("act", lambda e, ps=ps, dst=dst, half=half, fw=fw, fb=fb: e.activation(out=dst[:], in_=ps[:, HH // 2:NS + HH // 2], func=AF.Identity, bias=fb[half], scale=fw[half][1]), reads=[Bp, Bc], writes=[Bd])
                        EE = os.environ.get("MK_EE", "")
                        if EE == "noact2":
                            continue
                        S.op("dve", lambda e, ps=ps, dst=dst, half=half, fw=fw: e.scalar_tensor_tensor(out=dst[:], in0=ps[:, 0:NS], scalar=fw[half][0], in1=dst[:], op0=ALU.mult, op1=ALU.add), reads=[Bp, Bd, Bc], writes=[Bd])
                        if EE == "one":
                            continue
                        S.op("dve", lambda e, ps=ps, dst=dst, half=half, fw=fw: e.scalar_tensor_tensor(out=dst[:], in0=ps[:, 2:NS + 2], scalar=fw[half][2], in1=dst[:], op0=ALU.mult, op1=ALU.add), reads=[Bp, Bd, Bc], writes=[Bd])
                    if BB < 2:
                        continue
                    S.op("act", lambda e: e.activation(out=ca[:], in_=ca[:], func=AF.Gelu), reads=[B_ca], writes=[B_ca])
                    S.op("dve", lambda e, jf=jf, c0=c0: e.tensor_tensor(out=ge[:, jf, c0:c0 + NS], in0=ca[:], in1=cv[:], op=ALU.mult), reads=[B_ca, B_cv], writes=[B_ge])
            if BB < 3:
                g.flush()
                continue
            for ob in range(16):
                dj = di % 2
                di += 1
                src = g.w_dn[l, ob].rearrange("(fc p) n -> p fc n", p=128)
                for q2 in range(2):
                    S.dma("pool", lambda e, dj=dj, src=src, q2=q2: e.dma_start(out=wdn[dj][:, q2 * 22:(q2 + 1) * 22, :], in_=src[:, q2 * 22:(q2 + 1) * 22, :]), writes=[B_wd[dj]])
                for s in range(nsub):
                    c0 = s * NS
                    pb = 4 + (s % 2)
                    ps, Bp = g.ps[pb], g.B_ps[pb]
                    for fc in range(44):
                        S.op("pe", lambda e, ps=ps, fc=fc, dj=dj, c0=c0: e.matmul(ps[:, :NS], lhsT=wdn[dj][:, fc, :], rhs=ge[:, fc, c0:c0 + NS], start=(fc == 0), stop=(fc == 43)),
                             reads=[B_wd[dj], B_ge], writes=[Bp])
                    S.op("act", lambda e, ps=ps, ob=ob, c0=c0: e.activation(out=fl[:, ob, c0:c0 + NS], in_=ps[:, :NS], func=AF.Copy), reads=[Bp], writes=[B_fl])
            if BB < 4:
                g.flush()
                continue
            for s in range(nsub):
                c0 = s * NS
                tt = t0 + c0
                xsrc = g.res[:, tt:tt + NS].rearrange("(kc p) t -> p kc t", p=128)
                S.dma("sp", lambda e, xsrc=xsrc: e.dma_start(out=xt[:], in_=xsrc), reads=tb(g.B_res_t, tt, tt + NS), writes=[B_x])
                for kc in range(16):
                    S.op("act", lambda e, kc=kc, c0=c0: e.activation(out=sq[:, kc, :], in_=fl[:, kc, c0:c0 + NS], func=AF.Square), reads=[B_fl], writes=[B_sq])
                ps, Bp = g.ps[6], g.B_ps[6]
                for kc in range(16):
                    S.op("pe", lambda e, kc=kc, ps=ps: e.matmul(ps[:, :NS], lhsT=g.ones_bf[:], rhs=sq[:, kc, :], start=(kc == 0), stop=(kc == 15)), reads=[B_sq, Bc], writes=[Bp])
                S.op("act", lambda e, ps=ps: e.activation(out=rstd[:], in_=ps[:, :NS], func=AF.Sqrt, bias=g.epsb[:, 0:1], scale=1.0 / D), reads=[Bp, Bc], writes=[B_rstd])
                S.op("dve", lambda e: e.reciprocal(out=rstd[:], in_=rstd[:]), reads=[B_rstd], writes=[B_rstd])
                G = g.G2[:, l, :, which]
                for kc in range(16):
                    S.op("dve", lambda e, kc=kc, c0=c0, gk=G[:, kc:kc + 1]: e.scalar_tensor_tensor(out=tmp[:], in0=fl[:, kc, c0:c0 + NS], scalar=gk, in1=rstd[:], op0=ALU.mult, op1=ALU.mult), reads=[B_fl, B_rstd, Bc], writes=[B_tmp])
                    S.op("dve", lambda e, kc=kc: e.tensor_tensor(out=xt[:, kc, :], in0=xt[:, kc, :], in1=tmp[:], op=ALU.add), reads=[B_tmp, B_x], writes=[B_x])
                if final:
                    if which == 0:
                        ydst = g.y[:, tt:tt + NS].rearrange("(kc p) t -> p kc t", p=128)
                        S.dma("sp", lambda e, ydst=ydst: e.dma_start(out=ydst, in_=xt[:]), reads=[B_x], writes=[g.B_y])
                else:
                    xdst = g.res[:, tt:tt + NS].rearrange("(kc p) t -> p kc t", p=128)
                    S.dma("sp", lambda e, xdst=xdst: e.dma_start(out=xdst, in_=xt[:]), reads=[B_x], writes=tb(g.B_res_t, tt, tt + NS))
                    emit_norm_mod(g, xt, B_x, NS, g.A1[:, l + 1, :, which], g.modv[:, l + 1, 0:16, which], sq, B_sq, tmp, B_tmp, rstd, B_rstd, hn, B_hn, 6)
                    hdst = g.hT[:, tt:tt + NS].rearrange("(kc p) t -> p kc t", p=128)
                    S.dma("sp", lambda e, hdst=hdst: e.dma_start(out=hdst, in_=hn[:]), reads=[B_hn], writes=tb(g.B_hT_t, tt, tt + NS))
            g.flush()


def fm(a, inner):
    a = np.asarray(a, dtype=np.float32)
    lead = a.shape[:-1]
    x = a.shape[-1] // 128
    a = a.reshape(lead + (x, 128))
    a = np.moveaxis(a, -1, 0)
    return np.ascontiguousarray(a).reshape(128, -1)


def prep_inputs(inp):
    f32 = np.float32
    vec = np.zeros((128, NV), f32)

    def put(name, arr):
        vec[:, VO[name]:VO[name] + arr.shape[1]] = arr
    put("b_mod", fm(inp["b_mod"], 96))
    put("norm_g", fm(inp["norm_g"], 16))
    put("lb", fm(inp["hg_lower_bounds"], 8))
    put("hgg", fm(inp["hg_norm_g"], 8))
    put("rcw", fm(inp["rg_conv_w"], 8))
    put("rcb", fm(inp["rg_conv_b"], 8))
    put("rba", fm(inp["rg_ba"], 8))
    put("rbx", fm(inp["rg_bx"], 8))
    put("rlam", fm(inp["rg_lambda"], 8))
    put("fcw", fm(inp["ffn_conv_w"], 88))
    put("fcb", fm(inp["ffn_conv_b"], 88))
    wa = np.asarray(inp["rg_wa"], f32)
    wx = np.asarray(inp["rg_wx"], f32)
    rgw = np.stack([wa, wx], axis=2)
    rgw = np.ascontiguousarray(rgw.transpose(0, 4, 1, 2, 3, 5)).reshape(NL, 128, 4096)
    w_in = np.asarray(inp["w_in"], f32)
    hgc = w_in[:, :, :5120].reshape(NL, D, 5, 8, 128)
    w_hg = np.ascontiguousarray(hgc.transpose(0, 3, 1, 2, 4)).reshape(NL, 8, D, 640)
    rgc = w_in[:, :, 5120:].reshape(NL, D, 2, 8, 128)
    w_rg = np.ascontiguousarray(rgc.transpose(0, 3, 1, 2, 4)).reshape(NL, 8, D, 256)
    w_up = np.asarray(inp["ffn_w_up"], f32).reshape(NL, D, 2, 44, 128)
    w_up = np.ascontiguousarray(w_up.transpose(0, 3, 1, 2, 4)).reshape(NL, 44, D, 256)
    w_dn = np.asarray(inp["ffn_w_down"], f32).reshape(NL, DFF, 16, 128)
    w_dn = np.ascontiguousarray(w_dn.transpose(0, 2, 1, 3))
    shared = dict(vec=vec, rgw=rgw, w_mod=np.ascontiguousarray(inp["w_mod"], dtype=f32), w_hg=w_hg, w_rg=w_rg,
                  w_out=np.ascontiguousarray(inp["w_out"], dtype=f32), w_up=w_up, w_dn=w_dn)
    x = np.asarray(inp["x"], f32)
    ctx = np.asarray(inp["ctx"], f32)
    c = np.asarray(inp["c"], f32)
    c_ctx = np.asarray(inp["c_ctx"], f32)
    per = []
    for b in range(4):
        res0 = np.ascontiguousarray(np.concatenate([x[b].T, ctx[b].T], axis=1))
        cv = np.stack([c[b], c_ctx], axis=-1).reshape(16, 128, 2).transpose(1, 0, 2)
        per.append(dict(res0=res0, cvec=np.ascontiguousarray(cv)))
    return shared, per


_NC_CACHE = {}


def kernel(**inputs):
    shared, per = prep_inputs(inputs)
    if "nc" not in _NC_CACHE:
        _NC_CACHE["nc"] = build()
    nc = _NC_CACHE["nc"]
    in_maps = []
    for core in range(8):
        m = dict(shared)
        m.update(per[core % 4])
        in_maps.append(m)
    res = run_bass_kernel_spmd(nc, in_maps, core_ids=list(range(8)))
    out = np.stack([res.results[b]["y"].T for b in range(4)], axis=0)
    return np.ascontiguousarray(out.astype(np.float32))
```

```python
import contextlib
import os as _os
import numpy as np
import concourse.bass as bass
import concourse.mybir as mybir
from concourse.bass_utils import run_bass_kernel_spmd

F32 = mybir.dt.float32
BF16 = mybir.dt.bfloat16
I32 = mybir.dt.int32
AF = mybir.ActivationFunctionType
ALU = mybir.AluOpType

D = 2048
TL = 4096
TC = 256
T = TL + TC
NL = 4
NCH = 68
DFF = 5632
EPS = 1e-6
SAME_ENGINE_SYNC = bool(int(_os.environ.get('MK_SAME', '1')))
N_DMA_SEMS = 8
N_POOL_SEMS = 3
SEM_RESET_AT = int(_os.environ.get('MK_RESET', '1000000000'))

VO = {}
_o = 0
for _n, _sz in [("b_mod", 4 * 96), ("norm_g", 4 * 4 * 16), ("lb", 4 * 2 * 8), ("hgg", 4 * 8),
                ("rcw", 4 * 4 * 8), ("rcb", 4 * 8), ("rba", 4 * 2 * 8), ("rbx", 4 * 2 * 8), ("rlam", 4 * 2 * 8),
                ("fcw", 4 * 3 * 88), ("fcb", 4 * 88)]:
    VO[_n] = _o
    _o += _sz
NV = _o


_UC = [0]


def _u():
    _UC[0] += 1
    return "t%d_" % _UC[0]


_ALL_BUFS = []


class Buf:
    __slots__ = ("name", "w", "r")

    def __init__(self, name=""):
        self.name = name
        self.w = {}
        self.r = {}
        _ALL_BUFS.append(self)


class Sched:
    ENGS = ("pe", "dve", "act", "pool", "sp")

    def __init__(self, nc, sems, dma_sems):
        self.nc = nc
        self.sem = dict(sems)
        self.qsems = {}
        for q, lst in dma_sems.items():
            self.qsems[q] = []
            for i, s in enumerate(lst):
                self.sem[("dma", q, i)] = s
                self.qsems[q].append(("dma", q, i))
        self.qrr = {q: 0 for q in dma_sems}
        self.same = False
        self.prog = {e: [] for e in self.ENGS}
        self.cnt = {k: 0 for k in self.sem}
        self.known = {e: {} for e in self.ENGS}
        self.dma_rr = 0
        self.ninst = 0

    def _waits(self, e, reads, writes, extra=(), force_same=False, dma_write=False):
        need = {}
        for b in reads:
            for k, v in b.w.items():
                if need.get(k, 0) < v:
                    need[k] = v
        for b in writes:
            for k, v in b.w.items():
                if dma_write and isinstance(k, tuple):
                    continue
                if need.get(k, 0) < v:
                    need[k] = v
            for k, v in b.r.items():
                if need.get(k, 0) < v:
                    need[k] = v
        for k, v in extra:
            if need.get(k, 0) < v:
                need[k] = v
        out = []
        kn = self.known[e]
        for k, v in need.items():
            if k == e and not force_same and (e == "pe" or (e != "pool" and not (SAME_ENGINE_SYNC or self.same))):
                continue
            if kn.get(k, 0) >= v:
                continue
            kn[k] = v
            out.append((k, v))
        return out

    def op(self, e, fn, reads=(), writes=(), force_same=False):
        waits = self._waits(e, reads, writes, force_same=force_same)
        self.cnt[e] += 1
        t = self.cnt[e]
        self.prog[e].append((waits, fn, e, 1))
        for b in reads:
            b.r[e] = t
        for b in writes:
            b.w = {e: t}
            b.r = {}
        self.ninst += 1

    def dma(self, q, fn, reads=(), writes=()):
        i = self.qrr[q]
        self.qrr[q] = (i + 1) % len(self.qsems[q])
        k = self.qsems[q][i]
        extra = [(k, self.cnt[k])] if self.cnt[k] > 0 else []
        waits = self._waits(q, reads, writes, extra, force_same=True, dma_write=True)
        self.cnt[k] += 16
        t = self.cnt[k]
        self.prog[q].append((waits, fn, k, 16))
        for b in reads:
            b.r[k] = t
        for b in writes:
            b.w = {kk: vv for kk, vv in b.w.items() if isinstance(kk, tuple)}
            b.w[k] = t
            b.r = {}
        self.ninst += 1

    def reset_counts(self):
        for k in self.cnt:
            self.cnt[k] = 0
        self.known = {e: {} for e in self.ENGS}
        for b in _ALL_BUFS:
            b.w = {}
            b.r = {}

    def barrier(self):
        for e in self.ENGS:
            waits = []
            kn = self.known[e]
            for k, v in self.cnt.items():
                if v > 0 and k != e and kn.get(k, 0) < v:
                    kn[k] = v
                    waits.append((k, v))
            if e != "pe" and self.cnt[e] > 0 and kn.get(e, 0) < self.cnt[e]:
                kn[e] = self.cnt[e]
                waits.append((e, self.cnt[e]))
            if waits:
                self.prog[e].append((waits, None, None, 0))

    def emit(self, block):
        engmap = {"pe": block.tensor, "dve": block.vector, "act": block.scalar,
                  "pool": block.gpsimd, "sp": block.sync}
        sem = self.sem
        for e in self.ENGS:
            prog = self.prog[e]
            if not prog:
                continue

            def body(eng, prog=prog):
                for waits, fn, k, inc in prog:
                    for wk, wv in waits:
                        eng.wait_ge(sem[wk], wv)
                    if fn is not None:
                        fn(eng).then_inc(sem[k], inc)
            engmap[e](body)
            self.prog[e] = []


class Ctx:
    pass


def build(nlayers=NL, taps=()):
    nc = bass.Bass("TRN2", target_bir_lowering=False)
    g = Ctx()
    g.nc = nc
    dt_in = lambda n, s: nc.dram_tensor(n, s, F32, kind="ExternalInput").ap()
    g.res0 = dt_in("res0", [D, T])
    g.cvec = dt_in("cvec", [128, 16, 2])
    g.vec = dt_in("vec", [128, NV])
    g.rgw = dt_in("rgw", [NL, 128, 4096])
    g.w_mod = dt_in("w_mod", [NL, D, 6 * D])
    g.w_hg = dt_in("w_hg", [NL, 8, D, 640])
    g.w_rg = dt_in("w_rg", [NL, 8, D, 256])
    g.w_out = dt_in("w_out", [NL, D, D])
    g.w_up = dt_in("w_up", [NL, 44, D, 256])
    g.w_dn = dt_in("w_dn", [NL, 16, DFF, 128])
    g.y = nc.dram_tensor("y", [D, TL], F32, kind="ExternalOutput").ap()
    g.res = nc.dram_tensor("res", [D, T], F32, kind="Internal").ap()
    g.hT = nc.dram_tensor("hT", [D, T], BF16, kind="Internal").ap()
    g.mT = nc.dram_tensor("mT", [D, T], BF16, kind="Internal").ap()
    g.h2T = nc.dram_tensor("h2T", [D, T], BF16, kind="Internal").ap()
    g.wub = nc.dram_tensor("wub", [44, 128, 16 * 256], BF16, kind="Internal").ap()
    g.wdb = nc.dram_tensor("wdb", [16, 128, 44 * 128], BF16, kind="Internal").ap()
    g.B_wub = [Buf("wub%d" % i) for i in range(44)]
    g.B_wdb = [Buf("wdb%d" % i) for i in range(16)]
    g.taps = {}
    for name, shape, dt in taps:
        g.taps[name] = nc.dram_tensor("tap_" + name, shape, dt, kind="ExternalOutput").ap()
    g.B_res_t = [Buf("res%d" % i) for i in range(17)]
    g.B_hT_t = [Buf("hT%d" % i) for i in range(17)]
    g.B_h2T_t = [Buf("h2T%d" % i) for i in range(17)]
    g.B_mT = Buf("mT")
    g.B_y = Buf("y")
    g.B_tap = Buf("tap")

    with contextlib.ExitStack() as st:
        def sb(name, shape, dt):
            return st.enter_context(nc.sbuf_tensor(_u() + name, shape, dt))
        g.vecs = sb("vecs", [128, NV], F32)
        g.modv = sb("modv", [128, NL, 96, 2], F32)
        g.A1 = sb("A1", [128, NL, 16, 2], F32)
        g.G1 = sb("G1", [128, NL, 16, 2], F32)
        g.A2 = sb("A2", [128, NL, 16, 2], F32)
        g.G2 = sb("G2", [128, NL, 16, 2], F32)
        g.lbv = sb("lbv", [128, NL, 16], F32)
        g.omlb = sb("omlb", [128, NL, 16], F32)
        g.nsp8 = sb("nsp8", [128, NL, 16], F32)
        g.ones_bf = sb("ones_bf", [128, 128], BF16)
        g.ident_bf = sb("ident_bf", [128, 128], BF16)
        g.epsb = sb("epsb", [128, 1], F32)
        g.scv = sb("scv", [128, 16, 2], F32)
        g.B_const = Buf("const")
        g.ps = [st.enter_context(nc.psum_tensor("ps%d" % i, [128, 512], F32)) for i in range(7)]
        g.ps_bf = st.enter_context(nc.psum_tensor("ps_bf", [128, 1024], BF16))
        g.B_ps = [Buf("ps%d" % i) for i in range(7)]
        g.B_psbf = Buf("psbf")
        sems = {e: st.enter_context(nc.semaphore("s_" + e)) for e in Sched.ENGS}
        dsems = {"sp": [st.enter_context(nc.semaphore("dsp%d" % i)) for i in range(N_DMA_SEMS)],
                 "pool": [st.enter_context(nc.semaphore("dpl%d" % i)) for i in range(N_POOL_SEMS)]}
        S = Sched(nc, sems, dsems)
        g.S = S
        g.blk = None

        def open_block():
            cm = nc.Block()
            g.blk = (cm, cm.__enter__())

        def close_block():
            g.blk[0].__exit__(None, None, None)
            g.blk = None

        def sem_reset():
            close_block()
            with nc.Block() as b2:
                b2.tensor(lambda eng: eng.sem_clear(S.sem["pe"]))
                b2.vector(lambda eng: eng.sem_clear(S.sem["dve"]))
                b2.scalar(lambda eng: eng.sem_clear(S.sem["act"]))
                b2.gpsimd(lambda eng: eng.sem_clear(S.sem["pool"]))

                def spclr(eng):
                    eng.sem_clear(S.sem["sp"])
                    for kk in S.sem:
                        if isinstance(kk, tuple):
                            eng.sem_clear(S.sem[kk])
                b2.sync(spclr)
            S.reset_counts()
            open_block()

        def flush():
            S.barrier()
            S.emit(g.blk[1])
            if max(S.cnt.values()) > SEM_RESET_AT:
                sem_reset()
        g.flush = flush
        open_block()
        try:
            import os
            stop = os.environ.get("MK_STOP", "")
            phase_setup(g)
            flush()
            if stop == "setup":
                return nc
            phase_mod(g, nlayers)
            flush()
            if stop == "mod":
                return nc
            phase_norm1_l0(g)
            flush()
            if "h1" in g.taps:
                S.dma("sp", lambda e: e.dma_start(out=g.taps["h1"], in_=g.hT), reads=g.B_hT_t, writes=[g.B_tap])
                flush()
            if stop == "n1":
                return nc
            for l in range(nlayers):
                for hd in range(8):
                    if os.environ.get("MK_SKIPMIX"):
                        break
                    phase_hg(g, l, hd)
                    flush()
                    if stop == "hg0":
                        return nc
                if stop == "hg":
                    return nc
                if not os.environ.get("MK_SKIPMIX"):
                    S.same = True
                    phase_rg(g, l)
                    S.same = False
                flush()
                if "m" in g.taps and l == 0:
                    S.dma("sp", lambda e: e.dma_start(out=g.taps["m"], in_=g.mT), reads=[g.B_mT], writes=[g.B_tap])
                    flush()
                if stop == "rg":
                    return nc
                if not os.environ.get("MK_SKIPA"):
                    phase_a(g, l, last=(l == NL - 1))
                flush()
                if stop == "a":
                    return nc
                phase_b(g, l, last=(l == NL - 1), final=(l == nlayers - 1))
                flush()
        finally:
            if g.blk is not None:
                close_block()
    return nc


def V(g, name, *idx_shape):
    return g.vecs[:, VO[name]:]


def phase_setup(g):
    S, nc = g.S, g.nc
    Bc = g.B_const
    S.dma("sp", lambda e: e.dma_start(out=g.vecs[:], in_=g.vec), writes=[Bc])
    S.dma("sp", lambda e: e.dma_start(out=g.scv[:], in_=g.cvec), writes=[Bc])
    S.op("pool", lambda e: e.memset(g.ones_bf[:], 1.0), writes=[Bc])
    S.op("pool", lambda e: e.memset(g.epsb[:], EPS), writes=[Bc])
    with g.nc.sbuf_tensor(_u() + "identf", [128, 128], F32) as identf:
        S.op("pool", lambda e: e.memset(identf[:], 1.0), writes=[Bc])
        S.op("pool", lambda e: e.affine_select(out=identf[:], in_=identf[:], pattern=[[1, 128]],
                                                compare_op=ALU.is_equal, fill=0.0, base=0, channel_multiplier=-1),
             reads=[Bc], writes=[Bc])
        S.op("pool", lambda e: e.tensor_copy(out=g.ident_bf[:], in_=identf[:]), reads=[Bc], writes=[Bc])
        S.op("act", lambda e: e.activation(out=g.scv[:], in_=g.scv[:], func=AF.Silu), reads=[Bc], writes=[Bc])
        lbraw = g.vecs[:, VO["lb"]:VO["lb"] + 64].rearrange("p (l x) -> p l x", l=NL)
        with g.nc.sbuf_tensor(_u() + "lbe", [128, NL, 16], F32) as lbe, g.nc.sbuf_tensor(_u() + "lbs", [128, 16], F32) as lbs:
            S.op("act", lambda e: e.activation(out=lbe[:], in_=lbraw, func=AF.Exp), reads=[Bc], writes=[Bc])
            S.op("dve", lambda e: e.tensor_tensor(out=lbs[:], in0=lbe[:, 0, :], in1=lbe[:, 1, :], op=ALU.add), reads=[Bc], writes=[Bc])
            S.op("dve", lambda e: e.tensor_tensor(out=lbs[:], in0=lbs[:], in1=lbe[:, 2, :], op=ALU.add), reads=[Bc], writes=[Bc])
            S.op("dve", lambda e: e.tensor_tensor(out=lbs[:], in0=lbs[:], in1=lbe[:, 3, :], op=ALU.add), reads=[Bc], writes=[Bc])
            S.op("dve", lambda e: e.reciprocal(out=lbs[:], in_=lbs[:]), reads=[Bc], writes=[Bc])
            S.op("dve", lambda e: e.memset(g.lbv[:, 0, :], 0.0), writes=[Bc])
            for l in range(1, NL):
                S.op("dve", lambda e, l=l: e.tensor_tensor(out=lbe[:, l, :], in0=lbe[:, l, :], in1=lbs[:], op=ALU.mult), reads=[Bc], writes=[Bc])
                S.op("dve", lambda e, l=l: e.tensor_tensor(out=g.lbv[:, l, :], in0=g.lbv[:, l - 1, :], in1=lbe[:, l, :], op=ALU.add), reads=[Bc], writes=[Bc])
            S.op("dve", lambda e: e.tensor_scalar(out=g.omlb[:], in0=g.lbv[:], scalar1=-1.0, scalar2=1.0, op0=ALU.mult, op1=ALU.add), reads=[Bc], writes=[Bc])
            lam = g.vecs[:, VO["rlam"]:VO["rlam"] + 64].rearrange("p (l x) -> p l x", l=NL)
            S.op("act", lambda e: e.activation(out=g.nsp8[:], in_=lam, func=AF.Exp, scale=-1.0), reads=[Bc], writes=[Bc])
            S.op("act", lambda e: e.activation(out=g.nsp8[:], in_=g.nsp8[:], func=AF.Ln, bias=1.0, scale=1.0), reads=[Bc], writes=[Bc])
            S.op("dve", lambda e: e.tensor_scalar(out=g.nsp8[:], in0=g.nsp8[:], scalar1=-8.0, scalar2=None, op0=ALU.mult), reads=[Bc], writes=[Bc])
            g.flush()


def phase_mod(g, nlayers):
    S, nc = g.S, g.nc
    Bc = g.B_const
    with contextlib.ExitStack() as st:
        wt = [st.enter_context(nc.sbuf_tensor(_u() + "wmod%d" % i, [128, 16, 512], F32)) for i in range(2)]
        Bw = [Buf("wmod0"), Buf("wmod1")]
        Bp = g.B_ps[0]
        it = 0
        for l in range(nlayers):
            for cb in range(24):
                j = it % 2
                it += 1
                src = g.w_mod[l, :, cb * 512:(cb + 1) * 512].rearrange("(kc p) n -> p kc n", p=128)
                for half in range(2):
                    S.dma("sp", lambda e, j=j, src=src, half=half: e.dma_start(out=wt[j][:, half * 8:(half + 1) * 8, :], in_=src[:, half * 8:(half + 1) * 8, :]), writes=[Bw[j]])
                for mi in range(4):
                    m = cb * 4 + mi
                    for kc in range(16):
                        S.op("pe", lambda e, j=j, mi=mi, kc=kc, m=m: e.matmul(
                            g.ps[0][:, 2 * m:2 * m + 2], lhsT=wt[j][:, kc, mi * 128:(mi + 1) * 128], rhs=g.scv[:, kc, :],
                            start=(kc == 0), stop=(kc == 15)), reads=[Bw[j], Bc], writes=[Bp])
            bm = g.vecs[:, VO["b_mod"] + l * 96: VO["b_mod"] + (l + 1) * 96]
            S.op("dve", lambda e, l=l, bm=bm: e.tensor_tensor(
                out=g.modv[:, l, :, :], in0=g.ps[0][:, 0:192].rearrange("p (m w) -> p m w", w=2),
                in1=bm.unsqueeze(2).to_broadcast([128, 96, 2]), op=ALU.add), reads=[Bp, Bc], writes=[Bc])
            ng = g.vecs[:, VO["norm_g"] + l * 64: VO["norm_g"] + (l + 1) * 64].rearrange("p (j k) -> p j k", j=4)

            def gb(j):
                return ng[:, j, :].unsqueeze(2).to_broadcast([128, 16, 2])
            mv = g.modv[:, l, :, :]
            S.op("dve", lambda e, l=l, mv=mv, gb=gb: e.scalar_tensor_tensor(out=g.A1[:, l], in0=mv[:, 16:32, :], scalar=1.0, in1=gb(0), op0=ALU.add, op1=ALU.mult), reads=[Bc], writes=[Bc])
            S.op("dve", lambda e, l=l, mv=mv, gb=gb: e.tensor_tensor(out=g.G1[:, l], in0=mv[:, 32:48, :], in1=gb(1), op=ALU.mult), reads=[Bc], writes=[Bc])
            S.op("dve", lambda e, l=l, mv=mv, gb=gb: e.scalar_tensor_tensor(out=g.A2[:, l], in0=mv[:, 64:80, :], scalar=1.0, in1=gb(2), op0=ALU.add, op1=ALU.mult), reads=[Bc], writes=[Bc])
            S.op("dve", lambda e, l=l, mv=mv, gb=gb: e.tensor_tensor(out=g.G2[:, l], in0=mv[:, 80:96, :], in1=gb(3), op=ALU.mult), reads=[Bc], writes=[Bc])
            g.flush()


def tb(lst, lo, hi):
    return lst[lo // 256:(hi + 255) // 256]


def tok_tiles(nt):
    out = [(t0, nt, 0) for t0 in range(0, TL, nt)]
    out += [(TL + t0, min(nt, TC), 1) for t0 in range(0, TC, nt)]
    return out


def emit_norm_mod(g, xt, Bx, n, A, sh, sq, Bsq, tmp, Btmp, rstd, Brstd, hout, Bh, psb):
    S = g.S
    Bc = g.B_const
    Bp = g.B_ps[psb]
    ps = g.ps[psb]
    for kc in range(16):
        S.op("act", lambda e, kc=kc: e.activation(out=sq[:, kc, :n], in_=xt[:, kc, :n], func=AF.Square), reads=[Bx], writes=[Bsq])
    for kc in range(16):
        S.op("pe", lambda e, kc=kc: e.matmul(ps[:, :n], lhsT=g.ones_bf[:], rhs=sq[:, kc, :n], start=(kc == 0), stop=(kc == 15)), reads=[Bsq, Bc], writes=[Bp])
    S.op("act", lambda e: e.activation(out=rstd[:, :n], in_=ps[:, :n], func=AF.Sqrt, bias=g.epsb[:, 0:1], scale=1.0 / D), reads=[Bp, Bc], writes=[Brstd])
    S.op("dve", lambda e: e.reciprocal(out=rstd[:, :n], in_=rstd[:, :n]), reads=[Brstd], writes=[Brstd])
    for kc in range(16):
        S.op("dve", lambda e, kc=kc: e.scalar_tensor_tensor(out=tmp[:, :n], in0=xt[:, kc, :n], scalar=A[:, kc:kc + 1], in1=rstd[:, :n], op0=ALU.mult, op1=ALU.mult), reads=[Bx, Brstd, Bc], writes=[Btmp])
        S.op("act", lambda e, kc=kc: e.activation(out=hout[:, kc, :n], in_=tmp[:, :n], func=AF.Identity, bias=sh[:, kc:kc + 1], scale=1.0), reads=[Btmp, Bc], writes=[Bh])


def phase_norm1_l0(g):
    S, nc = g.S, g.nc
    NT = 256
    with contextlib.ExitStack() as st:
        xt = [st.enter_context(nc.sbuf_tensor(_u() + "n1x%d" % i, [128, 16, NT], F32)) for i in range(2)]
        ht = [st.enter_context(nc.sbuf_tensor(_u() + "n1h%d" % i, [128, 16, NT], BF16)) for i in range(2)]
        sq = st.enter_context(nc.sbuf_tensor(_u() + "n1sq", [128, 16, NT], BF16))
        tmp = st.enter_context(nc.sbuf_tensor(_u() + "n1tmp", [128, NT], F32))
        rstd = st.enter_context(nc.sbuf_tensor(_u() + "n1rstd", [128, NT], F32))
        Bx = [Buf(), Buf()]
        Bh = [Buf(), Buf()]
        Bsq, Btmp, Brstd = Buf(), Buf(), Buf()
        n1tiles = tok_tiles(NT)

        def issue_load(i):
            t0, n, which = n1tiles[i]
            j = i % 2
            src = g.res0[:, t0:t0 + n].rearrange("(kc p) t -> p kc t", p=128)
            S.dma("sp", lambda e, j=j, src=src, n=n: e.dma_start(out=xt[j][:, :, :n], in_=src), writes=[Bx[j]])
        issue_load(0)
        for i, (t0, n, which) in enumerate(n1tiles):
            j = i % 2
            if i + 1 < len(n1tiles):
                issue_load(i + 1)
            dst = g.res[:, t0:t0 + n].rearrange("(kc p) t -> p kc t", p=128)
            S.dma("sp", lambda e, j=j, dst=dst, n=n: e.dma_start(out=dst, in_=xt[j][:, :, :n]), reads=[Bx[j]], writes=tb(g.B_res_t, t0, t0 + n))
            A = g.A1[:, 0, :, which]
            sh = g.modv[:, 0, 0:16, which]
            emit_norm_mod(g, xt[j], Bx[j], n, A, sh, sq, Bsq, tmp, Btmp, rstd, Brstd, ht[j], Bh[j], 1 + j)
            dsth = g.hT[:, t0:t0 + n].rearrange("(kc p) t -> p kc t", p=128)
            S.dma("sp", lambda e, j=j, dsth=dsth, n=n: e.dma_start(out=dsth, in_=ht[j][:, :, :n]), reads=[Bh[j]], writes=tb(g.B_hT_t, t0, t0 + n))
        g.flush()


def hg_pos_view(buf3, i):
    return buf3[:, 4:68, 8 * i:8 * i + 8].rearrange("p c r -> p r c")


def phase_hg(g, l, hd):
    S, nc = g.S, g.nc
    Bc = g.B_const
    NT = 512
    tiles = tok_tiles(NT)
    lbf = [g.lbv[:, l, d * 8 + hd:d * 8 + hd + 1] for d in range(2)]
    omf = [g.omlb[:, l, d * 8 + hd:d * 8 + hd + 1] for d in range(2)]
    with contextlib.ExitStack() as st0:
        def sb0(name, shape, dt):
            return st0.enter_context(nc.sbuf_tensor(_u() + name, shape, dt))
        qs = sb0("hg_qs", [128, NCH, 64], BF16)
        sg = sb0("hg_sg", [128, NCH, 64], BF16)
        vT = sb0("hg_vT", [128, NCH, 64], BF16)
        kk = [sb0("hg_k%d" % d, [128, NCH, 64], BF16) for d in range(2)]
        qt = [sb0("hg_qt%d" % d, [128, NCH, 64], BF16) for d in range(2)]
        sm = sb0("hg_sm", [128, 2, 5, NCH], F32)
        B_qs, B_sg, B_vT = Buf(), Buf(), Buf()
        B_k = [Buf(), Buf()]
        B_qt = [Buf(), Buf()]
        B_sm = Buf()
        with contextlib.ExitStack() as st1:
            def sb1(name, shape, dt):
                return st1.enter_context(nc.sbuf_tensor(_u() + name, shape, dt))
            lf = [sb1("hg_lf%d" % d, [128, NCH, 65], F32) for d in range(2)]
            msk = sb1("hg_msk", [128, NCH, 65], BF16)
            B_lf = [Buf(), Buf()]
            B_msk = Buf()
            S.op("pool", lambda e: e.memset(msk[:], 1.0), writes=[B_msk])
            S.op("pool", lambda e: e.memset(msk[:, :, 0:1], 0.0), writes=[B_msk])
            for d in range(2):
                S.op("pool", lambda e, d=d: e.memset(lf[d][:, :, 0:1], 0.0), writes=[B_lf[d]])
            with contextlib.ExitStack() as st2:
                wt = st2.enter_context(nc.sbuf_tensor(_u() + "hg_w", [128, 16, 640], BF16))
                hts = [st2.enter_context(nc.sbuf_tensor(_u() + "hg_h%d" % i, [128, 16, NT], BF16)) for i in range(2)]
                sgt = st2.enter_context(nc.sbuf_tensor(_u() + "hg_sgt", [128, NT], F32))
                B_w, B_h, B_sgt = Buf(), [Buf(), Buf()], Buf()
                src = g.w_hg[l, hd].rearrange("(kc p) n -> p kc n", p=128)
                for q4 in range(4):
                    S.dma("pool", lambda e, q4=q4: e.dma_start(out=wt[:, q4 * 4:(q4 + 1) * 4, :], in_=src[:, q4 * 4:(q4 + 1) * 4, :]), writes=[B_w])
                for i, (t0, n, which) in enumerate(tiles):
                    j = i % 2
                    hsrc = g.hT[:, t0:t0 + n].rearrange("(kc p) t -> p kc t", p=128)
                    S.dma("sp", lambda e, j=j, hsrc=hsrc, n=n: e.dma_start(out=hts[j][:, :, :n], in_=hsrc), reads=tb(g.B_hT_t, t0, t0 + n), writes=[B_h[j]])

                    def dstv(buf, w0=0):
                        if which == 0:
                            return hg_pos_view(buf[:, :, w0:w0 + 64], i)
                        return buf[:, 0:4, w0:w0 + 64]

                    def srcv(ap):
                        if which == 0:
                            return ap.rearrange("p (r c) -> p r c", c=64)
                        return ap.rearrange("p (c r) -> p c r", r=64)
                    for blk in range(5):
                        pb = blk % 5
                        ps, Bp = g.ps[pb], g.B_ps[pb]
                        for kc in range(16):
                            S.op("pe", lambda e, ps=ps, kc=kc, blk=blk, j=j, n=n: e.matmul(
                                ps[:, :n], lhsT=wt[:, kc, blk * 128:(blk + 1) * 128], rhs=hts[j][:, kc, :n],
                                start=(kc == 0), stop=(kc == 15)), reads=[B_w, B_h[j]], writes=[Bp])
                        pv = srcv(ps[:, :n])
                        if blk == 0:
                            S.op("act", lambda e, pv=pv, o=dstv(qs): e.activation(out=o, in_=pv, func=AF.Silu), reads=[Bp], writes=[B_qs])
                        elif blk in (1, 2):
                            d = blk - 1
                            S.op("act", lambda e, ps=ps, n=n: e.activation(out=sgt[:, :n], in_=ps[:, :n], func=AF.Sigmoid), reads=[Bp], writes=[B_sgt])
                            S.op("dve", lambda e, d=d, sv=srcv(sgt[:, :n]), o=dstv(kk[d]): e.tensor_scalar(
                                out=o, in0=sv, scalar1=omf[d], scalar2=-1.0, op0=ALU.mult, op1=ALU.mult), reads=[B_sgt, Bc], writes=[B_k[d]])
                            S.op("dve", lambda e, d=d, o=dstv(kk[d]): e.tensor_scalar(
                                out=o, in0=o, scalar1=omf[d], scalar2=None, op0=ALU.add), reads=[B_k[d], Bc], writes=[B_k[d]])
                            S.op("act", lambda e, d=d, sv=srcv(sgt[:, :n]), o=dstv(lf[d], 1): e.activation(
                                out=o, in_=sv, func=AF.Ln, bias=lbf[d], scale=omf[d]), reads=[B_sgt, Bc], writes=[B_lf[d]])
                        elif blk == 3:
                            S.op("act", lambda e, pv=pv, o=dstv(vT): e.activation(out=o, in_=pv, func=AF.Copy), reads=[Bp], writes=[B_vT])
                        else:
                            S.op("act", lambda e, pv=pv, o=dstv(sg): e.activation(out=o, in_=pv, func=AF.Silu), reads=[Bp], writes=[B_sg])
                g.flush()
            with nc.sbuf_tensor(_u() + "hg_tmpE", [128, NCH, 64], F32) as tmpE:
                B_tE = Buf()
                for d in range(2):
                    lff = lf[d][:].rearrange("p c w -> p (c w)")
                    mf = msk[:].rearrange("p c w -> p (c w)")
                    S.op("dve", lambda e, lff=lff, mf=mf: e.tensor_tensor_scan(out=lff, data0=mf, data1=lff, initial=0.0, op0=ALU.mult, op1=ALU.add), reads=[B_lf[d], B_msk], writes=[B_lf[d]])
                    if d == 0:
                        Vw = lf[d][:, :, 1:65]
                        refcol = lf[d][:, :, 32]
                    else:
                        Vw = lf[d][:, :, 0:64]
                        refcol = lf[d][:, :, 32]
                    totcol = lf[d][:, :, 64]
                    sref, stot, sr, ss1, sa = [sm[:, d, x, :] for x in range(5)]
                    S.op("dve", lambda e, sref=sref, refcol=refcol: e.tensor_copy(out=sref, in_=refcol), reads=[B_lf[d]], writes=[B_sm])
                    S.op("dve", lambda e, stot=stot, totcol=totcol: e.tensor_copy(out=stot, in_=totcol), reads=[B_lf[d]], writes=[B_sm])
                    S.op("act", lambda e, sa=sa, stot=stot: e.activation(out=sa, in_=stot, func=AF.Exp), reads=[B_sm], writes=[B_sm])
                    e_ref, e_tr = (sr, ss1) if d == 0 else (ss1, sr)
                    S.op("act", lambda e, e_ref=e_ref, sref=sref: e.activation(out=e_ref, in_=sref, func=AF.Exp), reads=[B_sm], writes=[B_sm])
                    S.op("dve", lambda e, e_tr=e_tr, stot=stot, sref=sref: e.tensor_tensor(out=e_tr, in0=stot, in1=sref, op=ALU.subtract), reads=[B_sm], writes=[B_sm])
                    S.op("act", lambda e, e_tr=e_tr: e.activation(out=e_tr, in_=e_tr, func=AF.Exp), reads=[B_sm], writes=[B_sm])
                    S.op("dve", lambda e, Vw=Vw, sref=sref: e.tensor_tensor(out=Vw, in0=Vw, in1=sref.unsqueeze(2).to_broadcast([128, NCH, 64]), op=ALU.subtract), reads=[B_lf[d], B_sm], writes=[B_lf[d]])
                    sq_, sk_ = (1.0, -1.0) if d == 0 else (-1.0, 1.0)
                    S.op("act", lambda e, Vw=Vw, sq_=sq_: e.activation(out=tmpE[:], in_=Vw, func=AF.Exp, scale=sq_), reads=[B_lf[d]], writes=[B_tE])
                    S.op("dve", lambda e, d=d: e.tensor_tensor(out=qt[d][:], in0=qs[:], in1=tmpE[:], op=ALU.mult), reads=[B_qs, B_tE], writes=[B_qt[d]])
                    S.op("act", lambda e, Vw=Vw, sk_=sk_: e.activation(out=tmpE[:], in_=Vw, func=AF.Exp, scale=sk_), reads=[B_lf[d]], writes=[B_tE])
                    S.op("dve", lambda e, d=d: e.tensor_tensor(out=kk[d][:], in0=kk[d][:], in1=tmpE[:], op=ALU.mult), reads=[B_k[d], B_tE], writes=[B_k[d]])
                g.flush()
        with contextlib.ExitStack() as st1:
            def sb1(name, shape, dt):
                return st1.enter_context(nc.sbuf_tensor(_u() + name, shape, dt))
            v_tm = sb1("hg_vtm", [64, NCH, 128], BF16)
            k_tm = sb1("hg_ktm", [64, NCH, 128], BF16)
            o_sb = sb1("hg_o", [128, NCH, 64], F32)
            mk = [sb1("hg_mask%d" % d, [64, 8, 64], I32) for d in range(2)]
            scb = [sb1("hg_scb%d" % i, [64, 8, 64], BF16) for i in range(2)]
            Ur = sb1("hg_U", [128, 8, 128], F32)
            Sst = [sb1("hg_S%d" % i, [128, 128], F32) for i in range(2)]
            Spr = sb1("hg_Sp", [128, 4, 128], BF16)
            B_vtm, B_ktm, B_o = Buf(), Buf(), Buf()
            B_mk = Buf()
            B_scb = [Buf(), Buf()]
            B_U = [Buf() for _ in range(8)]
            B_S = [Buf(), Buf()]
            B_Sp = [Buf() for _ in range(4)]
            S.op("pool", lambda e: e.iota(mk[0][:], pattern=[[0, 8], [1, 64]], base=0, channel_multiplier=-1), writes=[B_mk])
            S.op("pool", lambda e: e.iota(mk[1][:], pattern=[[0, 8], [-1, 64]], base=0, channel_multiplier=1), writes=[B_mk])
            for d in range(2):
                S.op("dve", lambda e, d=d: e.tensor_single_scalar(out=mk[d][:], in_=mk[d][:], scalar=0, op=ALU.is_ge), reads=[B_mk], writes=[B_mk])
            for i in range(2):
                S.op("pool", lambda e, i=i: e.memset(scb[i][:], 0.0), writes=[B_scb[i]])

            def transpose_all(src, Bsrc, dst, Bdst):
                for c0 in range(0, NCH, 8):
                    ncg = min(8, NCH - c0)
                    for cc in range(ncg):
                        S.op("pe", lambda e, c0=c0, cc=cc: e.transpose(g.ps_bf[0:64, cc * 128:(cc + 1) * 128], src[:, c0 + cc, :], g.ident_bf[:]),
                             reads=[Bsrc, Bc], writes=[g.B_psbf])
                    S.op("act", lambda e, c0=c0, ncg=ncg: e.activation(out=dst[:, c0:c0 + ncg, :], in_=g.ps_bf[0:64, 0:ncg * 128].rearrange("p (c x) -> p c x", x=128), func=AF.Copy),
                         reads=[g.B_psbf], writes=[Bdst])
            transpose_all(vT, B_vT, v_tm, B_vtm)
            fwd_chain = list(range(NCH))
            bwd_chain = [3, 2, 1, 0] + list(range(67, 3, -1))
            for pas, d in enumerate((1, 0)):
                chain = bwd_chain if d == 1 else fwd_chain
                transpose_all(kk[d], B_k[d], k_tm, B_ktm)
                for i2 in range(2):
                    S.op("pool", lambda e, i2=i2: e.memset(scb[i2][:], 0.0), writes=[B_scb[i2]])
                sr, ss1, sa = sm[:, d, 2, :], sm[:, d, 3, :], sm[:, d, 4, :]
                groups = [chain[0:4]] + [chain[4 + 8 * i:12 + 8 * i] for i in range(8)]
                step = 0
                prev_sp = None
                for gi, grp in enumerate(groups):
                    cmin = min(grp)
                    ng = len(grp)
                    sj = gi % 2
                    for c in grp:
                        s = c - cmin
                        S.op("pe", lambda e, c=c, s=s, d=d: e.matmul(g.ps[5][0:64, s * 64:(s + 1) * 64], lhsT=kk[d][:, c, :], rhs=qt[d][:, c, :], start=True, stop=True),
                             reads=[B_k[d], B_qt[d]], writes=[g.B_ps[5]])
                    S.op("dve", lambda e, sj=sj, ng=ng, d=d: e.copy_predicated(out=scb[sj][:, 0:ng, :], mask=mk[d][:, 0:ng, :], data=g.ps[5][0:64, 0:ng * 64].rearrange("p (c t) -> p c t", t=64)),
                         reads=[g.B_ps[5], B_mk], writes=[B_scb[sj]])
                    for c in grp:
                        s = c - cmin
                        pb = 3 + (s // 4)
                        S.op("pe", lambda e, c=c, s=s, pb=pb: e.matmul(g.ps[pb][:, (s % 4) * 128:(s % 4 + 1) * 128], lhsT=k_tm[:, c, :], rhs=v_tm[:, c, :], start=True, stop=True),
                             reads=[B_ktm, B_vtm], writes=[g.B_ps[pb]])
                    for c in grp:
                        s = c - cmin
                        pb = 3 + (s // 4)
                        S.op("act", lambda e, c=c, s=s, pb=pb, ss1=ss1: e.activation(out=Ur[:, s, :], in_=g.ps[pb][:, (s % 4) * 128:(s % 4 + 1) * 128], func=AF.Identity, scale=ss1[:, c:c + 1]),
                             reads=[g.B_ps[pb], B_sm], writes=[B_U[s]])
                    for c in grp:
                        s = c - cmin
                        ov = g.ps[6][:, s * 64:(s + 1) * 64]
                        S.op("pe", lambda e, c=c, s=s, sj=sj, ov=ov, last=(prev_sp is None): e.matmul(ov, lhsT=v_tm[:, c, :], rhs=scb[sj][:, s, :], start=True, stop=last),
                             reads=[B_vtm, B_scb[sj]], writes=[g.B_ps[6]])
                        if prev_sp is not None:
                            S.op("pe", lambda e, c=c, ov=ov, psp=prev_sp, d=d: e.matmul(ov, lhsT=Spr[:, psp, :], rhs=qt[d][:, c, :], start=False, stop=True),
                                 reads=[B_Sp[prev_sp], B_qt[d]], writes=[g.B_ps[6]])
                        sn, so = step % 2, (step + 1) % 2
                        if step == 0:
                            S.op("dve", lambda e, s=s, sn=sn: e.tensor_copy(out=Sst[sn][:], in_=Ur[:, s, :]), reads=[B_U[s]], writes=[B_S[sn]])
                        else:
                            S.op("dve", lambda e, s=s, sn=sn, so=so, c=c, sa=sa: e.scalar_tensor_tensor(out=Sst[sn][:], in0=Sst[so][:], scalar=sa[:, c:c + 1], in1=Ur[:, s, :], op0=ALU.mult, op1=ALU.add),
                                 reads=[B_S[so], B_U[s], B_sm], writes=[B_S[sn]])
                        if step + 1 < NCH:
                            cn = chain[step + 1]
                            spi = step % 4
                            S.op("act", lambda e, sn=sn, spi=spi, cn=cn, sr=sr: e.activation(out=Spr[:, spi, :], in_=Sst[sn][:], func=AF.Identity, scale=sr[:, cn:cn + 1]),
                                 reads=[B_S[sn], B_sm], writes=[B_Sp[spi]])
                            prev_sp = spi
                        step += 1
                    osrc = g.ps[6][:, 0:ng * 64].rearrange("p (c t) -> p c t", t=64)
                    odst = o_sb[:, cmin:cmin + ng, :]
                    if pas == 0:
                        S.op("act", lambda e, osrc=osrc, odst=odst: e.activation(out=odst, in_=osrc, func=AF.Copy), reads=[g.B_ps[6]], writes=[B_o])
                    else:
                        S.op("dve", lambda e, osrc=osrc, odst=odst: e.tensor_tensor(out=odst, in0=osrc, in1=odst, op=ALU.add), reads=[g.B_ps[6], B_o], writes=[B_o])
            with contextlib.ExitStack() as st2:
                osq = st2.enter_context(nc.sbuf_tensor(_u() + "hg_osq", [128, 512], BF16))
                rs = st2.enter_context(nc.sbuf_tensor(_u() + "hg_rs", [128, 512], F32))
                tmp = st2.enter_context(nc.sbuf_tensor(_u() + "hg_tmp", [128, NCH, 64], F32))
                outr = st2.enter_context(nc.sbuf_tensor(_u() + "hg_outr", [128, T], BF16))
                B_osq, B_rs, B_tmp, B_outr = Buf(), Buf(), Buf(), Buf()
                of = o_sb[:].rearrange("p c t -> p (c t)")
                tf = tmp[:].rearrange("p c t -> p (c t)")
                hgg = g.vecs[:, VO["hgg"] + l * 8 + hd: VO["hgg"] + l * 8 + hd + 1]
                for t0 in range(0, T, 512):
                    n = min(512, T - t0)
                    S.op("act", lambda e, t0=t0, n=n: e.activation(out=osq[:, :n], in_=of[:, t0:t0 + n], func=AF.Square), reads=[B_o], writes=[B_osq])
                    S.op("pe", lambda e, n=n: e.matmul(g.ps[0][:, :n], lhsT=g.ones_bf[:], rhs=osq[:, :n], start=True, stop=True), reads=[B_osq, Bc], writes=[g.B_ps[0]])
                    S.op("act", lambda e, n=n: e.activation(out=rs[:, :n], in_=g.ps[0][:, :n], func=AF.Sqrt, bias=g.epsb[:, 0:1], scale=1.0 / 128), reads=[g.B_ps[0], Bc], writes=[B_rs])
                    S.op("dve", lambda e, n=n: e.reciprocal(out=rs[:, :n], in_=rs[:, :n]), reads=[B_rs], writes=[B_rs])
                    S.op("dve", lambda e, t0=t0, n=n: e.scalar_tensor_tensor(out=tf[:, t0:t0 + n], in0=of[:, t0:t0 + n], scalar=hgg, in1=rs[:, :n], op0=ALU.mult, op1=ALU.mult), reads=[B_o, B_rs, Bc], writes=[B_tmp])
                S.op("dve", lambda e: e.tensor_tensor(out=outr[:, TL:T].rearrange("p (c t) -> p c t", t=64), in0=tmp[:, 0:4, :], in1=sg[:, 0:4, :], op=ALU.mult), reads=[B_tmp, B_sg], writes=[B_outr])
                S.op("dve", lambda e: e.tensor_tensor(out=outr[:, 0:TL].rearrange("p (r c) -> p c r", c=64), in0=tmp[:, 4:68, :], in1=sg[:, 4:68, :], op=ALU.mult), reads=[B_tmp, B_sg], writes=[B_outr])
                S.dma("sp", lambda e: e.dma_start(out=g.mT[hd * 128:(hd + 1) * 128, :], in_=outr[:]), reads=[B_outr], writes=[g.B_mT])
                if "o" in g.taps and l == 0:
                    S.dma("sp", lambda e: e.dma_start(out=g.taps["o"][hd], in_=o_sb[:].rearrange("p c t -> p (c t)")), reads=[B_o], writes=[g.B_tap])
                g.flush()


def phase_rg(g, l):
    S, nc = g.S, g.nc
    Bc = g.B_const
    NT = 512
    tiles = tok_tiles(NT)
    with contextlib.ExitStack() as st0:
        def sb0(name, shape, dt):
            return st0.enter_context(nc.sbuf_tensor(_u() + name, shape, dt))
        gw = sb0("rg_gw", [128, 2, 2, 8, 128], BF16)
        B_gw = Buf()
        gwf = gw[:].rearrange("p a b n d -> p (a b n d)")
        for q4 in range(4):
            S.dma("pool", lambda e, q4=q4: e.dma_start(out=gwf[:, q4 * 1024:(q4 + 1) * 1024], in_=g.rgw[l, :, q4 * 1024:(q4 + 1) * 1024]), writes=[B_gw])
        wts = [sb0("rg_w%d" % i, [128, 16, 256], BF16) for i in range(2)]
        hts = [sb0("rg_h%d" % i, [128, 16, NT], BF16) for i in range(2)]
        rxl = sb0("rg_rxl", [128, TL + 3], F32)
        rxc = sb0("rg_rxc", [128, TC + 3], F32)
        gg = sb0("rg_gg", [128, T], BF16)
        xc = sb0("rg_xc", [128, T], F32)
        xcb = sb0("rg_xcb", [128, T], BF16)
        av = sb0("rg_a", [128, T], F32)
        bt = sb0("rg_bt", [128, T], F32)
        hs = [sb0("rg_hs%d" % d, [128, T], F32) for d in range(2)]
        t1 = sb0("rg_t1", [128, NT], F32)
        t2 = sb0("rg_t2", [128, NT], F32)
        t3 = sb0("rg_t3", [128, NT], F32)
        outr = sb0("rg_outr", [128, T], BF16)
        B_w, B_h = [Buf(), Buf()], [Buf(), Buf()]
        B_rx, B_gg, B_xc, B_xcb, B_a, B_bt = Buf(), Buf(), Buf(), Buf(), Buf(), Buf()
        B_hs = [Buf(), Buf()]
        B_t1, B_t2, B_t3, B_outr = Buf(), Buf(), Buf(), Buf()
        for n_ in range(8):
            wj = n_ % 2
            src = g.w_rg[l, n_].rearrange("(kc p) n -> p kc n", p=128)
            for q2 in range(2):
                S.dma("pool", lambda e, wj=wj, src=src, q2=q2: e.dma_start(out=wts[wj][:, q2 * 8:(q2 + 1) * 8, :], in_=src[:, q2 * 8:(q2 + 1) * 8, :]), writes=[B_w[wj]])
            S.op("pool", lambda e: e.memset(rxl[:, 0:2], 0.0), writes=[B_rx])
            S.op("pool", lambda e: e.memset(rxl[:, TL + 2:TL + 3], 0.0), writes=[B_rx])
            S.op("pool", lambda e: e.memset(rxc[:, 0:2], 0.0), writes=[B_rx])
            S.op("pool", lambda e: e.memset(rxc[:, TC + 2:TC + 3], 0.0), writes=[B_rx])
            for i, (t0, n, which) in enumerate(tiles):
                j = i % 2
                hsrc = g.hT[:, t0:t0 + n].rearrange("(kc p) t -> p kc t", p=128)
                S.dma("sp", lambda e, j=j, hsrc=hsrc, n=n: e.dma_start(out=hts[j][:, :, :n], in_=hsrc), reads=tb(g.B_hT_t, t0, t0 + n), writes=[B_h[j]])
                for blk in range(2):
                    ps, Bp = g.ps[blk], g.B_ps[blk]
                    for kc in range(16):
                        S.op("pe", lambda e, ps=ps, kc=kc, blk=blk, j=j, n=n, wj=wj: e.matmul(
                            ps[:, :n], lhsT=wts[wj][:, kc, blk * 128:(blk + 1) * 128], rhs=hts[j][:, kc, :n],
                            start=(kc == 0), stop=(kc == 15)), reads=[B_w[wj], B_h[j]], writes=[Bp])
                    if blk == 0:
                        dst = rxl[:, 2 + t0:2 + t0 + n] if which == 0 else rxc[:, 2 + t0 - TL:2 + t0 - TL + n]
                        S.op("act", lambda e, ps=ps, n=n, dst=dst: e.activation(out=dst, in_=ps[:, :n], func=AF.Copy), reads=[Bp], writes=[B_rx])
                    else:
                        S.op("act", lambda e, ps=ps, n=n, t0=t0: e.activation(out=gg[:, t0:t0 + n], in_=ps[:, :n], func=AF.Gelu), reads=[Bp], writes=[B_gg])
            cw = [g.vecs[:, VO["rcw"] + l * 32 + k * 8 + n_: VO["rcw"] + l * 32 + k * 8 + n_ + 1] for k in range(4)]
            cb = g.vecs[:, VO["rcb"] + l * 8 + n_: VO["rcb"] + l * 8 + n_ + 1]
            for (rx, o0, nn) in ((rxl, 0, TL), (rxc, TL, TC)):
                S.op("dve", lambda e, rx=rx, o0=o0, nn=nn, c0_=cw[0], cb=cb: e.tensor_scalar(out=xc[:, o0:o0 + nn], in0=rx[:, 0:nn], scalar1=c0_, scalar2=cb, op0=ALU.mult, op1=ALU.add), reads=[B_rx, Bc], writes=[B_xc])
                for k in range(1, 4):
                    S.op("dve", lambda e, rx=rx, o0=o0, nn=nn, k=k, ck=cw[k]: e.scalar_tensor_tensor(out=xc[:, o0:o0 + nn], in0=rx[:, k:k + nn], scalar=ck, in1=xc[:, o0:o0 + nn], op0=ALU.mult, op1=ALU.add), reads=[B_rx, B_xc, Bc], writes=[B_xc])
            S.op("act", lambda e: e.activation(out=xcb[:], in_=xc[:], func=AF.Copy), reads=[B_xc], writes=[B_xcb])
            for d in range(2):
                ba = g.vecs[:, VO["rba"] + l * 16 + d * 8 + n_: VO["rba"] + l * 16 + d * 8 + n_ + 1]
                bx = g.vecs[:, VO["rbx"] + l * 16 + d * 8 + n_: VO["rbx"] + l * 16 + d * 8 + n_ + 1]
                nsp = g.nsp8[:, l, d * 8 + n_: d * 8 + n_ + 1]
                for (t0, n, which) in tiles:
                    S.op("pe", lambda e, d=d, t0=t0, n=n, n_=n_: e.matmul(g.ps[2][:, :n], lhsT=gw[:, d, 0, n_, :], rhs=xcb[:, t0:t0 + n], start=True, stop=True), reads=[B_gw, B_xcb], writes=[g.B_ps[2]])
                    S.op("pe", lambda e, d=d, t0=t0, n=n, n_=n_: e.matmul(g.ps[3][:, :n], lhsT=gw[:, d, 1, n_, :], rhs=xcb[:, t0:t0 + n], start=True, stop=True), reads=[B_gw, B_xcb], writes=[g.B_ps[3]])
                    S.op("act", lambda e, n=n, ba=ba: e.activation(out=t1[:, :n], in_=g.ps[2][:, :n], func=AF.Sigmoid, bias=ba, scale=1.0), reads=[g.B_ps[2], Bc], writes=[B_t1])
                    S.op("act", lambda e, n=n, t0=t0, nsp=nsp: e.activation(out=av[:, t0:t0 + n], in_=t1[:, :n], func=AF.Exp, scale=nsp), reads=[B_t1, Bc], writes=[B_a])
                    S.op("act", lambda e, n=n, t0=t0: e.activation(out=t2[:, :n], in_=av[:, t0:t0 + n], func=AF.Square), reads=[B_a], writes=[B_t2])
                    S.op("act", lambda e, n=n: e.activation(out=t2[:, :n], in_=t2[:, :n], func=AF.Sqrt, bias=1.0, scale=-1.0), reads=[B_t2], writes=[B_t2])
                    S.op("act", lambda e, n=n, bx=bx: e.activation(out=t3[:, :n], in_=g.ps[3][:, :n], func=AF.Sigmoid, bias=bx, scale=1.0), reads=[g.B_ps[3], Bc], writes=[B_t3])
                    S.op("dve", lambda e, n=n: e.tensor_tensor(out=t2[:, :n], in0=t2[:, :n], in1=t3[:, :n], op=ALU.mult), reads=[B_t2, B_t3], writes=[B_t2])
                    S.op("dve", lambda e, n=n, t0=t0: e.tensor_tensor(out=bt[:, t0:t0 + n], in0=t2[:, :n], in1=xc[:, t0:t0 + n], op=ALU.mult), reads=[B_t2, B_xc], writes=[B_bt])
                if d == 0:
                    S.op("dve", lambda e: e.tensor_tensor_scan(out=hs[0][:, TL:T], data0=av[:, TL:T], data1=bt[:, TL:T], initial=0.0, op0=ALU.mult, op1=ALU.add), reads=[B_a, B_bt], writes=[B_hs[0]])
                    S.op("dve", lambda e: e.tensor_tensor_scan(out=hs[0][:, 0:TL], data0=av[:, 0:TL], data1=bt[:, 0:TL], initial=hs[0][:, T - 1:T], op0=ALU.mult, op1=ALU.add), reads=[B_a, B_bt, B_hs[0]], writes=[B_hs[0]], force_same=True)
                else:
                    S.op("dve", lambda e: e.tensor_tensor_scan(out=hs[1][:, TL:T][:, ::-1], data0=av[:, TL:T][:, ::-1], data1=bt[:, TL:T][:, ::-1], initial=0.0, op0=ALU.mult, op1=ALU.add), reads=[B_a, B_bt], writes=[B_hs[1]])
                    S.op("dve", lambda e: e.tensor_tensor_scan(out=hs[1][:, 0:TL][:, ::-1], data0=av[:, 0:TL][:, ::-1], data1=bt[:, 0:TL][:, ::-1], initial=hs[1][:, TL:TL + 1], op0=ALU.mult, op1=ALU.add), reads=[B_a, B_bt, B_hs[1]], writes=[B_hs[1]], force_same=True)
            S.op("dve", lambda e: e.tensor_tensor(out=hs[0][:], in0=hs[0][:], in1=hs[1][:], op=ALU.add), reads=[B_hs[0], B_hs[1]], writes=[B_hs[0]])
            S.op("dve", lambda e: e.tensor_tensor(out=outr[:], in0=hs[0][:], in1=gg[:], op=ALU.mult), reads=[B_hs[0], B_gg], writes=[B_outr])
            S.dma("sp", lambda e, n_=n_: e.dma_start(out=g.mT[1024 + n_ * 128:1024 + (n_ + 1) * 128, :], in_=outr[:]), reads=[B_outr], writes=[g.B_mT])
        g.flush()


def phase_a(g, l, last):
    S, nc = g.S, g.nc
    Bc = g.B_const
    NT = 256
    with contextlib.ExitStack() as st:
        def sb(name, shape, dt):
            return st.enter_context(nc.sbuf_tensor(_u() + name, shape, dt))
        wo = sb("a_wo", [128, 16, D], BF16)
        mts = [sb("a_m%d" % i, [128, 16, NT], BF16) for i in range(2)]
        xts = [sb("a_x%d" % i, [128, 16, NT], F32) for i in range(2)]
        hto = [sb("a_h%d" % i, [128, 16, NT], BF16) for i in range(2)]
        mix = sb("a_mix", [128, 16, NT], F32)
        sq = sb("a_sq", [128, 16, NT], BF16)
        tmp = sb("a_tmp", [128, NT], F32)
        rstd = sb("a_rstd", [128, NT], F32)
        B_wo, B_m, B_x, B_h = Buf(), [Buf(), Buf()], [Buf(), Buf()], [Buf(), Buf()]
        B_mix, B_sq, B_tmp, B_rstd = Buf(), Buf(), Buf(), Buf()
        src = g.w_out[l].rearrange("(kc p) n -> p kc n", p=128)
        for kc4 in range(0, 16, 4):
            for hh in range(4):
                S.dma("pool", lambda e, kc4=kc4, hh=hh: e.dma_start(out=wo[:, kc4:kc4 + 4, hh * 512:(hh + 1) * 512], in_=src[:, kc4:kc4 + 4, hh * 512:(hh + 1) * 512]), writes=[B_wo])
        for jf in range(44):
            csrc = g.w_up[l, jf].rearrange("(kc p) n -> p kc n", p=128)
            cdst = g.wub[jf].rearrange("p (kc n) -> p kc n", n=256)
            for q2 in range(2):
                S.dma("pool", lambda e, csrc=csrc, cdst=cdst, q2=q2: e.dma_start(out=cdst[:, q2 * 8:(q2 + 1) * 8, :], in_=csrc[:, q2 * 8:(q2 + 1) * 8, :]), writes=[g.B_wub[jf]])
        for ob in range(16):
            csrc = g.w_dn[l, ob].rearrange("(fc p) n -> p fc n", p=128)
            cdst = g.wdb[ob].rearrange("p (fc n) -> p fc n", n=128)
            for q2 in range(4):
                S.dma("pool", lambda e, csrc=csrc, cdst=cdst, q2=q2: e.dma_start(out=cdst[:, q2 * 11:(q2 + 1) * 11, :], in_=csrc[:, q2 * 11:(q2 + 1) * 11, :]), writes=[g.B_wdb[ob]])
        tiles = tok_tiles(NT)
        if last:
            tiles = [t for t in tiles if t[2] == 0]
        import os
        AB = int(os.environ.get("MK_AB", "99"))
        if AB < 99:
            tiles = tiles[:1]
        def issue_load(i):
            t0, n, which = tiles[i]
            j = i % 2
            msrc = g.mT[:, t0:t0 + n].rearrange("(kc p) t -> p kc t", p=128)
            xsrc = g.res[:, t0:t0 + n].rearrange("(kc p) t -> p kc t", p=128)
            S.dma("sp", lambda e, j=j, msrc=msrc: e.dma_start(out=mts[j][:], in_=msrc), reads=[g.B_mT], writes=[B_m[j]])
            S.dma("sp", lambda e, j=j, xsrc=xsrc: e.dma_start(out=xts[j][:], in_=xsrc), reads=tb(g.B_res_t, t0, t0 + n), writes=[B_x[j]])
        issue_load(0)
        for i, (t0, n, which) in enumerate(tiles):
            j = i % 2
            if i + 1 < len(tiles):
                issue_load(i + 1)
            if AB < 1:
                continue
            for ob in range(16):
                pb = ob % 4
                ps, Bp = g.ps[pb], g.B_ps[pb]
                for kc in range(16):
                    S.op("pe", lambda e, ps=ps, kc=kc, ob=ob, j=j: e.matmul(ps[:, :NT], lhsT=wo[:, kc, ob * 128:(ob + 1) * 128], rhs=mts[j][:, kc, :],
                                                                             start=(kc == 0), stop=(kc == 15)), reads=[B_wo, B_m[j]], writes=[Bp])
                S.op("dve", lambda e, ps=ps, ob=ob: e.tensor_copy(out=mix[:, ob, :], in_=ps[:, :NT]), reads=[Bp], writes=[B_mix])
                S.op("act", lambda e, ob=ob: e.activation(out=sq[:, ob, :], in_=mix[:, ob, :], func=AF.Square), reads=[B_mix], writes=[B_sq])
            if AB < 2:
                continue
            ps, Bp = g.ps[4], g.B_ps[4]
            for kc in range(16):
                S.op("pe", lambda e, kc=kc, ps=ps: e.matmul(ps[:, :NT], lhsT=g.ones_bf[:], rhs=sq[:, kc, :], start=(kc == 0), stop=(kc == 15)), reads=[B_sq, Bc], writes=[Bp])
            S.op("act", lambda e, ps=ps: e.activation(out=rstd[:], in_=ps[:, :NT], func=AF.Sqrt, bias=g.epsb[:, 0:1], scale=1.0 / D), reads=[Bp, Bc], writes=[B_rstd])
            S.op("dve", lambda e: e.reciprocal(out=rstd[:], in_=rstd[:]), reads=[B_rstd], writes=[B_rstd])
            if AB < 3:
                continue
            G = g.G1[:, l, :, which]
            for kc in range(16):
                S.op("dve", lambda e, kc=kc, gk=G[:, kc:kc + 1]: e.scalar_tensor_tensor(out=tmp[:], in0=mix[:, kc, :], scalar=gk, in1=rstd[:], op0=ALU.mult, op1=ALU.mult), reads=[B_mix, B_rstd, Bc], writes=[B_tmp])
                S.op("dve", lambda e, kc=kc, j=j: e.tensor_tensor(out=xts[j][:, kc, :], in0=xts[j][:, kc, :], in1=tmp[:], op=ALU.add), reads=[B_tmp, B_x[j]], writes=[B_x[j]])
            if AB < 4:
                continue
            xdst = g.res[:, t0:t0 + n].rearrange("(kc p) t -> p kc t", p=128)
            S.dma("sp", lambda e, j=j, xdst=xdst: e.dma_start(out=xdst, in_=xts[j][:]), reads=[B_x[j]], writes=tb(g.B_res_t, t0, t0 + n))
            if AB < 5:
                continue
            emit_norm_mod(g, xts[j], B_x[j], NT, g.A2[:, l, :, which], g.modv[:, l, 48:64, which], sq, B_sq, tmp, B_tmp, rstd, B_rstd, hto[j], B_h[j], 5)
            hdst = g.h2T[:, t0:t0 + n].rearrange("(kc p) t -> p kc t", p=128)
            S.dma("sp", lambda e, j=j, hdst=hdst: e.dma_start(out=hdst, in_=hto[j][:]), reads=[B_h[j]], writes=tb(g.B_h2T_t, t0, t0 + n))
            if "xa" in g.taps and l == 0:
                S.dma("sp", lambda e, j=j, t0=t0, n=n: e.dma_start(out=g.taps["xa"][:, t0:t0 + n].rearrange("(kc p) t -> p kc t", p=128), in_=xts[j][:]), reads=[B_x[j]], writes=[g.B_tap])
        g.flush()


def phase_b(g, l, last, final):
    S, nc = g.S, g.nc
    Bc = g.B_const
    TS = 512
    NS = 256
    with contextlib.ExitStack() as st:
        def sb(name, shape, dt):
            return st.enter_context(nc.sbuf_tensor(_u() + name, shape, dt))
        h2 = sb("b_h2", [128, 16, TS + 2], BF16)
        ge = sb("b_ge", [128, 44, TS], BF16)
        wup = [sb("b_wu%d" % i, [128, 16, 256], BF16) for i in range(2)]
        wdn = [sb("b_wd%d" % i, [128, 44, 128], BF16) for i in range(2)]
        fl = sb("b_fl", [128, 16, TS], F32)
        sq = sb("b_sq", [128, 16, NS], BF16)
        xt = sb("b_x", [128, 16, NS], F32)
        hn = sb("b_hn", [128, 16, NS], BF16)
        ca = sb("b_ca", [128, NS], F32)
        cv = sb("b_cv", [128, NS], F32)
        tmp = sb("b_tmp", [128, NS], F32)
        rstd = sb("b_rstd", [128, NS], F32)
        B_h2, B_ge, B_wu, B_wd = Buf(), Buf(), [Buf(), Buf()], [Buf(), Buf()]
        B_fl, B_sq, B_x, B_hn, B_ca, B_cv, B_tmp, B_rstd = Buf(), Buf(), Buf(), Buf(), Buf(), Buf(), Buf(), Buf()
        stiles = [(t0, TS, 0) for t0 in range(0, TL, TS)]
        if not last:
            stiles.append((TL, TC, 1))
        wi = 0
        di = 0
        import os
        if os.environ.get("MK_NST"):
            stiles = stiles[:int(os.environ["MK_NST"])]
        for (t0, n, which) in stiles:
            seq0 = 0 if which == 0 else TL
            seq1 = TL if which == 0 else T
            lo = max(t0 - 1, seq0)
            hi = min(t0 + n + 1, seq1)
            if lo == t0:
                S.op("dve", lambda e: e.memset(h2[:, :, 0:1], 0.0), writes=[B_h2])
            if hi == t0 + n:
                S.op("dve", lambda e, n=n: e.memset(h2[:, :, n + 1:n + 2], 0.0), writes=[B_h2])
            hsrc = g.h2T[:, lo:hi].rearrange("(kc p) t -> p kc t", p=128)
            S.dma("sp", lambda e, hsrc=hsrc, lo=lo, hi=hi, t0=t0: e.dma_start(out=h2[:, :, lo - (t0 - 1):hi - (t0 - 1)], in_=hsrc), reads=tb(g.B_h2T_t, lo, hi), writes=[B_h2])
            nsub = n // NS
            import os
            BB = int(os.environ.get("MK_BB", "99"))
            if BB < 1:
                g.flush()
                continue
            for jf in range(44):
                wj = wi % 2
                wi += 1
                S.dma("sp", lambda e, wj=wj, jf=jf: e.dma_start(out=wup[wj][:].rearrange("p kc n -> p (kc n)"), in_=g.wub[jf]), reads=[g.B_wub[jf]], writes=[B_wu[wj]])
                fw = [[g.vecs[:, VO["fcw"] + l * 264 + k * 88 + half * 44 + jf: VO["fcw"] + l * 264 + k * 88 + half * 44 + jf + 1] for k in range(3)] for half in range(2)]
                fb = [g.vecs[:, VO["fcb"] + l * 88 + half * 44 + jf: VO["fcb"] + l * 88 + half * 44 + jf + 1] for half in range(2)]
                for s in range(nsub):
                    c0 = s * NS
                    for half in range(2):
                        pb = (s * 2 + half) % 4
                        ps, Bp = g.ps[pb], g.B_ps[pb]
                        HH = int(os.environ.get("MK_HH", "2"))
                        for kc in range(16):
                            S.op("pe", lambda e, ps=ps, kc=kc, wj=wj, half=half, c0=c0: e.matmul(
                                ps[:, :NS + HH], lhsT=wup[wj][:, kc, half * 128:(half + 1) * 128], rhs=h2[:, kc, c0:c0 + NS + HH],
                                start=(kc == 0), stop=(kc == 15)), reads=[B_wu[wj], B_h2], writes=[Bp])
                        dst, Bd = (ca, B_ca) if half == 0 else (cv, B_cv)
                        if os.environ.get("MK_EE", "") == "noact":
                            continue
                        S.op("act", lambda e, ps=ps, dst=dst, half=half, fw=fw, fb=fb: e.activation(out=dst[:], in_=ps[:, HH // 2:NS + HH // 2], func=AF.Identity, bias=fb[half], scale=fw[half][1]), reads=[Bp, Bc], writes=[Bd])
                        EE = os.environ.get("MK_EE", "")
                        if EE == "noact2":
                            continue
                        S.op("dve", lambda e, ps=ps, dst=dst, half=half, fw=fw: e.scalar_tensor_tensor(out=dst[:], in0=ps[:, 0:NS], scalar=fw[half][0], in1=dst[:], op0=ALU.mult, op1=ALU.add), reads=[Bp, Bd, Bc], writes=[Bd])
                        if EE == "one":
                            continue
                        S.op("dve", lambda e, ps=ps, dst=dst, half=half, fw=fw: e.scalar_tensor_tensor(out=dst[:], in0=ps[:, 2:NS + 2], scalar=fw[half][2], in1=dst[:], op0=ALU.mult, op1=ALU.add), reads=[Bp, Bd, Bc], writes=[Bd])
                    if BB < 2:
                        continue
                    S.op("act", lambda e: e.activation(out=ca[:], in_=ca[:], func=AF.Gelu), reads=[B_ca], writes=[B_ca])
                    S.op("dve", lambda e, jf=jf, c0=c0: e.tensor_tensor(out=ge[:, jf, c0:c0 + NS], in0=ca[:], in1=cv[:], op=ALU.mult), reads=[B_ca, B_cv], writes=[B_ge])
            if BB < 3:
                g.flush()
                continue
            for ob in range(16):
                dj = di % 2
                di += 1
                S.dma("sp", lambda e, dj=dj, ob=ob: e.dma_start(out=wdn[dj][:].rearrange("p fc n -> p (fc n)"), in_=g.wdb[ob]), reads=[g.B_wdb[ob]], writes=[B_wd[dj]])
                for s in range(nsub):
                    c0 = s * NS
                    pb = 4 + (s % 2)
                    ps, Bp = g.ps[pb], g.B_ps[pb]
                    for fc in range(44):
                        S.op("pe", lambda e, ps=ps, fc=fc, dj=dj, c0=c0: e.matmul(ps[:, :NS], lhsT=wdn[dj][:, fc, :], rhs=ge[:, fc, c0:c0 + NS], start=(fc == 0), stop=(fc == 43)),
                             reads=[B_wd[dj], B_ge], writes=[Bp])
                    S.op("act", lambda e, ps=ps, ob=ob, c0=c0: e.activation(out=fl[:, ob, c0:c0 + NS], in_=ps[:, :NS], func=AF.Copy), reads=[Bp], writes=[B_fl])
            if BB < 4:
                g.flush()
                continue
            S.same = True
            for s in range(nsub):
                c0 = s * NS
                tt = t0 + c0
                xsrc = g.res[:, tt:tt + NS].rearrange("(kc p) t -> p kc t", p=128)
                S.dma("sp", lambda e, xsrc=xsrc: e.dma_start(out=xt[:], in_=xsrc), reads=tb(g.B_res_t, tt, tt + NS), writes=[B_x])
                for kc in range(16):
                    S.op("act", lambda e, kc=kc, c0=c0: e.activation(out=sq[:, kc, :], in_=fl[:, kc, c0:c0 + NS], func=AF.Square), reads=[B_fl], writes=[B_sq])
                ps, Bp = g.ps[6], g.B_ps[6]
                for kc in range(16):
                    S.op("pe", lambda e, kc=kc, ps=ps: e.matmul(ps[:, :NS], lhsT=g.ones_bf[:], rhs=sq[:, kc, :], start=(kc == 0), stop=(kc == 15)), reads=[B_sq, Bc], writes=[Bp])
                S.op("act", lambda e, ps=ps: e.activation(out=rstd[:], in_=ps[:, :NS], func=AF.Sqrt, bias=g.epsb[:, 0:1], scale=1.0 / D), reads=[Bp, Bc], writes=[B_rstd])
                S.op("dve", lambda e: e.reciprocal(out=rstd[:], in_=rstd[:]), reads=[B_rstd], writes=[B_rstd])
                G = g.G2[:, l, :, which]
                for kc in range(16):
                    S.op("dve", lambda e, kc=kc, c0=c0, gk=G[:, kc:kc + 1]: e.scalar_tensor_tensor(out=tmp[:], in0=fl[:, kc, c0:c0 + NS], scalar=gk, in1=rstd[:], op0=ALU.mult, op1=ALU.mult), reads=[B_fl, B_rstd, Bc], writes=[B_tmp])
                    S.op("dve", lambda e, kc=kc: e.tensor_tensor(out=xt[:, kc, :], in0=xt[:, kc, :], in1=tmp[:], op=ALU.add), reads=[B_tmp, B_x], writes=[B_x])
                if final:
                    if which == 0:
                        ydst = g.y[:, tt:tt + NS].rearrange("(kc p) t -> p kc t", p=128)
                        S.dma("sp", lambda e, ydst=ydst: e.dma_start(out=ydst, in_=xt[:]), reads=[B_x], writes=[g.B_y])
                else:
                    xdst = g.res[:, tt:tt + NS].rearrange("(kc p) t -> p kc t", p=128)
                    S.dma("sp", lambda e, xdst=xdst: e.dma_start(out=xdst, in_=xt[:]), reads=[B_x], writes=tb(g.B_res_t, tt, tt + NS))
                    emit_norm_mod(g, xt, B_x, NS, g.A1[:, l + 1, :, which], g.modv[:, l + 1, 0:16, which], sq, B_sq, tmp, B_tmp, rstd, B_rstd, hn, B_hn, 6)
                    hdst = g.hT[:, tt:tt + NS].rearrange("(kc p) t -> p kc t", p=128)
                    S.dma("sp", lambda e, hdst=hdst: e.dma_start(out=hdst, in_=hn[:]), reads=[B_hn], writes=tb(g.B_hT_t, tt, tt + NS))
            S.same = False
            g.flush()


def fm(a, inner):
    a = np.asarray(a, dtype=np.float32)
    lead = a.shape[:-1]
    x = a.shape[-1] // 128
    a = a.reshape(lead + (x, 128))
    a = np.moveaxis(a, -1, 0)
    return np.ascontiguousarray(a).reshape(128, -1)


def prep_inputs(inp):
    f32 = np.float32
    vec = np.zeros((128, NV), f32)

    def put(name, arr):
        vec[:, VO[name]:VO[name] + arr.shape[1]] = arr
    put("b_mod", fm(inp["b_mod"], 96))
    put("norm_g", fm(inp["norm_g"], 16))
    put("lb", fm(inp["hg_lower_bounds"], 8))
    put("hgg", fm(inp["hg_norm_g"], 8))
    put("rcw", fm(inp["rg_conv_w"], 8))
    put("rcb", fm(inp["rg_conv_b"], 8))
    put("rba", fm(inp["rg_ba"], 8))
    put("rbx", fm(inp["rg_bx"], 8))
    put("rlam", fm(inp["rg_lambda"], 8))
    put("fcw", fm(inp["ffn_conv_w"], 88))
    put("fcb", fm(inp["ffn_conv_b"], 88))
    wa = np.asarray(inp["rg_wa"], f32)
    wx = np.asarray(inp["rg_wx"], f32)
    rgw = np.stack([wa, wx], axis=2)
    rgw = np.ascontiguousarray(rgw.transpose(0, 4, 1, 2, 3, 5)).reshape(NL, 128, 4096)
    w_in = np.asarray(inp["w_in"], f32)
    hgc = w_in[:, :, :5120].reshape(NL, D, 5, 8, 128)
    w_hg = np.ascontiguousarray(hgc.transpose(0, 3, 1, 2, 4)).reshape(NL, 8, D, 640)
    rgc = w_in[:, :, 5120:].reshape(NL, D, 2, 8, 128)
    w_rg = np.ascontiguousarray(rgc.transpose(0, 3, 1, 2, 4)).reshape(NL, 8, D, 256)
    w_up = np.asarray(inp["ffn_w_up"], f32).reshape(NL, D, 2, 44, 128)
    w_up = np.ascontiguousarray(w_up.transpose(0, 3, 1, 2, 4)).reshape(NL, 44, D, 256)
    w_dn = np.asarray(inp["ffn_w_down"], f32).reshape(NL, DFF, 16, 128)
    w_dn = np.ascontiguousarray(w_dn.transpose(0, 2, 1, 3))
    shared = dict(vec=vec, rgw=rgw, w_mod=np.ascontiguousarray(inp["w_mod"], dtype=f32), w_hg=w_hg, w_rg=w_rg,
                  w_out=np.ascontiguousarray(inp["w_out"], dtype=f32), w_up=w_up, w_dn=w_dn)
    x = np.asarray(inp["x"], f32)
    ctx = np.asarray(inp["ctx"], f32)
    c = np.asarray(inp["c"], f32)
    c_ctx = np.asarray(inp["c_ctx"], f32)
    per = []
    for b in range(4):
        res0 = np.ascontiguousarray(np.concatenate([x[b].T, ctx[b].T], axis=1))
        cv = np.stack([c[b], c_ctx], axis=-1).reshape(16, 128, 2).transpose(1, 0, 2)
        per.append(dict(res0=res0, cvec=np.ascontiguousarray(cv)))
    return shared, per


_NC_CACHE = {}


def kernel(**inputs):
    shared, per = prep_inputs(inputs)
    if "nc" not in _NC_CACHE:
        _NC_CACHE["nc"] = build()
    nc = _NC_CACHE["nc"]
    in_maps = []
    for core in range(8):
        m = dict(shared)
        m.update(per[core % 4])
        in_maps.append(m)
    res = run_bass_kernel_spmd(nc, in_maps, core_ids=list(range(8)))
    out = np.stack([res.results[b]["y"].T for b in range(4)], axis=0)
    return np.ascontiguousarray(out.astype(np.float32))
```

```python
import contextlib
import os as _os
import numpy as np
import concourse.bass as bass
import concourse.mybir as mybir
from concourse.bass_utils import run_bass_kernel_spmd

F32 = mybir.dt.float32
BF16 = mybir.dt.bfloat16
I32 = mybir.dt.int32
AF = mybir.ActivationFunctionType
ALU = mybir.AluOpType

D = 2048
TL = 4096
TC = 256
T = TL + TC
NL = 4
NCH = 68
DFF = 5632
EPS = 1e-6
SAME_ENGINE_SYNC = bool(int(_os.environ.get('MK_SAME', '1')))
N_DMA_SEMS = 8
N_POOL_SEMS = 3
SEM_RESET_AT = int(_os.environ.get('MK_RESET', '1000000000'))

VO = {}
_o = 0
for _n, _sz in [("b_mod", 4 * 96), ("norm_g", 4 * 4 * 16), ("lb", 4 * 2 * 8), ("hgg", 4 * 8),
                ("rcw", 4 * 4 * 8), ("rcb", 4 * 8), ("rba", 4 * 2 * 8), ("rbx", 4 * 2 * 8), ("rlam", 4 * 2 * 8),
                ("fcw", 4 * 3 * 88), ("fcb", 4 * 88)]:
    VO[_n] = _o
    _o += _sz
NV = _o


_UC = [0]


def _u():
    _UC[0] += 1
    return "t%d_" % _UC[0]


_ALL_BUFS = []


class Buf:
    __slots__ = ("name", "w", "r")

    def __init__(self, name=""):
        self.name = name
        self.w = {}
        self.r = {}
        _ALL_BUFS.append(self)


class Sched:
    ENGS = ("pe", "dve", "act", "pool", "sp")

    def __init__(self, nc, sems, dma_sems):
        self.nc = nc
        self.sem = dict(sems)
        self.qsems = {}
        for q, lst in dma_sems.items():
            self.qsems[q] = []
            for i, s in enumerate(lst):
                self.sem[("dma", q, i)] = s
                self.qsems[q].append(("dma", q, i))
        self.qrr = {q: 0 for q in dma_sems}
        self.same = False
        self.prog = {e: [] for e in self.ENGS}
        self.cnt = {k: 0 for k in self.sem}
        self.known = {e: {} for e in self.ENGS}
        self.dma_rr = 0
        self.ninst = 0

    def _waits(self, e, reads, writes, extra=(), force_same=False, dma_write=False):
        need = {}
        for b in reads:
            for k, v in b.w.items():
                if need.get(k, 0) < v:
                    need[k] = v
        for b in writes:
            for k, v in b.w.items():
                if dma_write and isinstance(k, tuple):
                    continue
                if need.get(k, 0) < v:
                    need[k] = v
            for k, v in b.r.items():
                if need.get(k, 0) < v:
                    need[k] = v
        for k, v in extra:
            if need.get(k, 0) < v:
                need[k] = v
        out = []
        kn = self.known[e]
        for k, v in need.items():
            if k == e and not force_same and (e == "pe" or (e != "pool" and not (SAME_ENGINE_SYNC or self.same))):
                continue
            if kn.get(k, 0) >= v:
                continue
            kn[k] = v
            out.append((k, v))
        return out

    def op(self, e, fn, reads=(), writes=(), force_same=False):
        waits = self._waits(e, reads, writes, force_same=force_same)
        self.cnt[e] += 1
        t = self.cnt[e]
        self.prog[e].append((waits, fn, e, 1))
        for b in reads:
            b.r[e] = t
        for b in writes:
            b.w = {e: t}
            b.r = {}
        self.ninst += 1

    def dma(self, q, fn, reads=(), writes=()):
        i = self.qrr[q]
        self.qrr[q] = (i + 1) % len(self.qsems[q])
        k = self.qsems[q][i]
        extra = [(k, self.cnt[k])] if self.cnt[k] > 0 else []
        waits = self._waits(q, reads, writes, extra, force_same=True, dma_write=True)
        self.cnt[k] += 16
        t = self.cnt[k]
        self.prog[q].append((waits, fn, k, 16))
        for b in reads:
            b.r[k] = t
        for b in writes:
            b.w = {kk: vv for kk, vv in b.w.items() if isinstance(kk, tuple)}
            b.w[k] = t
            b.r = {}
        self.ninst += 1

    def reset_counts(self):
        for k in self.cnt:
            self.cnt[k] = 0
        self.known = {e: {} for e in self.ENGS}
        for b in _ALL_BUFS:
            b.w = {}
            b.r = {}

    def barrier(self):
        for e in self.ENGS:
            waits = []
            kn = self.known[e]
            for k, v in self.cnt.items():
                if v > 0 and k != e and kn.get(k, 0) < v:
                    kn[k] = v
                    waits.append((k, v))
            if e != "pe" and self.cnt[e] > 0 and kn.get(e, 0) < self.cnt[e]:
                kn[e] = self.cnt[e]
                waits.append((e, self.cnt[e]))
            if waits:
                self.prog[e].append((waits, None, None, 0))

    def emit(self, block):
        engmap = {"pe": block.tensor, "dve": block.vector, "act": block.scalar,
                  "pool": block.gpsimd, "sp": block.sync}
        sem = self.sem
        for e in self.ENGS:
            prog = self.prog[e]
            if not prog:
                continue

            def body(eng, prog=prog):
                for waits, fn, k, inc in prog:
                    for wk, wv in waits:
                        eng.wait_ge(sem[wk], wv)
                    if fn is not None:
                        fn(eng).then_inc(sem[k], inc)
            engmap[e](body)
            self.prog[e] = []


class Ctx:
    pass


def build(nlayers=NL, taps=()):
    nc = bass.Bass("TRN2", target_bir_lowering=False)
    g = Ctx()
    g.nc = nc
    dt_in = lambda n, s: nc.dram_tensor(n, s, F32, kind="ExternalInput").ap()
    g.res0 = dt_in("res0", [D, T])
    g.cvec = dt_in("cvec", [128, 16, 2])
    g.vec = dt_in("vec", [128, NV])
    g.rgw = dt_in("rgw", [NL, 128, 4096])
    g.w_mod = dt_in("w_mod", [NL, D, 6 * D])
    g.w_hg = dt_in("w_hg", [NL, 8, D, 640])
    g.w_rg = dt_in("w_rg", [NL, 8, D, 256])
    g.w_out = dt_in("w_out", [NL, D, D])
    g.w_up = dt_in("w_up", [NL, 44, D, 256])
    g.w_dn = dt_in("w_dn", [NL, 16, DFF, 128])
    g.y = nc.dram_tensor("y", [D, TL], F32, kind="ExternalOutput").ap()
    g.res = nc.dram_tensor("res", [D, T], F32, kind="Internal").ap()
    g.hT = nc.dram_tensor("hT", [D, T], BF16, kind="Internal").ap()
    g.mT = nc.dram_tensor("mT", [D, T], BF16, kind="Internal").ap()
    g.h2T = nc.dram_tensor("h2T", [D, T], BF16, kind="Internal").ap()
    g.wub = nc.dram_tensor("wub", [44, 128, 16 * 256], BF16, kind="Internal").ap()
    g.wdb = nc.dram_tensor("wdb", [16, 128, 44 * 128], BF16, kind="Internal").ap()
    g.B_wub = [Buf("wub%d" % i) for i in range(44)]
    g.B_wdb = [Buf("wdb%d" % i) for i in range(16)]
    g.taps = {}
    for name, shape, dt in taps:
        g.taps[name] = nc.dram_tensor("tap_" + name, shape, dt, kind="ExternalOutput").ap()
    g.B_res_t = [Buf("res%d" % i) for i in range(17)]
    g.B_hT_t = [Buf("hT%d" % i) for i in range(17)]
    g.B_h2T_t = [Buf("h2T%d" % i) for i in range(17)]
    g.B_mT = Buf("mT")
    g.B_y = Buf("y")
    g.B_tap = Buf("tap")

    with contextlib.ExitStack() as st:
        def sb(name, shape, dt):
            return st.enter_context(nc.sbuf_tensor(_u() + name, shape, dt))
        g.vecs = sb("vecs", [128, NV], F32)
        g.modv = sb("modv", [128, NL, 96, 2], F32)
        g.A1 = sb("A1", [128, NL, 16, 2], F32)
        g.G1 = sb("G1", [128, NL, 16, 2], F32)
        g.A2 = sb("A2", [128, NL, 16, 2], F32)
        g.G2 = sb("G2", [128, NL, 16, 2], F32)
        g.lbv = sb("lbv", [128, NL, 16], F32)
        g.omlb = sb("omlb", [128, NL, 16], F32)
        g.nsp8 = sb("nsp8", [128, NL, 16], F32)
        g.ones_bf = sb("ones_bf", [128, 128], BF16)
        g.ident_bf = sb("ident_bf", [128, 128], BF16)
        g.epsb = sb("epsb", [128, 1], F32)
        g.scv = sb("scv", [128, 16, 2], F32)
        g.B_const = Buf("const")
        g.ps = [st.enter_context(nc.psum_tensor("ps%d" % i, [128, 512], F32)) for i in range(7)]
        g.ps_bf = st.enter_context(nc.psum_tensor("ps_bf", [128, 1024], BF16))
        g.B_ps = [Buf("ps%d" % i) for i in range(7)]
        g.B_psbf = Buf("psbf")
        sems = {e: st.enter_context(nc.semaphore("s_" + e)) for e in Sched.ENGS}
        dsems = {"sp": [st.enter_context(nc.semaphore("dsp%d" % i)) for i in range(N_DMA_SEMS)],
                 "pool": [st.enter_context(nc.semaphore("dpl%d" % i)) for i in range(N_POOL_SEMS)]}
        S = Sched(nc, sems, dsems)
        g.S = S
        g.blk = None

        def open_block():
            cm = nc.Block()
            g.blk = (cm, cm.__enter__())

        def close_block():
            g.blk[0].__exit__(None, None, None)
            g.blk = None

        def sem_reset():
            close_block()
            with nc.Block() as b2:
                b2.tensor(lambda eng: eng.sem_clear(S.sem["pe"]))
                b2.vector(lambda eng: eng.sem_clear(S.sem["dve"]))
                b2.scalar(lambda eng: eng.sem_clear(S.sem["act"]))
                b2.gpsimd(lambda eng: eng.sem_clear(S.sem["pool"]))

                def spclr(eng):
                    eng.sem_clear(S.sem["sp"])
                    for kk in S.sem:
                        if isinstance(kk, tuple):
                            eng.sem_clear(S.sem[kk])
                b2.sync(spclr)
            S.reset_counts()
            open_block()

        def flush():
            S.barrier()
            S.emit(g.blk[1])
            if max(S.cnt.values()) > SEM_RESET_AT:
                sem_reset()
        g.flush = flush
        open_block()
        try:
            import os
            stop = os.environ.get("MK_STOP", "")
            phase_setup(g)
            flush()
            if stop == "setup":
                return nc
            phase_mod(g, nlayers)
            flush()
            if stop == "mod":
                return nc
            phase_norm1_l0(g)
            flush()
            if "h1" in g.taps:
                S.dma("sp", lambda e: e.dma_start(out=g.taps["h1"], in_=g.hT), reads=g.B_hT_t, writes=[g.B_tap])
                flush()
            if stop == "n1":
                return nc
            for l in range(nlayers):
                for hd in range(8):
                    if os.environ.get("MK_SKIPMIX"):
                        break
                    phase_hg(g, l, hd)
                    flush()
                    if stop == "hg0":
                        return nc
                if stop == "hg":
                    return nc
                if not os.environ.get("MK_SKIPMIX"):
                    S.same = True
                    phase_rg(g, l)
                    S.same = False
                flush()
                if "m" in g.taps and l == 0:
                    S.dma("sp", lambda e: e.dma_start(out=g.taps["m"], in_=g.mT), reads=[g.B_mT], writes=[g.B_tap])
                    flush()
                if stop == "rg":
                    return nc
                if not os.environ.get("MK_SKIPA"):
                    phase_a(g, l, last=(l == NL - 1))
                flush()
                if stop == "a":
                    return nc
                phase_b(g, l, last=(l == NL - 1), final=(l == nlayers - 1))
                flush()
        finally:
            if g.blk is not None:
                close_block()
    return nc


def V(g, name, *idx_shape):
    return g.vecs[:, VO[name]:]


def phase_setup(g):
    S, nc = g.S, g.nc
    Bc = g.B_const
    S.dma("sp", lambda e: e.dma_start(out=g.vecs[:], in_=g.vec), writes=[Bc])
    S.dma("sp", lambda e: e.dma_start(out=g.scv[:], in_=g.cvec), writes=[Bc])
    S.op("pool", lambda e: e.memset(g.ones_bf[:], 1.0), writes=[Bc])
    S.op("pool", lambda e: e.memset(g.epsb[:], EPS), writes=[Bc])
    with g.nc.sbuf_tensor(_u() + "identf", [128, 128], F32) as identf:
        S.op("pool", lambda e: e.memset(identf[:], 1.0), writes=[Bc])
        S.op("pool", lambda e: e.affine_select(out=identf[:], in_=identf[:], pattern=[[1, 128]],
                                                compare_op=ALU.is_equal, fill=0.0, base=0, channel_multiplier=-1),
             reads=[Bc], writes=[Bc])
        S.op("pool", lambda e: e.tensor_copy(out=g.ident_bf[:], in_=identf[:]), reads=[Bc], writes=[Bc])
        S.op("act", lambda e: e.activation(out=g.scv[:], in_=g.scv[:], func=AF.Silu), reads=[Bc], writes=[Bc])
        lbraw = g.vecs[:, VO["lb"]:VO["lb"] + 64].rearrange("p (l x) -> p l x", l=NL)
        with g.nc.sbuf_tensor(_u() + "lbe", [128, NL, 16], F32) as lbe, g.nc.sbuf_tensor(_u() + "lbs", [128, 16], F32) as lbs:
            S.op("act", lambda e: e.activation(out=lbe[:], in_=lbraw, func=AF.Exp), reads=[Bc], writes=[Bc])
            S.op("dve", lambda e: e.tensor_tensor(out=lbs[:], in0=lbe[:, 0, :], in1=lbe[:, 1, :], op=ALU.add), reads=[Bc], writes=[Bc])
            S.op("dve", lambda e: e.tensor_tensor(out=lbs[:], in0=lbs[:], in1=lbe[:, 2, :], op=ALU.add), reads=[Bc], writes=[Bc])
            S.op("dve", lambda e: e.tensor_tensor(out=lbs[:], in0=lbs[:], in1=lbe[:, 3, :], op=ALU.add), reads=[Bc], writes=[Bc])
            S.op("dve", lambda e: e.reciprocal(out=lbs[:], in_=lbs[:]), reads=[Bc], writes=[Bc])
            S.op("dve", lambda e: e.memset(g.lbv[:, 0, :], 0.0), writes=[Bc])
            for l in range(1, NL):
                S.op("dve", lambda e, l=l: e.tensor_tensor(out=lbe[:, l, :], in0=lbe[:, l, :], in1=lbs[:], op=ALU.mult), reads=[Bc], writes=[Bc])
                S.op("dve", lambda e, l=l: e.tensor_tensor(out=g.lbv[:, l, :], in0=g.lbv[:, l - 1, :], in1=lbe[:, l, :], op=ALU.add), reads=[Bc], writes=[Bc])
            S.op("dve", lambda e: e.tensor_scalar(out=g.omlb[:], in0=g.lbv[:], scalar1=-1.0, scalar2=1.0, op0=ALU.mult, op1=ALU.add), reads=[Bc], writes=[Bc])
            lam = g.vecs[:, VO["rlam"]:VO["rlam"] + 64].rearrange("p (l x) -> p l x", l=NL)
            S.op("act", lambda e: e.activation(out=g.nsp8[:], in_=lam, func=AF.Exp, scale=-1.0), reads=[Bc], writes=[Bc])
            S.op("act", lambda e: e.activation(out=g.nsp8[:], in_=g.nsp8[:], func=AF.Ln, bias=1.0, scale=1.0), reads=[Bc], writes=[Bc])
            S.op("dve", lambda e: e.tensor_scalar(out=g.nsp8[:], in0=g.nsp8[:], scalar1=-8.0, scalar2=None, op0=ALU.mult), reads=[Bc], writes=[Bc])
            g.flush()


def phase_mod(g, nlayers):
    S, nc = g.S, g.nc
    Bc = g.B_const
    with contextlib.ExitStack() as st:
        wt = [st.enter_context(nc.sbuf_tensor(_u() + "wmod%d" % i, [128, 16, 512], F32)) for i in range(2)]
        Bw = [Buf("wmod0"), Buf("wmod1")]
        Bp = g.B_ps[0]
        it = 0
        for l in range(nlayers):
            for cb in range(24):
                j = it % 2
                it += 1
                src = g.w_mod[l, :, cb * 512:(cb + 1) * 512].rearrange("(kc p) n -> p kc n", p=128)
                for half in range(2):
                    S.dma("sp", lambda e, j=j, src=src, half=half: e.dma_start(out=wt[j][:, half * 8:(half + 1) * 8, :], in_=src[:, half * 8:(half + 1) * 8, :]), writes=[Bw[j]])
                for mi in range(4):
                    m = cb * 4 + mi
                    for kc in range(16):
                        S.op("pe", lambda e, j=j, mi=mi, kc=kc, m=m: e.matmul(
                            g.ps[0][:, 2 * m:2 * m + 2], lhsT=wt[j][:, kc, mi * 128:(mi + 1) * 128], rhs=g.scv[:, kc, :],
                            start=(kc == 0), stop=(kc == 15)), reads=[Bw[j], Bc], writes=[Bp])
            bm = g.vecs[:, VO["b_mod"] + l * 96: VO["b_mod"] + (l + 1) * 96]
            S.op("dve", lambda e, l=l, bm=bm: e.tensor_tensor(
                out=g.modv[:, l, :, :], in0=g.ps[0][:, 0:192].rearrange("p (m w) -> p m w", w=2),
                in1=bm.unsqueeze(2).to_broadcast([128, 96, 2]), op=ALU.add), reads=[Bp, Bc], writes=[Bc])
            ng = g.vecs[:, VO["norm_g"] + l * 64: VO["norm_g"] + (l + 1) * 64].rearrange("p (j k) -> p j k", j=4)

            def gb(j):
                return ng[:, j, :].unsqueeze(2).to_broadcast([128, 16, 2])
            mv = g.modv[:, l, :, :]
            S.op("dve", lambda e, l=l, mv=mv, gb=gb: e.scalar_tensor_tensor(out=g.A1[:, l], in0=mv[:, 16:32, :], scalar=1.0, in1=gb(0), op0=ALU.add, op1=ALU.mult), reads=[Bc], writes=[Bc])
            S.op("dve", lambda e, l=l, mv=mv, gb=gb: e.tensor_tensor(out=g.G1[:, l], in0=mv[:, 32:48, :], in1=gb(1), op=ALU.mult), reads=[Bc], writes=[Bc])
            S.op("dve", lambda e, l=l, mv=mv, gb=gb: e.scalar_tensor_tensor(out=g.A2[:, l], in0=mv[:, 64:80, :], scalar=1.0, in1=gb(2), op0=ALU.add, op1=ALU.mult), reads=[Bc], writes=[Bc])
            S.op("dve", lambda e, l=l, mv=mv, gb=gb: e.tensor_tensor(out=g.G2[:, l], in0=mv[:, 80:96, :], in1=gb(3), op=ALU.mult), reads=[Bc], writes=[Bc])
            g.flush()


def tb(lst, lo, hi):
    return lst[lo // 256:(hi + 255) // 256]


def tok_tiles(nt):
    out = [(t0, nt, 0) for t0 in range(0, TL, nt)]
    out += [(TL + t0, min(nt, TC), 1) for t0 in range(0, TC, nt)]
    return out


def emit_norm_mod(g, xt, Bx, n, A, sh, sq, Bsq, tmp, Btmp, rstd, Brstd, hout, Bh, psb):
    S = g.S
    Bc = g.B_const
    Bp = g.B_ps[psb]
    ps = g.ps[psb]
    for kc in range(16):
        S.op("act", lambda e, kc=kc: e.activation(out=sq[:, kc, :n], in_=xt[:, kc, :n], func=AF.Square), reads=[Bx], writes=[Bsq])
    for kc in range(16):
        S.op("pe", lambda e, kc=kc: e.matmul(ps[:, :n], lhsT=g.ones_bf[:], rhs=sq[:, kc, :n], start=(kc == 0), stop=(kc == 15)), reads=[Bsq, Bc], writes=[Bp])
    S.op("act", lambda e: e.activation(out=rstd[:, :n], in_=ps[:, :n], func=AF.Sqrt, bias=g.epsb[:, 0:1], scale=1.0 / D), reads=[Bp, Bc], writes=[Brstd])
    S.op("dve", lambda e: e.reciprocal(out=rstd[:, :n], in_=rstd[:, :n]), reads=[Brstd], writes=[Brstd])
    for kc in range(16):
        S.op("dve", lambda e, kc=kc: e.scalar_tensor_tensor(out=tmp[:, :n], in0=xt[:, kc, :n], scalar=A[:, kc:kc + 1], in1=rstd[:, :n], op0=ALU.mult, op1=ALU.mult), reads=[Bx, Brstd, Bc], writes=[Btmp])
        S.op("act", lambda e, kc=kc: e.activation(out=hout[:, kc, :n], in_=tmp[:, :n], func=AF.Identity, bias=sh[:, kc:kc + 1], scale=1.0), reads=[Btmp, Bc], writes=[Bh])


def phase_norm1_l0(g):
    S, nc = g.S, g.nc
    NT = 256
    with contextlib.ExitStack() as st:
        xt = [st.enter_context(nc.sbuf_tensor(_u() + "n1x%d" % i, [128, 16, NT], F32)) for i in range(2)]
        ht = [st.enter_context(nc.sbuf_tensor(_u() + "n1h%d" % i, [128, 16, NT], BF16)) for i in range(2)]
        sq = st.enter_context(nc.sbuf_tensor(_u() + "n1sq", [128, 16, NT], BF16))
        tmp = st.enter_context(nc.sbuf_tensor(_u() + "n1tmp", [128, NT], F32))
        rstd = st.enter_context(nc.sbuf_tensor(_u() + "n1rstd", [128, NT], F32))
        Bx = [Buf(), Buf()]
        Bh = [Buf(), Buf()]
        Bsq, Btmp, Brstd = Buf(), Buf(), Buf()
        n1tiles = tok_tiles(NT)

        def issue_load(i):
            t0, n, which = n1tiles[i]
            j = i % 2
            src = g.res0[:, t0:t0 + n].rearrange("(kc p) t -> p kc t", p=128)
            S.dma("sp", lambda e, j=j, src=src, n=n: e.dma_start(out=xt[j][:, :, :n], in_=src), writes=[Bx[j]])
        issue_load(0)
        for i, (t0, n, which) in enumerate(n1tiles):
            j = i % 2
            if i + 1 < len(n1tiles):
                issue_load(i + 1)
            dst = g.res[:, t0:t0 + n].rearrange("(kc p) t -> p kc t", p=128)
            S.dma("sp", lambda e, j=j, dst=dst, n=n: e.dma_start(out=dst, in_=xt[j][:, :, :n]), reads=[Bx[j]], writes=tb(g.B_res_t, t0, t0 + n))
            A = g.A1[:, 0, :, which]
            sh = g.modv[:, 0, 0:16, which]
            emit_norm_mod(g, xt[j], Bx[j], n, A, sh, sq, Bsq, tmp, Btmp, rstd, Brstd, ht[j], Bh[j], 1 + j)
            dsth = g.hT[:, t0:t0 + n].rearrange("(kc p) t -> p kc t", p=128)
            S.dma("sp", lambda e, j=j, dsth=dsth, n=n: e.dma_start(out=dsth, in_=ht[j][:, :, :n]), reads=[Bh[j]], writes=tb(g.B_hT_t, t0, t0 + n))
        g.flush()


def hg_pos_view(buf3, i):
    return buf3[:, 4:68, 8 * i:8 * i + 8].rearrange("p c r -> p r c")


def phase_hg(g, l, hd):
    S, nc = g.S, g.nc
    Bc = g.B_const
    NT = 512
    tiles = tok_tiles(NT)
    lbf = [g.lbv[:, l, d * 8 + hd:d * 8 + hd + 1] for d in range(2)]
    omf = [g.omlb[:, l, d * 8 + hd:d * 8 + hd + 1] for d in range(2)]
    with contextlib.ExitStack() as st0:
        def sb0(name, shape, dt):
            return st0.enter_context(nc.sbuf_tensor(_u() + name, shape, dt))
        qs = sb0("hg_qs", [128, NCH, 64], BF16)
        sg = sb0("hg_sg", [128, NCH, 64], BF16)
        vT = sb0("hg_vT", [128, NCH, 64], BF16)
        kk = [sb0("hg_k%d" % d, [128, NCH, 64], BF16) for d in range(2)]
        qt = [sb0("hg_qt%d" % d, [128, NCH, 64], BF16) for d in range(2)]
        sm = sb0("hg_sm", [128, 2, 5, NCH], F32)
        B_qs, B_sg, B_vT = Buf(), Buf(), Buf()
        B_k = [Buf(), Buf()]
        B_qt = [Buf(), Buf()]
        B_sm = Buf()
        with contextlib.ExitStack() as st1:
            def sb1(name, shape, dt):
                return st1.enter_context(nc.sbuf_tensor(_u() + name, shape, dt))
            lf = [sb1("hg_lf%d" % d, [128, NCH, 65], F32) for d in range(2)]
            msk = sb1("hg_msk", [128, NCH, 65], BF16)
            B_lf = [Buf(), Buf()]
            B_msk = Buf()
            S.op("pool", lambda e: e.memset(msk[:], 1.0), writes=[B_msk])
            S.op("pool", lambda e: e.memset(msk[:, :, 0:1], 0.0), writes=[B_msk])
            for d in range(2):
                S.op("pool", lambda e, d=d: e.memset(lf[d][:, :, 0:1], 0.0), writes=[B_lf[d]])
            with contextlib.ExitStack() as st2:
                wt = st2.enter_context(nc.sbuf_tensor(_u() + "hg_w", [128, 16, 640], BF16))
                hts = [st2.enter_context(nc.sbuf_tensor(_u() + "hg_h%d" % i, [128, 16, NT], BF16)) for i in range(2)]
                sgt = st2.enter_context(nc.sbuf_tensor(_u() + "hg_sgt", [128, NT], F32))
                B_w, B_h, B_sgt = Buf(), [Buf(), Buf()], Buf()
                src = g.w_hg[l, hd].rearrange("(kc p) n -> p kc n", p=128)
                for q4 in range(4):
                    S.dma("pool", lambda e, q4=q4: e.dma_start(out=wt[:, q4 * 4:(q4 + 1) * 4, :], in_=src[:, q4 * 4:(q4 + 1) * 4, :]), writes=[B_w])
                for i, (t0, n, which) in enumerate(tiles):
                    j = i % 2
                    hsrc = g.hT[:, t0:t0 + n].rearrange("(kc p) t -> p kc t", p=128)
                    S.dma("sp", lambda e, j=j, hsrc=hsrc, n=n: e.dma_start(out=hts[j][:, :, :n], in_=hsrc), reads=tb(g.B_hT_t, t0, t0 + n), writes=[B_h[j]])

                    def dstv(buf, w0=0):
                        if which == 0:
                            return hg_pos_view(buf[:, :, w0:w0 + 64], i)
                        return buf[:, 0:4, w0:w0 + 64]

                    def srcv(ap):
                        if which == 0:
                            return ap.rearrange("p (r c) -> p r c", c=64)
                        return ap.rearrange("p (c r) -> p c r", r=64)
                    for blk in range(5):
                        pb = blk % 5
                        ps, Bp = g.ps[pb], g.B_ps[pb]
                        for kc in range(16):
                            S.op("pe", lambda e, ps=ps, kc=kc, blk=blk, j=j, n=n: e.matmul(
                                ps[:, :n], lhsT=wt[:, kc, blk * 128:(blk + 1) * 128], rhs=hts[j][:, kc, :n],
                                start=(kc == 0), stop=(kc == 15)), reads=[B_w, B_h[j]], writes=[Bp])
                        pv = srcv(ps[:, :n])
                        if blk == 0:
                            S.op("act", lambda e, pv=pv, o=dstv(qs): e.activation(out=o, in_=pv, func=AF.Silu), reads=[Bp], writes=[B_qs])
                        elif blk in (1, 2):
                            d = blk - 1
                            S.op("act", lambda e, ps=ps, n=n: e.activation(out=sgt[:, :n], in_=ps[:, :n], func=AF.Sigmoid), reads=[Bp], writes=[B_sgt])
                            S.op("dve", lambda e, d=d, sv=srcv(sgt[:, :n]), o=dstv(kk[d]): e.tensor_scalar(
                                out=o, in0=sv, scalar1=omf[d], scalar2=-1.0, op0=ALU.mult, op1=ALU.mult), reads=[B_sgt, Bc], writes=[B_k[d]])
                            S.op("dve", lambda e, d=d, o=dstv(kk[d]): e.tensor_scalar(
                                out=o, in0=o, scalar1=omf[d], scalar2=None, op0=ALU.add), reads=[B_k[d], Bc], writes=[B_k[d]])
                            S.op("act", lambda e, d=d, sv=srcv(sgt[:, :n]), o=dstv(lf[d], 1): e.activation(
                                out=o, in_=sv, func=AF.Ln, bias=lbf[d], scale=omf[d]), reads=[B_sgt, Bc], writes=[B_lf[d]])
                        elif blk == 3:
                            S.op("act", lambda e, pv=pv, o=dstv(vT): e.activation(out=o, in_=pv, func=AF.Copy), reads=[Bp], writes=[B_vT])
                        else:
                            S.op("act", lambda e, pv=pv, o=dstv(sg): e.activation(out=o, in_=pv, func=AF.Silu), reads=[Bp], writes=[B_sg])
                g.flush()
            with nc.sbuf_tensor(_u() + "hg_tmpE", [128, NCH, 64], F32) as tmpE:
                B_tE = Buf()
                for d in range(2):
                    lff = lf[d][:].rearrange("p c w -> p (c w)")
                    mf = msk[:].rearrange("p c w -> p (c w)")
                    S.op("dve", lambda e, lff=lff, mf=mf: e.tensor_tensor_scan(out=lff, data0=mf, data1=lff, initial=0.0, op0=ALU.mult, op1=ALU.add), reads=[B_lf[d], B_msk], writes=[B_lf[d]])
                    if d == 0:
                        Vw = lf[d][:, :, 1:65]
                        refcol = lf[d][:, :, 32]
                    else:
                        Vw = lf[d][:, :, 0:64]
                        refcol = lf[d][:, :, 32]
                    totcol = lf[d][:, :, 64]
                    sref, stot, sr, ss1, sa = [sm[:, d, x, :] for x in range(5)]
                    S.op("dve", lambda e, sref=sref, refcol=refcol: e.tensor_copy(out=sref, in_=refcol), reads=[B_lf[d]], writes=[B_sm])
                    S.op("dve", lambda e, stot=stot, totcol=totcol: e.tensor_copy(out=stot, in_=totcol), reads=[B_lf[d]], writes=[B_sm])
                    S.op("act", lambda e, sa=sa, stot=stot: e.activation(out=sa, in_=stot, func=AF.Exp), reads=[B_sm], writes=[B_sm])
                    e_ref, e_tr = (sr, ss1) if d == 0 else (ss1, sr)
                    S.op("act", lambda e, e_ref=e_ref, sref=sref: e.activation(out=e_ref, in_=sref, func=AF.Exp), reads=[B_sm], writes=[B_sm])
                    S.op("dve", lambda e, e_tr=e_tr, stot=stot, sref=sref: e.tensor_tensor(out=e_tr, in0=stot, in1=sref, op=ALU.subtract), reads=[B_sm], writes=[B_sm])
                    S.op("act", lambda e, e_tr=e_tr: e.activation(out=e_tr, in_=e_tr, func=AF.Exp), reads=[B_sm], writes=[B_sm])
                    S.op("dve", lambda e, Vw=Vw, sref=sref: e.tensor_tensor(out=Vw, in0=Vw, in1=sref.unsqueeze(2).to_broadcast([128, NCH, 64]), op=ALU.subtract), reads=[B_lf[d], B_sm], writes=[B_lf[d]])
                    sq_, sk_ = (1.0, -1.0) if d == 0 else (-1.0, 1.0)
                    S.op("act", lambda e, Vw=Vw, sq_=sq_: e.activation(out=tmpE[:], in_=Vw, func=AF.Exp, scale=sq_), reads=[B_lf[d]], writes=[B_tE])
                    S.op("dve", lambda e, d=d: e.tensor_tensor(out=qt[d][:], in0=qs[:], in1=tmpE[:], op=ALU.mult), reads=[B_qs, B_tE], writes=[B_qt[d]])
                    S.op("act", lambda e, Vw=Vw, sk_=sk_: e.activation(out=tmpE[:], in_=Vw, func=AF.Exp, scale=sk_), reads=[B_lf[d]], writes=[B_tE])
                    S.op("dve", lambda e, d=d: e.tensor_tensor(out=kk[d][:], in0=kk[d][:], in1=tmpE[:], op=ALU.mult), reads=[B_k[d], B_tE], writes=[B_k[d]])
                g.flush()
        with contextlib.ExitStack() as st1:
            def sb1(name, shape, dt):
                return st1.enter_context(nc.sbuf_tensor(_u() + name, shape, dt))
            v_tm = sb1("hg_vtm", [64, NCH, 128], BF16)
            k_tm = sb1("hg_ktm", [64, NCH, 128], BF16)
            o_sb = sb1("hg_o", [128, NCH, 64], F32)
            mk = [sb1("hg_mask%d" % d, [64, 8, 64], I32) for d in range(2)]
            scb = [sb1("hg_scb%d" % i, [64, 8, 64], BF16) for i in range(2)]
            Ur = sb1("hg_U", [128, 8, 128], F32)
            Sst = [sb1("hg_S%d" % i, [128, 128], F32) for i in range(2)]
            Spr = sb1("hg_Sp", [128, 4, 128], BF16)
            B_vtm, B_ktm, B_o = Buf(), Buf(), Buf()
            B_mk = Buf()
            B_scb = [Buf(), Buf()]
            B_U = [Buf() for _ in range(8)]
            B_S = [Buf(), Buf()]
            B_Sp = [Buf() for _ in range(4)]
            S.op("pool", lambda e: e.iota(mk[0][:], pattern=[[0, 8], [1, 64]], base=0, channel_multiplier=-1), writes=[B_mk])
            S.op("pool", lambda e: e.iota(mk[1][:], pattern=[[0, 8], [-1, 64]], base=0, channel_multiplier=1), writes=[B_mk])
            for d in range(2):
                S.op("dve", lambda e, d=d: e.tensor_single_scalar(out=mk[d][:], in_=mk[d][:], scalar=0, op=ALU.is_ge), reads=[B_mk], writes=[B_mk])
            for i in range(2):
                S.op("pool", lambda e, i=i: e.memset(scb[i][:], 0.0), writes=[B_scb[i]])

            def transpose_all(src, Bsrc, dst, Bdst):
                for c0 in range(0, NCH, 8):
                    ncg = min(8, NCH - c0)
                    for cc in range(ncg):
                        S.op("pe", lambda e, c0=c0, cc=cc: e.transpose(g.ps_bf[0:64, cc * 128:(cc + 1) * 128], src[:, c0 + cc, :], g.ident_bf[:]),
                             reads=[Bsrc, Bc], writes=[g.B_psbf])
                    S.op("act", lambda e, c0=c0, ncg=ncg: e.activation(out=dst[:, c0:c0 + ncg, :], in_=g.ps_bf[0:64, 0:ncg * 128].rearrange("p (c x) -> p c x", x=128), func=AF.Copy),
                         reads=[g.B_psbf], writes=[Bdst])
            transpose_all(vT, B_vT, v_tm, B_vtm)
            fwd_chain = list(range(NCH))
            bwd_chain = [3, 2, 1, 0] + list(range(67, 3, -1))
            for pas, d in enumerate((1, 0)):
                chain = bwd_chain if d == 1 else fwd_chain
                transpose_all(kk[d], B_k[d], k_tm, B_ktm)
                for i2 in range(2):
                    S.op("pool", lambda e, i2=i2: e.memset(scb[i2][:], 0.0), writes=[B_scb[i2]])
                sr, ss1, sa = sm[:, d, 2, :], sm[:, d, 3, :], sm[:, d, 4, :]
                groups = [chain[0:4]] + [chain[4 + 8 * i:12 + 8 * i] for i in range(8)]
                step = 0
                prev_sp = None
                for gi, grp in enumerate(groups):
                    cmin = min(grp)
                    ng = len(grp)
                    sj = gi % 2
                    for c in grp:
                        s = c - cmin
                        S.op("pe", lambda e, c=c, s=s, d=d: e.matmul(g.ps[5][0:64, s * 64:(s + 1) * 64], lhsT=kk[d][:, c, :], rhs=qt[d][:, c, :], start=True, stop=True),
                             reads=[B_k[d], B_qt[d]], writes=[g.B_ps[5]])
                    S.op("dve", lambda e, sj=sj, ng=ng, d=d: e.copy_predicated(out=scb[sj][:, 0:ng, :], mask=mk[d][:, 0:ng, :], data=g.ps[5][0:64, 0:ng * 64].rearrange("p (c t) -> p c t", t=64)),
                         reads=[g.B_ps[5], B_mk], writes=[B_scb[sj]])
                    for c in grp:
                        s = c - cmin
                        pb = 3 + (s // 4)
                        S.op("pe", lambda e, c=c, s=s, pb=pb: e.matmul(g.ps[pb][:, (s % 4) * 128:(s % 4 + 1) * 128], lhsT=k_tm[:, c, :], rhs=v_tm[:, c, :], start=True, stop=True),
                             reads=[B_ktm, B_vtm], writes=[g.B_ps[pb]])
                    for c in grp:
                        s = c - cmin
                        pb = 3 + (s // 4)
                        S.op("act", lambda e, c=c, s=s, pb=pb, ss1=ss1: e.activation(out=Ur[:, s, :], in_=g.ps[pb][:, (s % 4) * 128:(s % 4 + 1) * 128], func=AF.Identity, scale=ss1[:, c:c + 1]),
                             reads=[g.B_ps[pb], B_sm], writes=[B_U[s]])
                    for c in grp:
                        s = c - cmin
                        ov = g.ps[6][:, s * 64:(s + 1) * 64]
                        S.op("pe", lambda e, c=c, s=s, sj=sj, ov=ov, last=(prev_sp is None): e.matmul(ov, lhsT=v_tm[:, c, :], rhs=scb[sj][:, s, :], start=True, stop=last),
                             reads=[B_vtm, B_scb[sj]], writes=[g.B_ps[6]])
                        if prev_sp is not None:
                            S.op("pe", lambda e, c=c, ov=ov, psp=prev_sp, d=d: e.matmul(ov, lhsT=Spr[:, psp, :], rhs=qt[d][:, c, :], start=False, stop=True),
                                 reads=[B_Sp[prev_sp], B_qt[d]], writes=[g.B_ps[6]])
                        sn, so = step % 2, (step + 1) % 2
                        if step == 0:
                            S.op("dve", lambda e, s=s, sn=sn: e.tensor_copy(out=Sst[sn][:], in_=Ur[:, s, :]), reads=[B_U[s]], writes=[B_S[sn]])
                        else:
                            S.op("dve", lambda e, s=s, sn=sn, so=so, c=c, sa=sa: e.scalar_tensor_tensor(out=Sst[sn][:], in0=Sst[so][:], scalar=sa[:, c:c + 1], in1=Ur[:, s, :], op0=ALU.mult, op1=ALU.add),
                                 reads=[B_S[so], B_U[s], B_sm], writes=[B_S[sn]])
                        if step + 1 < NCH:
                            cn = chain[step + 1]
                            spi = step % 4
                            S.op("act", lambda e, sn=sn, spi=spi, cn=cn, sr=sr: e.activation(out=Spr[:, spi, :], in_=Sst[sn][:], func=AF.Identity, scale=sr[:, cn:cn + 1]),
                                 reads=[B_S[sn], B_sm], writes=[B_Sp[spi]])
                            prev_sp = spi
                        step += 1
                    osrc = g.ps[6][:, 0:ng * 64].rearrange("p (c t) -> p c t", t=64)
                    odst = o_sb[:, cmin:cmin + ng, :]
                    if pas == 0:
                        S.op("act", lambda e, osrc=osrc, odst=odst: e.activation(out=odst, in_=osrc, func=AF.Copy), reads=[g.B_ps[6]], writes=[B_o])
                    else:
                        S.op("dve", lambda e, osrc=osrc, odst=odst: e.tensor_tensor(out=odst, in0=osrc, in1=odst, op=ALU.add), reads=[g.B_ps[6], B_o], writes=[B_o])
            with contextlib.ExitStack() as st2:
                osq = st2.enter_context(nc.sbuf_tensor(_u() + "hg_osq", [128, 512], BF16))
                rs = st2.enter_context(nc.sbuf_tensor(_u() + "hg_rs", [128, 512], F32))
                tmp = st2.enter_context(nc.sbuf_tensor(_u() + "hg_tmp", [128, NCH, 64], F32))
                outr = st2.enter_context(nc.sbuf_tensor(_u() + "hg_outr", [128, T], BF16))
                B_osq, B_rs, B_tmp, B_outr = Buf(), Buf(), Buf(), Buf()
                of = o_sb[:].rearrange("p c t -> p (c t)")
                tf = tmp[:].rearrange("p c t -> p (c t)")
                hgg = g.vecs[:, VO["hgg"] + l * 8 + hd: VO["hgg"] + l * 8 + hd + 1]
                for t0 in range(0, T, 512):
                    n = min(512, T - t0)
                    S.op("act", lambda e, t0=t0, n=n: e.activation(out=osq[:, :n], in_=of[:, t0:t0 + n], func=AF.Square), reads=[B_o], writes=[B_osq])
                    S.op("pe", lambda e, n=n: e.matmul(g.ps[0][:, :n], lhsT=g.ones_bf[:], rhs=osq[:, :n], start=True, stop=True), reads=[B_osq, Bc], writes=[g.B_ps[0]])
                    S.op("act", lambda e, n=n: e.activation(out=rs[:, :n], in_=g.ps[0][:, :n], func=AF.Sqrt, bias=g.epsb[:, 0:1], scale=1.0 / 128), reads=[g.B_ps[0], Bc], writes=[B_rs])
                    S.op("dve", lambda e, n=n: e.reciprocal(out=rs[:, :n], in_=rs[:, :n]), reads=[B_rs], writes=[B_rs])
                    S.op("dve", lambda e, t0=t0, n=n: e.scalar_tensor_tensor(out=tf[:, t0:t0 + n], in0=of[:, t0:t0 + n], scalar=hgg, in1=rs[:, :n], op0=ALU.mult, op1=ALU.mult), reads=[B_o, B_rs, Bc], writes=[B_tmp])
                S.op("dve", lambda e: e.tensor_tensor(out=outr[:, TL:T].rearrange("p (c t) -> p c t", t=64), in0=tmp[:, 0:4, :], in1=sg[:, 0:4, :], op=ALU.mult), reads=[B_tmp, B_sg], writes=[B_outr])
                S.op("dve", lambda e: e.tensor_tensor(out=outr[:, 0:TL].rearrange("p (r c) -> p c r", c=64), in0=tmp[:, 4:68, :], in1=sg[:, 4:68, :], op=ALU.mult), reads=[B_tmp, B_sg], writes=[B_outr])
                S.dma("sp", lambda e: e.dma_start(out=g.mT[hd * 128:(hd + 1) * 128, :], in_=outr[:]), reads=[B_outr], writes=[g.B_mT])
                if "o" in g.taps and l == 0:
                    S.dma("sp", lambda e: e.dma_start(out=g.taps["o"][hd], in_=o_sb[:].rearrange("p c t -> p (c t)")), reads=[B_o], writes=[g.B_tap])
                g.flush()


def phase_rg(g, l):
    S, nc = g.S, g.nc
    Bc = g.B_const
    NT = 512
    tiles = tok_tiles(NT)
    with contextlib.ExitStack() as st0:
        def sb0(name, shape, dt):
            return st0.enter_context(nc.sbuf_tensor(_u() + name, shape, dt))
        gw = sb0("rg_gw", [128, 2, 2, 8, 128], BF16)
        B_gw = Buf()
        gwf = gw[:].rearrange("p a b n d -> p (a b n d)")
        for q4 in range(4):
            S.dma("pool", lambda e, q4=q4: e.dma_start(out=gwf[:, q4 * 1024:(q4 + 1) * 1024], in_=g.rgw[l, :, q4 * 1024:(q4 + 1) * 1024]), writes=[B_gw])
        wts = [sb0("rg_w%d" % i, [128, 16, 256], BF16) for i in range(2)]
        hts = [sb0("rg_h%d" % i, [128, 16, NT], BF16) for i in range(2)]
        rxl = sb0("rg_rxl", [128, TL + 3], F32)
        rxc = sb0("rg_rxc", [128, TC + 3], F32)
        gg = sb0("rg_gg", [128, T], BF16)
        xc = sb0("rg_xc", [128, T], F32)
        xcb = sb0("rg_xcb", [128, T], BF16)
        av = sb0("rg_a", [128, T], F32)
        bt = sb0("rg_bt", [128, T], F32)
        hs = [sb0("rg_hs%d" % d, [128, T], F32) for d in range(2)]
        t1 = sb0("rg_t1", [128, NT], F32)
        t2 = sb0("rg_t2", [128, NT], F32)
        t3 = sb0("rg_t3", [128, NT], F32)
        outr = sb0("rg_outr", [128, T], BF16)
        B_w, B_h = [Buf(), Buf()], [Buf(), Buf()]
        B_rx, B_gg, B_xc, B_xcb, B_a, B_bt = Buf(), Buf(), Buf(), Buf(), Buf(), Buf()
        B_hs = [Buf(), Buf()]
        B_t1, B_t2, B_t3, B_outr = Buf(), Buf(), Buf(), Buf()
        for n_ in range(8):
            wj = n_ % 2
            src = g.w_rg[l, n_].rearrange("(kc p) n -> p kc n", p=128)
            for q2 in range(2):
                S.dma("pool", lambda e, wj=wj, src=src, q2=q2: e.dma_start(out=wts[wj][:, q2 * 8:(q2 + 1) * 8, :], in_=src[:, q2 * 8:(q2 + 1) * 8, :]), writes=[B_w[wj]])
            S.op("pool", lambda e: e.memset(rxl[:, 0:2], 0.0), writes=[B_rx])
            S.op("pool", lambda e: e.memset(rxl[:, TL + 2:TL + 3], 0.0), writes=[B_rx])
            S.op("pool", lambda e: e.memset(rxc[:, 0:2], 0.0), writes=[B_rx])
            S.op("pool", lambda e: e.memset(rxc[:, TC + 2:TC + 3], 0.0), writes=[B_rx])
            for i, (t0, n, which) in enumerate(tiles):
                j = i % 2
                hsrc = g.hT[:, t0:t0 + n].rearrange("(kc p) t -> p kc t", p=128)
                S.dma("sp", lambda e, j=j, hsrc=hsrc, n=n: e.dma_start(out=hts[j][:, :, :n], in_=hsrc), reads=tb(g.B_hT_t, t0, t0 + n), writes=[B_h[j]])
                for blk in range(2):
                    ps, Bp = g.ps[blk], g.B_ps[blk]
                    for kc in range(16):
                        S.op("pe", lambda e, ps=ps, kc=kc, blk=blk, j=j, n=n, wj=wj: e.matmul(
                            ps[:, :n], lhsT=wts[wj][:, kc, blk * 128:(blk + 1) * 128], rhs=hts[j][:, kc, :n],
                            start=(kc == 0), stop=(kc == 15)), reads=[B_w[wj], B_h[j]], writes=[Bp])
                    if blk == 0:
                        dst = rxl[:, 2 + t0:2 + t0 + n] if which == 0 else rxc[:, 2 + t0 - TL:2 + t0 - TL + n]
                        S.op("act", lambda e, ps=ps, n=n, dst=dst: e.activation(out=dst, in_=ps[:, :n], func=AF.Copy), reads=[Bp], writes=[B_rx])
                    else:
                        S.op("act", lambda e, ps=ps, n=n, t0=t0: e.activation(out=gg[:, t0:t0 + n], in_=ps[:, :n], func=AF.Gelu), reads=[Bp], writes=[B_gg])
            cw = [g.vecs[:, VO["rcw"] + l * 32 + k * 8 + n_: VO["rcw"] + l * 32 + k * 8 + n_ + 1] for k in range(4)]
            cb = g.vecs[:, VO["rcb"] + l * 8 + n_: VO["rcb"] + l * 8 + n_ + 1]
            for (rx, o0, nn) in ((rxl, 0, TL), (rxc, TL, TC)):
                S.op("dve", lambda e, rx=rx, o0=o0, nn=nn, c0_=cw[0], cb=cb: e.tensor_scalar(out=xc[:, o0:o0 + nn], in0=rx[:, 0:nn], scalar1=c0_, scalar2=cb, op0=ALU.mult, op1=ALU.add), reads=[B_rx, Bc], writes=[B_xc])
                for k in range(1, 4):
                    S.op("dve", lambda e, rx=rx, o0=o0, nn=nn, k=k, ck=cw[k]: e.scalar_tensor_tensor(out=xc[:, o0:o0 + nn], in0=rx[:, k:k + nn], scalar=ck, in1=xc[:, o0:o0 + nn], op0=ALU.mult, op1=ALU.add), reads=[B_rx, B_xc, Bc], writes=[B_xc])
            S.op("act", lambda e: e.activation(out=xcb[:], in_=xc[:], func=AF.Copy), reads=[B_xc], writes=[B_xcb])
            for d in range(2):
                ba = g.vecs[:, VO["rba"] + l * 16 + d * 8 + n_: VO["rba"] + l * 16 + d * 8 + n_ + 1]
                bx = g.vecs[:, VO["rbx"] + l * 16 + d * 8 + n_: VO["rbx"] + l * 16 + d * 8 + n_ + 1]
                nsp = g.nsp8[:, l, d * 8 + n_: d * 8 + n_ + 1]
                for (t0, n, which) in tiles:
                    S.op("pe", lambda e, d=d, t0=t0, n=n, n_=n_: e.matmul(g.ps[2][:, :n], lhsT=gw[:, d, 0, n_, :], rhs=xcb[:, t0:t0 + n], start=True, stop=True), reads=[B_gw, B_xcb], writes=[g.B_ps[2]])
                    S.op("pe", lambda e, d=d, t0=t0, n=n, n_=n_: e.matmul(g.ps[3][:, :n], lhsT=gw[:, d, 1, n_, :], rhs=xcb[:, t0:t0 + n], start=True, stop=True), reads=[B_gw, B_xcb], writes=[g.B_ps[3]])
                    S.op("act", lambda e, n=n, ba=ba: e.activation(out=t1[:, :n], in_=g.ps[2][:, :n], func=AF.Sigmoid, bias=ba, scale=1.0), reads=[g.B_ps[2], Bc], writes=[B_t1])
                    S.op("act", lambda e, n=n, t0=t0, nsp=nsp: e.activation(out=av[:, t0:t0 + n], in_=t1[:, :n], func=AF.Exp, scale=nsp), reads=[B_t1, Bc], writes=[B_a])
                    S.op("act", lambda e, n=n, t0=t0: e.activation(out=t2[:, :n], in_=av[:, t0:t0 + n], func=AF.Square), reads=[B_a], writes=[B_t2])
                    S.op("act", lambda e, n=n: e.activation(out=t2[:, :n], in_=t2[:, :n], func=AF.Sqrt, bias=1.0, scale=-1.0), reads=[B_t2], writes=[B_t2])
                    S.op("act", lambda e, n=n, bx=bx: e.activation(out=t3[:, :n], in_=g.ps[3][:, :n], func=AF.Sigmoid, bias=bx, scale=1.0), reads=[g.B_ps[3], Bc], writes=[B_t3])
                    S.op("dve", lambda e, n=n: e.tensor_tensor(out=t2[:, :n], in0=t2[:, :n], in1=t3[:, :n], op=ALU.mult), reads=[B_t2, B_t3], writes=[B_t2])
                    S.op("dve", lambda e, n=n, t0=t0: e.tensor_tensor(out=bt[:, t0:t0 + n], in0=t2[:, :n], in1=xc[:, t0:t0 + n], op=ALU.mult), reads=[B_t2, B_xc], writes=[B_bt])
                if d == 0:
                    S.op("dve", lambda e: e.tensor_tensor_scan(out=hs[0][:, TL:T], data0=av[:, TL:T], data1=bt[:, TL:T], initial=0.0, op0=ALU.mult, op1=ALU.add), reads=[B_a, B_bt], writes=[B_hs[0]])
                    S.op("dve", lambda e: e.tensor_tensor_scan(out=hs[0][:, 0:TL], data0=av[:, 0:TL], data1=bt[:, 0:TL], initial=hs[0][:, T - 1:T], op0=ALU.mult, op1=ALU.add), reads=[B_a, B_bt, B_hs[0]], writes=[B_hs[0]], force_same=True)
                else:
                    S.op("dve", lambda e: e.tensor_tensor_scan(out=hs[1][:, TL:T][:, ::-1], data0=av[:, TL:T][:, ::-1], data1=bt[:, TL:T][:, ::-1], initial=0.0, op0=ALU.mult, op1=ALU.add), reads=[B_a, B_bt], writes=[B_hs[1]])
                    S.op("dve", lambda e: e.tensor_tensor_scan(out=hs[1][:, 0:TL][:, ::-1], data0=av[:, 0:TL][:, ::-1], data1=bt[:, 0:TL][:, ::-1], initial=hs[1][:, TL:TL + 1], op0=ALU.mult, op1=ALU.add), reads=[B_a, B_bt, B_hs[1]], writes=[B_hs[1]], force_same=True)
            S.op("dve", lambda e: e.tensor_tensor(out=hs[0][:], in0=hs[0][:], in1=hs[1][:], op=ALU.add), reads=[B_hs[0], B_hs[1]], writes=[B_hs[0]])
            S.op("dve", lambda e: e.tensor_tensor(out=outr[:], in0=hs[0][:], in1=gg[:], op=ALU.mult), reads=[B_hs[0], B_gg], writes=[B_outr])
            S.dma("sp", lambda e, n_=n_: e.dma_start(out=g.mT[1024 + n_ * 128:1024 + (n_ + 1) * 128, :], in_=outr[:]), reads=[B_outr], writes=[g.B_mT])
        g.flush()


def phase_a(g, l, last):
    S, nc = g.S, g.nc
    Bc = g.B_const
    NT = 256
    with contextlib.ExitStack() as st:
        def sb(name, shape, dt):
            return st.enter_context(nc.sbuf_tensor(_u() + name, shape, dt))
        wo = sb("a_wo", [128, 16, D], BF16)
        mts = [sb("a_m%d" % i, [128, 16, NT], BF16) for i in range(2)]
        xts = [sb("a_x%d" % i, [128, 16, NT], F32) for i in range(2)]
        hto = [sb("a_h%d" % i, [128, 16, NT], BF16) for i in range(2)]
        mix = sb("a_mix", [128, 16, NT], F32)
        sq = sb("a_sq", [128, 16, NT], BF16)
        tmp = sb("a_tmp", [128, NT], F32)
        rstd = sb("a_rstd", [128, NT], F32)
        B_wo, B_m, B_x, B_h = Buf(), [Buf(), Buf()], [Buf(), Buf()], [Buf(), Buf()]
        B_mix, B_sq, B_tmp, B_rstd = Buf(), Buf(), Buf(), Buf()
        src = g.w_out[l].rearrange("(kc p) n -> p kc n", p=128)
        for kc4 in range(0, 16, 4):
            for hh in range(4):
                S.dma("pool", lambda e, kc4=kc4, hh=hh: e.dma_start(out=wo[:, kc4:kc4 + 4, hh * 512:(hh + 1) * 512], in_=src[:, kc4:kc4 + 4, hh * 512:(hh + 1) * 512]), writes=[B_wo])
        for jf in range(44):
            csrc = g.w_up[l, jf].rearrange("(kc p) n -> p kc n", p=128)
            cdst = g.wub[jf].rearrange("p (kc n) -> p kc n", n=256)
            for q2 in range(2):
                S.dma("pool", lambda e, csrc=csrc, cdst=cdst, q2=q2: e.dma_start(out=cdst[:, q2 * 8:(q2 + 1) * 8, :], in_=csrc[:, q2 * 8:(q2 + 1) * 8, :]), writes=[g.B_wub[jf]])
        for ob in range(16):
            csrc = g.w_dn[l, ob].rearrange("(fc p) n -> p fc n", p=128)
            cdst = g.wdb[ob].rearrange("p (fc n) -> p fc n", n=128)
            for q2 in range(4):
                S.dma("pool", lambda e, csrc=csrc, cdst=cdst, q2=q2: e.dma_start(out=cdst[:, q2 * 11:(q2 + 1) * 11, :], in_=csrc[:, q2 * 11:(q2 + 1) * 11, :]), writes=[g.B_wdb[ob]])
        tiles = tok_tiles(NT)
        if last:
            tiles = [t for t in tiles if t[2] == 0]
        import os
        AB = int(os.environ.get("MK_AB", "99"))
        if AB < 99:
            tiles = tiles[:1]
        def issue_load(i):
            t0, n, which = tiles[i]
            j = i % 2
            msrc = g.mT[:, t0:t0 + n].rearrange("(kc p) t -> p kc t", p=128)
            xsrc = g.res[:, t0:t0 + n].rearrange("(kc p) t -> p kc t", p=128)
            S.dma("sp", lambda e, j=j, msrc=msrc: e.dma_start(out=mts[j][:], in_=msrc), reads=[g.B_mT], writes=[B_m[j]])
            S.dma("sp", lambda e, j=j, xsrc=xsrc: e.dma_start(out=xts[j][:], in_=xsrc), reads=tb(g.B_res_t, t0, t0 + n), writes=[B_x[j]])
        issue_load(0)
        for i, (t0, n, which) in enumerate(tiles):
            j = i % 2
            if i + 1 < len(tiles):
                issue_load(i + 1)
            if AB < 1:
                continue
            for ob in range(16):
                pb = ob % 4
                ps, Bp = g.ps[pb], g.B_ps[pb]
                for kc in range(16):
                    S.op("pe", lambda e, ps=ps, kc=kc, ob=ob, j=j: e.matmul(ps[:, :NT], lhsT=wo[:, kc, ob * 128:(ob + 1) * 128], rhs=mts[j][:, kc, :],
                                                                             start=(kc == 0), stop=(kc == 15)), reads=[B_wo, B_m[j]], writes=[Bp])
                S.op("dve", lambda e, ps=ps, ob=ob: e.tensor_copy(out=mix[:, ob, :], in_=ps[:, :NT]), reads=[Bp], writes=[B_mix])
                S.op("act", lambda e, ob=ob: e.activation(out=sq[:, ob, :], in_=mix[:, ob, :], func=AF.Square), reads=[B_mix], writes=[B_sq])
            if AB < 2:
                continue
            ps, Bp = g.ps[4], g.B_ps[4]
            for kc in range(16):
                S.op("pe", lambda e, kc=kc, ps=ps: e.matmul(ps[:, :NT], lhsT=g.ones_bf[:], rhs=sq[:, kc, :], start=(kc == 0), stop=(kc == 15)), reads=[B_sq, Bc], writes=[Bp])
            S.op("act", lambda e, ps=ps: e.activation(out=rstd[:], in_=ps[:, :NT], func=AF.Sqrt, bias=g.epsb[:, 0:1], scale=1.0 / D), reads=[Bp, Bc], writes=[B_rstd])
            S.op("dve", lambda e: e.reciprocal(out=rstd[:], in_=rstd[:]), reads=[B_rstd], writes=[B_rstd])
            if AB < 3:
                continue
            G = g.G1[:, l, :, which]
            for kc in range(16):
                S.op("dve", lambda e, kc=kc, gk=G[:, kc:kc + 1]: e.scalar_tensor_tensor(out=tmp[:], in0=mix[:, kc, :], scalar=gk, in1=rstd[:], op0=ALU.mult, op1=ALU.mult), reads=[B_mix, B_rstd, Bc], writes=[B_tmp])
                S.op("dve", lambda e, kc=kc, j=j: e.tensor_tensor(out=xts[j][:, kc, :], in0=xts[j][:, kc, :], in1=tmp[:], op=ALU.add), reads=[B_tmp, B_x[j]], writes=[B_x[j]])
            if AB < 4:
                continue
            xdst = g.res[:, t0:t0 + n].rearrange("(kc p) t -> p kc t", p=128)
            S.dma("sp", lambda e, j=j, xdst=xdst: e.dma_start(out=xdst, in_=xts[j][:]), reads=[B_x[j]], writes=tb(g.B_res_t, t0, t0 + n))
            if AB < 5:
                continue
            emit_norm_mod(g, xts[j], B_x[j], NT, g.A2[:, l, :, which], g.modv[:, l, 48:64, which], sq, B_sq, tmp, B_tmp, rstd, B_rstd, hto[j], B_h[j], 5)
            hdst = g.h2T[:, t0:t0 + n].rearrange("(kc p) t -> p kc t", p=128)
            S.dma("sp", lambda e, j=j, hdst=hdst: e.dma_start(out=hdst, in_=hto[j][:]), reads=[B_h[j]], writes=tb(g.B_h2T_t, t0, t0 + n))
            if "xa" in g.taps and l == 0:
                S.dma("sp", lambda e, j=j, t0=t0, n=n: e.dma_start(out=g.taps["xa"][:, t0:t0 + n].rearrange("(kc p) t -> p kc t", p=128), in_=xts[j][:]), reads=[B_x[j]], writes=[g.B_tap])
        g.flush()


def phase_b(g, l, last, final):
    S, nc = g.S, g.nc
    Bc = g.B_const
    TS = 512
    NS = 256
    with contextlib.ExitStack() as st:
        def sb(name, shape, dt):
            return st.enter_context(nc.sbuf_tensor(_u() + name, shape, dt))
        h2 = sb("b_h2", [128, 16, TS + 2], BF16)
        ge = sb("b_ge", [128, 44, TS], BF16)
        wup = [sb("b_wu%d" % i, [128, 16, 256], BF16) for i in range(2)]
        wdn = [sb("b_wd%d" % i, [128, 44, 128], BF16) for i in range(2)]
        fl = sb("b_fl", [128, 16, TS], F32)
        sq = sb("b_sq", [128, 16, NS], BF16)
        xt = sb("b_x", [128, 16, NS], F32)
        hn = sb("b_hn", [128, 16, NS], BF16)
        ca2 = [sb("b_ca%d" % i, [128, NS], F32) for i in range(2)]
        cv2 = [sb("b_cv%d" % i, [128, NS], F32) for i in range(2)]
        B_ca2 = [Buf(), Buf()]
        B_cv2 = [Buf(), Buf()]
        cvi = [0]
        tmp = sb("b_tmp", [128, NS], F32)
        rstd = sb("b_rstd", [128, NS], F32)
        B_h2, B_ge, B_wu, B_wd = Buf(), Buf(), [Buf(), Buf()], [Buf(), Buf()]
        B_fl, B_sq, B_x, B_hn, B_tmp, B_rstd = Buf(), Buf(), Buf(), Buf(), Buf(), Buf()
        stiles = [(t0, TS, 0) for t0 in range(0, TL, TS)]
        if not last:
            stiles.append((TL, TC, 1))
        wi = 0
        di = 0
        import os
        if os.environ.get("MK_NST"):
            stiles = stiles[:int(os.environ["MK_NST"])]
        def issue_h2(idx):
            t0, n, which = stiles[idx]
            seq0 = 0 if which == 0 else TL
            seq1 = TL if which == 0 else T
            lo = max(t0 - 1, seq0)
            hi = min(t0 + n + 1, seq1)
            if lo == t0:
                S.op("dve", lambda e: e.memset(h2[:, :, 0:1], 0.0), writes=[B_h2])
            if hi == t0 + n:
                S.op("dve", lambda e, n=n: e.memset(h2[:, :, n + 1:n + 2], 0.0), writes=[B_h2])
            hsrc = g.h2T[:, lo:hi].rearrange("(kc p) t -> p kc t", p=128)
            S.dma("sp", lambda e, hsrc=hsrc, lo=lo, hi=hi, t0=t0: e.dma_start(out=h2[:, :, lo - (t0 - 1):hi - (t0 - 1)], in_=hsrc), reads=tb(g.B_h2T_t, lo, hi), writes=[B_h2])
        issue_h2(0)
        for sidx, (t0, n, which) in enumerate(stiles):
            nsub = n // NS
            import os
            BB = int(os.environ.get("MK_BB", "99"))
            if BB < 1:
                g.flush()
                continue
            for jf in range(44):
                wj = wi % 2
                wi += 1
                S.dma("sp", lambda e, wj=wj, jf=jf: e.dma_start(out=wup[wj][:].rearrange("p kc n -> p (kc n)"), in_=g.wub[jf]), reads=[g.B_wub[jf]], writes=[B_wu[wj]])
                fw = [[g.vecs[:, VO["fcw"] + l * 264 + k * 88 + half * 44 + jf: VO["fcw"] + l * 264 + k * 88 + half * 44 + jf + 1] for k in range(3)] for half in range(2)]
                fb = [g.vecs[:, VO["fcb"] + l * 88 + half * 44 + jf: VO["fcb"] + l * 88 + half * 44 + jf + 1] for half in range(2)]
                for s in range(nsub):
                    c0 = s * NS
                    ci = cvi[0] % 2
                    cvi[0] += 1
                    ca, cv, B_ca, B_cv = ca2[ci], cv2[ci], B_ca2[ci], B_cv2[ci]
                    for half in range(2):
                        pb = (s * 2 + half) % 4
                        ps, Bp = g.ps[pb], g.B_ps[pb]
                        HH = int(os.environ.get("MK_HH", "2"))
                        for kc in range(16):
                            S.op("pe", lambda e, ps=ps, kc=kc, wj=wj, half=half, c0=c0: e.matmul(
                                ps[:, :NS + HH], lhsT=wup[wj][:, kc, half * 128:(half + 1) * 128], rhs=h2[:, kc, c0:c0 + NS + HH],
                                start=(kc == 0), stop=(kc == 15)), reads=[B_wu[wj], B_h2], writes=[Bp])
                        dst, Bd = (ca, B_ca) if half == 0 else (cv, B_cv)
                        if os.environ.get("MK_EE", "") == "noact":
                            continue
                        S.op("act", lambda e, ps=ps, dst=dst, half=half, fw=fw, fb=fb: e.activation(out=dst[:], in_=ps[:, HH // 2:NS + HH // 2], func=AF.Identity, bias=fb[half], scale=fw[half][1]), reads=[Bp, Bc], writes=[Bd])
                        EE = os.environ.get("MK_EE", "")
                        if EE == "noact2":
                            continue
                        S.op("dve", lambda e, ps=ps, dst=dst, half=half, fw=fw: e.scalar_tensor_tensor(out=dst[:], in0=ps[:, 0:NS], scalar=fw[half][0], in1=dst[:], op0=ALU.mult, op1=ALU.add), reads=[Bp, Bd, Bc], writes=[Bd])
                        if EE == "one":
                            continue
                        S.op("dve", lambda e, ps=ps, dst=dst, half=half, fw=fw: e.scalar_tensor_tensor(out=dst[:], in0=ps[:, 2:NS + 2], scalar=fw[half][2], in1=dst[:], op0=ALU.mult, op1=ALU.add), reads=[Bp, Bd, Bc], writes=[Bd])
                    if BB < 2:
                        continue
                    S.op("act", lambda e, ca=ca: e.activation(out=ca[:], in_=ca[:], func=AF.Gelu), reads=[B_ca], writes=[B_ca])
                    S.op("dve", lambda e, jf=jf, c0=c0, ca=ca, cv=cv: e.tensor_tensor(out=ge[:, jf, c0:c0 + NS], in0=ca[:], in1=cv[:], op=ALU.mult), reads=[B_ca, B_cv], writes=[B_ge])
            if BB < 3:
                g.flush()
                continue
            for ob in range(16):
                dj = di % 2
                di += 1
                S.dma("sp", lambda e, dj=dj, ob=ob: e.dma_start(out=wdn[dj][:].rearrange("p fc n -> p (fc n)"), in_=g.wdb[ob]), reads=[g.B_wdb[ob]], writes=[B_wd[dj]])
                for s in range(nsub):
                    c0 = s * NS
                    pb = 4 + (s % 2)
                    ps, Bp = g.ps[pb], g.B_ps[pb]
                    for fc in range(44):
                        S.op("pe", lambda e, ps=ps, fc=fc, dj=dj, c0=c0: e.matmul(ps[:, :NS], lhsT=wdn[dj][:, fc, :], rhs=ge[:, fc, c0:c0 + NS], start=(fc == 0), stop=(fc == 43)),
                             reads=[B_wd[dj], B_ge], writes=[Bp])
                    S.op("act", lambda e, ps=ps, ob=ob, c0=c0: e.activation(out=fl[:, ob, c0:c0 + NS], in_=ps[:, :NS], func=AF.Copy), reads=[Bp], writes=[B_fl])
            if sidx + 1 < len(stiles):
                issue_h2(sidx + 1)
            S.same = True
            for s in range(nsub):
                c0 = s * NS
                tt = t0 + c0
                xsrc = g.res[:, tt:tt + NS].rearrange("(kc p) t -> p kc t", p=128)
                S.dma("sp", lambda e, xsrc=xsrc: e.dma_start(out=xt[:], in_=xsrc), reads=tb(g.B_res_t, tt, tt + NS), writes=[B_x])
                for kc in range(16):
                    S.op("act", lambda e, kc=kc, c0=c0: e.activation(out=sq[:, kc, :], in_=fl[:, kc, c0:c0 + NS], func=AF.Square), reads=[B_fl], writes=[B_sq])
                ps, Bp = g.ps[6], g.B_ps[6]
                for kc in range(16):
                    S.op("pe", lambda e, kc=kc, ps=ps: e.matmul(ps[:, :NS], lhsT=g.ones_bf[:], rhs=sq[:, kc, :], start=(kc == 0), stop=(kc == 15)), reads=[B_sq, Bc], writes=[Bp])
                S.op("act", lambda e, ps=ps: e.activation(out=rstd[:], in_=ps[:, :NS], func=AF.Sqrt, bias=g.epsb[:, 0:1], scale=1.0 / D), reads=[Bp, Bc], writes=[B_rstd])
                S.op("dve", lambda e: e.reciprocal(out=rstd[:], in_=rstd[:]), reads=[B_rstd], writes=[B_rstd])
                G = g.G2[:, l, :, which]
                for kc in range(16):
                    S.op("dve", lambda e, kc=kc, c0=c0, gk=G[:, kc:kc + 1]: e.scalar_tensor_tensor(out=tmp[:], in0=fl[:, kc, c0:c0 + NS], scalar=gk, in1=rstd[:], op0=ALU.mult, op1=ALU.mult), reads=[B_fl, B_rstd, Bc], writes=[B_tmp])
                    S.op("dve", lambda e, kc=kc: e.tensor_tensor(out=xt[:, kc, :], in0=xt[:, kc, :], in1=tmp[:], op=ALU.add), reads=[B_tmp, B_x], writes=[B_x])
                if final:
                    if which == 0:
                        ydst = g.y[:, tt:tt + NS].rearrange("(kc p) t -> p kc t", p=128)
                        S.dma("sp", lambda e, ydst=ydst: e.dma_start(out=ydst, in_=xt[:]), reads=[B_x], writes=[g.B_y])
                else:
                    xdst = g.res[:, tt:tt + NS].rearrange("(kc p) t -> p kc t", p=128)
                    S.dma("sp", lambda e, xdst=xdst: e.dma_start(out=xdst, in_=xt[:]), reads=[B_x], writes=tb(g.B_res_t, tt, tt + NS))
                    emit_norm_mod(g, xt, B_x, NS, g.A1[:, l + 1, :, which], g.modv[:, l + 1, 0:16, which], sq, B_sq, tmp, B_tmp, rstd, B_rstd, hn, B_hn, 6)
                    hdst = g.hT[:, tt:tt + NS].rearrange("(kc p) t -> p kc t", p=128)
                    S.dma("sp", lambda e, hdst=hdst: e.dma_start(out=hdst, in_=hn[:]), reads=[B_hn], writes=tb(g.B_hT_t, tt, tt + NS))
            S.same = False
        g.flush()


def fm(a, inner):
    a = np.asarray(a, dtype=np.float32)
    lead = a.shape[:-1]
    x = a.shape[-1] // 128
    a = a.reshape(lead + (x, 128))
    a = np.moveaxis(a, -1, 0)
    return np.ascontiguousarray(a).reshape(128, -1)


def prep_inputs(inp):
    f32 = np.float32
    vec = np.zeros((128, NV), f32)

    def put(name, arr):
        vec[:, VO[name]:VO[name] + arr.shape[1]] = arr
    put("b_mod", fm(inp["b_mod"], 96))
    put("norm_g", fm(inp["norm_g"], 16))
    put("lb", fm(inp["hg_lower_bounds"], 8))
    put("hgg", fm(inp["hg_norm_g"], 8))
    put("rcw", fm(inp["rg_conv_w"], 8))
    put("rcb", fm(inp["rg_conv_b"], 8))
    put("rba", fm(inp["rg_ba"], 8))
    put("rbx", fm(inp["rg_bx"], 8))
    put("rlam", fm(inp["rg_lambda"], 8))
    put("fcw", fm(inp["ffn_conv_w"], 88))
    put("fcb", fm(inp["ffn_conv_b"], 88))
    wa = np.asarray(inp["rg_wa"], f32)
    wx = np.asarray(inp["rg_wx"], f32)
    rgw = np.stack([wa, wx], axis=2)
    rgw = np.ascontiguousarray(rgw.transpose(0, 4, 1, 2, 3, 5)).reshape(NL, 128, 4096)
    w_in = np.asarray(inp["w_in"], f32)
    hgc = w_in[:, :, :5120].reshape(NL, D, 5, 8, 128)
    w_hg = np.ascontiguousarray(hgc.transpose(0, 3, 1, 2, 4)).reshape(NL, 8, D, 640)
    rgc = w_in[:, :, 5120:].reshape(NL, D, 2, 8, 128)
    w_rg = np.ascontiguousarray(rgc.transpose(0, 3, 1, 2, 4)).reshape(NL, 8, D, 256)
    w_up = np.asarray(inp["ffn_w_up"], f32).reshape(NL, D, 2, 44, 128)
    w_up = np.ascontiguousarray(w_up.transpose(0, 3, 1, 2, 4)).reshape(NL, 44, D, 256)
    w_dn = np.asarray(inp["ffn_w_down"], f32).reshape(NL, DFF, 16, 128)
    w_dn = np.ascontiguousarray(w_dn.transpose(0, 2, 1, 3))
    shared = dict(vec=vec, rgw=rgw, w_mod=np.ascontiguousarray(inp["w_mod"], dtype=f32), w_hg=w_hg, w_rg=w_rg,
                  w_out=np.ascontiguousarray(inp["w_out"], dtype=f32), w_up=w_up, w_dn=w_dn)
    x = np.asarray(inp["x"], f32)
    ctx = np.asarray(inp["ctx"], f32)
    c = np.asarray(inp["c"], f32)
    c_ctx = np.asarray(inp["c_ctx"], f32)
    per = []
    for b in range(4):
        res0 = np.ascontiguousarray(np.concatenate([x[b].T, ctx[b].T], axis=1))
        cv = np.stack([c[b], c_ctx], axis=-1).reshape(16, 128, 2).transpose(1, 0, 2)
        per.append(dict(res0=res0, cvec=np.ascontiguousarray(cv)))
    return shared, per


_NC_CACHE = {}


def kernel(**inputs):
    shared, per = prep_inputs(inputs)
    if "nc" not in _NC_CACHE:
        _NC_CACHE["nc"] = build()
    nc = _NC_CACHE["nc"]
    in_maps = []
    for core in range(8):
        m = dict(shared)
        m.update(per[core % 4])
        in_maps.append(m)
    res = run_bass_kernel_spmd(nc, in_maps, core_ids=list(range(8)))
    out = np.stack([res.results[b]["y"].T for b in range(4)], axis=0)
    return np.ascontiguousarray(out.astype(np.float32))
```

```python
import contextlib
import os as _os
import numpy as np
import concourse.bass as bass
import concourse.mybir as mybir
from concourse.bass_utils import run_bass_kernel_spmd

F32 = mybir.dt.float32
BF16 = mybir.dt.bfloat16
I32 = mybir.dt.int32
AF = mybir.ActivationFunctionType
ALU = mybir.AluOpType

D = 2048
TL = 4096
TC = 256
T = TL + TC
NL = 4
NCH = 68
DFF = 5632
EPS = 1e-6
SAME_ENGINE_SYNC = False
N_DMA_SEMS = 8
N_POOL_SEMS = 3
SEM_RESET_AT = int(_os.environ.get('MK_RESET', '1000000000'))

VO = {}
_o = 0
for _n, _sz in [("b_mod", 4 * 96), ("norm_g", 4 * 4 * 16), ("lb", 4 * 2 * 8), ("hgg", 4 * 8),
                ("rcw", 4 * 4 * 8), ("rcb", 4 * 8), ("rba", 4 * 2 * 8), ("rbx", 4 * 2 * 8), ("rlam", 4 * 2 * 8),
                ("fcw", 4 * 3 * 88), ("fcb", 4 * 88)]:
    VO[_n] = _o
    _o += _sz
NV = _o


_UC = [0]


def _u():
    _UC[0] += 1
    return "t%d_" % _UC[0]


_ALL_BUFS = []


class Buf:
    __slots__ = ("name", "w", "r")

    def __init__(self, name=""):
        self.name = name
        self.w = {}
        self.r = {}
        _ALL_BUFS.append(self)


class Sched:
    ENGS = ("pe", "dve", "act", "pool", "sp")

    def __init__(self, nc, sems, dma_sems):
        self.nc = nc
        self.sem = dict(sems)
        self.qsems = {}
        for q, lst in dma_sems.items():
            self.qsems[q] = []
            for i, s in enumerate(lst):
                self.sem[("dma", q, i)] = s
                self.qsems[q].append(("dma", q, i))
        self.qrr = {q: 0 for q in dma_sems}
        self.same = True
        self.off = set(_os.environ.get('MK_OFF', 'hg,a').split(','))
        self.prog = {e: [] for e in self.ENGS}
        self.cnt = {k: 0 for k in self.sem}
        self.known = {e: {} for e in self.ENGS}
        self.dma_rr = 0
        self.ninst = 0

    def _waits(self, e, reads, writes, extra=(), force_same=False, dma_write=False):
        need = {}
        for b in reads:
            for k, v in b.w.items():
                if need.get(k, 0) < v:
                    need[k] = v
        for b in writes:
            for k, v in b.w.items():
                if dma_write and isinstance(k, tuple):
                    continue
                if need.get(k, 0) < v:
                    need[k] = v
            for k, v in b.r.items():
                if need.get(k, 0) < v:
                    need[k] = v
        for k, v in extra:
            if need.get(k, 0) < v:
                need[k] = v
        out = []
        kn = self.known[e]
        for k, v in need.items():
            if k == e and not force_same and (e == "pe" or (e != "pool" and not (SAME_ENGINE_SYNC or self.same))):
                continue
            if kn.get(k, 0) >= v:
                continue
            kn[k] = v
            out.append((k, v))
        return out

    def op(self, e, fn, reads=(), writes=(), force_same=False):
        waits = self._waits(e, reads, writes, force_same=force_same)
        self.cnt[e] += 1
        t = self.cnt[e]
        self.prog[e].append((waits, fn, e, 1))
        for b in reads:
            b.r[e] = t
        for b in writes:
            b.w = {e: t}
            b.r = {}
        self.ninst += 1

    def dma(self, q, fn, reads=(), writes=()):
        i = self.qrr[q]
        self.qrr[q] = (i + 1) % len(self.qsems[q])
        k = self.qsems[q][i]
        extra = [(k, self.cnt[k])] if self.cnt[k] > 0 else []
        waits = self._waits(q, reads, writes, extra, force_same=True, dma_write=True)
        self.cnt[k] += 16
        t = self.cnt[k]
        self.prog[q].append((waits, fn, k, 16))
        for b in reads:
            b.r[k] = t
        for b in writes:
            b.w = {kk: vv for kk, vv in b.w.items() if isinstance(kk, tuple)}
            b.w[k] = t
            b.r = {}
        self.ninst += 1

    def reset_counts(self):
        for k in self.cnt:
            self.cnt[k] = 0
        self.known = {e: {} for e in self.ENGS}
        for b in _ALL_BUFS:
            b.w = {}
            b.r = {}

    def barrier(self):
        for e in self.ENGS:
            waits = []
            kn = self.known[e]
            for k, v in self.cnt.items():
                if v > 0 and k != e and kn.get(k, 0) < v:
                    kn[k] = v
                    waits.append((k, v))
            if e != "pe" and self.cnt[e] > 0 and kn.get(e, 0) < self.cnt[e]:
                kn[e] = self.cnt[e]
                waits.append((e, self.cnt[e]))
            if waits:
                self.prog[e].append((waits, None, None, 0))

    def emit(self, block):
        engmap = {"pe": block.tensor, "dve": block.vector, "act": block.scalar,
                  "pool": block.gpsimd, "sp": block.sync}
        sem = self.sem
        for e in self.ENGS:
            prog = self.prog[e]
            if not prog:
                continue

            def body(eng, prog=prog):
                for waits, fn, k, inc in prog:
                    for wk, wv in waits:
                        eng.wait_ge(sem[wk], wv)
                    if fn is not None:
                        fn(eng).then_inc(sem[k], inc)
            engmap[e](body)
            self.prog[e] = []


class Ctx:
    pass


def build(nlayers=NL, taps=()):
    nc = bass.Bass("TRN2", target_bir_lowering=False)
    g = Ctx()
    g.nc = nc
    dt_in = lambda n, s: nc.dram_tensor(n, s, F32, kind="ExternalInput").ap()
    g.res0 = dt_in("res0", [D, T])
    g.cvec = dt_in("cvec", [128, 16, 2])
    g.vec = dt_in("vec", [128, NV])
    g.rgw = dt_in("rgw", [NL, 128, 4096])
    g.w_mod = dt_in("w_mod", [NL, D, 6 * D])
    g.w_hg = dt_in("w_hg", [NL, 8, D, 640])
    g.w_rg = dt_in("w_rg", [NL, 8, D, 256])
    g.w_out = dt_in("w_out", [NL, D, D])
    g.w_up = dt_in("w_up", [NL, 44, D, 256])
    g.w_dn = dt_in("w_dn", [NL, 16, DFF, 128])
    g.y = nc.dram_tensor("y", [D, TL], F32, kind="ExternalOutput").ap()
    g.res = nc.dram_tensor("res", [D, T], F32, kind="Internal").ap()
    g.hT = nc.dram_tensor("hT", [D, T], BF16, kind="Internal").ap()
    g.mT = nc.dram_tensor("mT", [D, T], BF16, kind="Internal").ap()
    g.h2T = nc.dram_tensor("h2T", [D, T], BF16, kind="Internal").ap()
    g.wub = nc.dram_tensor("wub", [44, 128, 16 * 256], BF16, kind="Internal").ap()
    g.wdb = nc.dram_tensor("wdb", [16, 128, 44 * 128], BF16, kind="Internal").ap()
    g.B_wub = [Buf("wub%d" % i) for i in range(44)]
    g.B_wdb = [Buf("wdb%d" % i) for i in range(16)]
    g.taps = {}
    for name, shape, dt in taps:
        g.taps[name] = nc.dram_tensor("tap_" + name, shape, dt, kind="ExternalOutput").ap()
    g.B_res_t = [Buf("res%d" % i) for i in range(17)]
    g.B_hT_t = [Buf("hT%d" % i) for i in range(17)]
    g.B_h2T_t = [Buf("h2T%d" % i) for i in range(17)]
    g.B_mT = Buf("mT")
    g.B_y = Buf("y")
    g.B_tap = Buf("tap")

    with contextlib.ExitStack() as st:
        def sb(name, shape, dt):
            return st.enter_context(nc.sbuf_tensor(_u() + name, shape, dt))
        g.vecs = sb("vecs", [128, NV], F32)
        g.modv = sb("modv", [128, NL, 96, 2], F32)
        g.A1 = sb("A1", [128, NL, 16, 2], F32)
        g.G1 = sb("G1", [128, NL, 16, 2], F32)
        g.A2 = sb("A2", [128, NL, 16, 2], F32)
        g.G2 = sb("G2", [128, NL, 16, 2], F32)
        g.lbv = sb("lbv", [128, NL, 16], F32)
        g.omlb = sb("omlb", [128, NL, 16], F32)
        g.nsp8 = sb("nsp8", [128, NL, 16], F32)
        g.ones_bf = sb("ones_bf", [128, 128], BF16)
        g.ident_bf = sb("ident_bf", [128, 128], BF16)
        g.epsb = sb("epsb", [128, 1], F32)
        g.scv = sb("scv", [128, 16, 2], F32)
        g.B_const = Buf("const")
        g.ps = [st.enter_context(nc.psum_tensor("ps%d" % i, [128, 512], F32)) for i in range(7)]
        g.ps_bf = st.enter_context(nc.psum_tensor("ps_bf", [128, 1024], BF16))
        g.B_ps = [Buf("ps%d" % i) for i in range(7)]
        g.B_psbf = Buf("psbf")
        sems = {e: st.enter_context(nc.semaphore("s_" + e)) for e in Sched.ENGS}
        dsems = {"sp": [st.enter_context(nc.semaphore("dsp%d" % i)) for i in range(N_DMA_SEMS)],
                 "pool": [st.enter_context(nc.semaphore("dpl%d" % i)) for i in range(N_POOL_SEMS)]}
        S = Sched(nc, sems, dsems)
        g.S = S
        g.blk = None

        def open_block():
            cm = nc.Block()
            g.blk = (cm, cm.__enter__())

        def close_block():
            g.blk[0].__exit__(None, None, None)
            g.blk = None

        def sem_reset():
            close_block()
            with nc.Block() as b2:
                b2.tensor(lambda eng: eng.sem_clear(S.sem["pe"]))
                b2.vector(lambda eng: eng.sem_clear(S.sem["dve"]))
                b2.scalar(lambda eng: eng.sem_clear(S.sem["act"]))
                b2.gpsimd(lambda eng: eng.sem_clear(S.sem["pool"]))

                def spclr(eng):
                    eng.sem_clear(S.sem["sp"])
                    for kk in S.sem:
                        if isinstance(kk, tuple):
                            eng.sem_clear(S.sem[kk])
                b2.sync(spclr)
            S.reset_counts()
            open_block()

        def flush():
            S.barrier()
            S.emit(g.blk[1])
            if max(S.cnt.values()) > SEM_RESET_AT:
                sem_reset()
        g.flush = flush
        open_block()
        try:
            import os
            stop = os.environ.get("MK_STOP", "")
            phase_setup(g)
            flush()
            if stop == "setup":
                return nc
            phase_mod(g, nlayers)
            flush()
            if stop == "mod":
                return nc
            phase_norm1_l0(g)
            flush()
            if "h1" in g.taps:
                S.dma("sp", lambda e: e.dma_start(out=g.taps["h1"], in_=g.hT), reads=g.B_hT_t, writes=[g.B_tap])
                flush()
            if stop == "n1":
                return nc
            for l in range(nlayers):
                for hd in range(8):
                    if os.environ.get("MK_SKIPMIX"):
                        break
                    S.same = "hg" not in S.off
                    phase_hg(g, l, hd)
                    S.same = True
                    flush()
                    if stop == "hg0":
                        return nc
                if stop == "hg":
                    return nc
                if not os.environ.get("MK_SKIPMIX"):
                    S.same = True
                    phase_rg(g, l)
                flush()
                if "m" in g.taps and l == 0:
                    S.dma("sp", lambda e: e.dma_start(out=g.taps["m"], in_=g.mT), reads=[g.B_mT], writes=[g.B_tap])
                    flush()
                if stop == "rg":
                    return nc
                if not os.environ.get("MK_SKIPA"):
                    S.same = "a" not in S.off
                    phase_a(g, l, last=(l == NL - 1))
                    S.same = True
                flush()
                if stop == "a":
                    return nc
                phase_b(g, l, last=(l == NL - 1), final=(l == nlayers - 1))
                flush()
        finally:
            if g.blk is not None:
                close_block()
    return nc


def V(g, name, *idx_shape):
    return g.vecs[:, VO[name]:]


def phase_setup(g):
    S, nc = g.S, g.nc
    Bc = g.B_const
    S.dma("sp", lambda e: e.dma_start(out=g.vecs[:], in_=g.vec), writes=[Bc])
    S.dma("sp", lambda e: e.dma_start(out=g.scv[:], in_=g.cvec), writes=[Bc])
    S.op("pool", lambda e: e.memset(g.ones_bf[:], 1.0), writes=[Bc])
    S.op("pool", lambda e: e.memset(g.epsb[:], EPS), writes=[Bc])
    with g.nc.sbuf_tensor(_u() + "identf", [128, 128], F32) as identf:
        S.op("pool", lambda e: e.memset(identf[:], 1.0), writes=[Bc])
        S.op("pool", lambda e: e.affine_select(out=identf[:], in_=identf[:], pattern=[[1, 128]],
                                                compare_op=ALU.is_equal, fill=0.0, base=0, channel_multiplier=-1),
             reads=[Bc], writes=[Bc])
        S.op("pool", lambda e: e.tensor_copy(out=g.ident_bf[:], in_=identf[:]), reads=[Bc], writes=[Bc])
        S.op("act", lambda e: e.activation(out=g.scv[:], in_=g.scv[:], func=AF.Silu), reads=[Bc], writes=[Bc])
        lbraw = g.vecs[:, VO["lb"]:VO["lb"] + 64].rearrange("p (l x) -> p l x", l=NL)
        with g.nc.sbuf_tensor(_u() + "lbe", [128, NL, 16], F32) as lbe, g.nc.sbuf_tensor(_u() + "lbs", [128, 16], F32) as lbs:
            S.op("act", lambda e: e.activation(out=lbe[:], in_=lbraw, func=AF.Exp), reads=[Bc], writes=[Bc])
            S.op("dve", lambda e: e.tensor_tensor(out=lbs[:], in0=lbe[:, 0, :], in1=lbe[:, 1, :], op=ALU.add), reads=[Bc], writes=[Bc])
            S.op("dve", lambda e: e.tensor_tensor(out=lbs[:], in0=lbs[:], in1=lbe[:, 2, :], op=ALU.add), reads=[Bc], writes=[Bc])
            S.op("dve", lambda e: e.tensor_tensor(out=lbs[:], in0=lbs[:], in1=lbe[:, 3, :], op=ALU.add), reads=[Bc], writes=[Bc])
            S.op("dve", lambda e: e.reciprocal(out=lbs[:], in_=lbs[:]), reads=[Bc], writes=[Bc])
            S.op("dve", lambda e: e.memset(g.lbv[:, 0, :], 0.0), writes=[Bc])
            for l in range(1, NL):
                S.op("dve", lambda e, l=l: e.tensor_tensor(out=lbe[:, l, :], in0=lbe[:, l, :], in1=lbs[:], op=ALU.mult), reads=[Bc], writes=[Bc])
                S.op("dve", lambda e, l=l: e.tensor_tensor(out=g.lbv[:, l, :], in0=g.lbv[:, l - 1, :], in1=lbe[:, l, :], op=ALU.add), reads=[Bc], writes=[Bc])
            S.op("dve", lambda e: e.tensor_scalar(out=g.omlb[:], in0=g.lbv[:], scalar1=-1.0, scalar2=1.0, op0=ALU.mult, op1=ALU.add), reads=[Bc], writes=[Bc])
            lam = g.vecs[:, VO["rlam"]:VO["rlam"] + 64].rearrange("p (l x) -> p l x", l=NL)
            S.op("act", lambda e: e.activation(out=g.nsp8[:], in_=lam, func=AF.Exp, scale=-1.0), reads=[Bc], writes=[Bc])
            S.op("act", lambda e: e.activation(out=g.nsp8[:], in_=g.nsp8[:], func=AF.Ln, bias=1.0, scale=1.0), reads=[Bc], writes=[Bc])
            S.op("dve", lambda e: e.tensor_scalar(out=g.nsp8[:], in0=g.nsp8[:], scalar1=-8.0, scalar2=None, op0=ALU.mult), reads=[Bc], writes=[Bc])
            g.flush()


def phase_mod(g, nlayers):
    S, nc = g.S, g.nc
    Bc = g.B_const
    with contextlib.ExitStack() as st:
        wt = [st.enter_context(nc.sbuf_tensor(_u() + "wmod%d" % i, [128, 16, 512], F32)) for i in range(2)]
        Bw = [Buf("wmod0"), Buf("wmod1")]
        Bp = g.B_ps[0]
        it = 0
        for l in range(nlayers):
            for cb in range(24):
                j = it % 2
                it += 1
                src = g.w_mod[l, :, cb * 512:(cb + 1) * 512].rearrange("(kc p) n -> p kc n", p=128)
                for half in range(2):
                    S.dma("sp", lambda e, j=j, src=src, half=half: e.dma_start(out=wt[j][:, half * 8:(half + 1) * 8, :], in_=src[:, half * 8:(half + 1) * 8, :]), writes=[Bw[j]])
                for mi in range(4):
                    m = cb * 4 + mi
                    for kc in range(16):
                        S.op("pe", lambda e, j=j, mi=mi, kc=kc, m=m: e.matmul(
                            g.ps[0][:, 2 * m:2 * m + 2], lhsT=wt[j][:, kc, mi * 128:(mi + 1) * 128], rhs=g.scv[:, kc, :],
                            start=(kc == 0), stop=(kc == 15)), reads=[Bw[j], Bc], writes=[Bp])
            bm = g.vecs[:, VO["b_mod"] + l * 96: VO["b_mod"] + (l + 1) * 96]
            S.op("dve", lambda e, l=l, bm=bm: e.tensor_tensor(
                out=g.modv[:, l, :, :], in0=g.ps[0][:, 0:192].rearrange("p (m w) -> p m w", w=2),
                in1=bm.unsqueeze(2).to_broadcast([128, 96, 2]), op=ALU.add), reads=[Bp, Bc], writes=[Bc])
            ng = g.vecs[:, VO["norm_g"] + l * 64: VO["norm_g"] + (l + 1) * 64].rearrange("p (j k) -> p j k", j=4)

            def gb(j):
                return ng[:, j, :].unsqueeze(2).to_broadcast([128, 16, 2])
            mv = g.modv[:, l, :, :]
            S.op("dve", lambda e, l=l, mv=mv, gb=gb: e.scalar_tensor_tensor(out=g.A1[:, l], in0=mv[:, 16:32, :], scalar=1.0, in1=gb(0), op0=ALU.add, op1=ALU.mult), reads=[Bc], writes=[Bc])
            S.op("dve", lambda e, l=l, mv=mv, gb=gb: e.tensor_tensor(out=g.G1[:, l], in0=mv[:, 32:48, :], in1=gb(1), op=ALU.mult), reads=[Bc], writes=[Bc])
            S.op("dve", lambda e, l=l, mv=mv, gb=gb: e.scalar_tensor_tensor(out=g.A2[:, l], in0=mv[:, 64:80, :], scalar=1.0, in1=gb(2), op0=ALU.add, op1=ALU.mult), reads=[Bc], writes=[Bc])
            S.op("dve", lambda e, l=l, mv=mv, gb=gb: e.tensor_tensor(out=g.G2[:, l], in0=mv[:, 80:96, :], in1=gb(3), op=ALU.mult), reads=[Bc], writes=[Bc])
            g.flush()


def tb(lst, lo, hi):
    return lst[lo // 256:(hi + 255) // 256]


def tok_tiles(nt):
    out = [(t0, nt, 0) for t0 in range(0, TL, nt)]
    out += [(TL + t0, min(nt, TC), 1) for t0 in range(0, TC, nt)]
    return out


def emit_norm_mod(g, xt, Bx, n, A, sh, sq, Bsq, tmp, Btmp, rstd, Brstd, hout, Bh, psb):
    S = g.S
    Bc = g.B_const
    Bp = g.B_ps[psb]
    ps = g.ps[psb]
    for kc in range(16):
        S.op("act", lambda e, kc=kc: e.activation(out=sq[:, kc, :n], in_=xt[:, kc, :n], func=AF.Square), reads=[Bx], writes=[Bsq])
    for kc in range(16):
        S.op("pe", lambda e, kc=kc: e.matmul(ps[:, :n], lhsT=g.ones_bf[:], rhs=sq[:, kc, :n], start=(kc == 0), stop=(kc == 15)), reads=[Bsq, Bc], writes=[Bp])
    S.op("act", lambda e: e.activation(out=rstd[:, :n], in_=ps[:, :n], func=AF.Sqrt, bias=g.epsb[:, 0:1], scale=1.0 / D), reads=[Bp, Bc], writes=[Brstd])
    S.op("dve", lambda e: e.reciprocal(out=rstd[:, :n], in_=rstd[:, :n]), reads=[Brstd], writes=[Brstd])
    for kc in range(16):
        S.op("dve", lambda e, kc=kc: e.scalar_tensor_tensor(out=tmp[:, :n], in0=xt[:, kc, :n], scalar=A[:, kc:kc + 1], in1=rstd[:, :n], op0=ALU.mult, op1=ALU.mult), reads=[Bx, Brstd, Bc], writes=[Btmp])
        S.op("act", lambda e, kc=kc: e.activation(out=hout[:, kc, :n], in_=tmp[:, :n], func=AF.Identity, bias=sh[:, kc:kc + 1], scale=1.0), reads=[Btmp, Bc], writes=[Bh])


def phase_norm1_l0(g):
    S, nc = g.S, g.nc
    NT = 256
    with contextlib.ExitStack() as st:
        xt = [st.enter_context(nc.sbuf_tensor(_u() + "n1x%d" % i, [128, 16, NT], F32)) for i in range(2)]
        ht = [st.enter_context(nc.sbuf_tensor(_u() + "n1h%d" % i, [128, 16, NT], BF16)) for i in range(2)]
        sq = st.enter_context(nc.sbuf_tensor(_u() + "n1sq", [128, 16, NT], BF16))
        tmp = st.enter_context(nc.sbuf_tensor(_u() + "n1tmp", [128, NT], F32))
        rstd = st.enter_context(nc.sbuf_tensor(_u() + "n1rstd", [128, NT], F32))
        Bx = [Buf(), Buf()]
        Bh = [Buf(), Buf()]
        Bsq, Btmp, Brstd = Buf(), Buf(), Buf()
        n1tiles = tok_tiles(NT)

        def issue_load(i):
            t0, n, which = n1tiles[i]
            j = i % 2
            src = g.res0[:, t0:t0 + n].rearrange("(kc p) t -> p kc t", p=128)
            S.dma("sp", lambda e, j=j, src=src, n=n: e.dma_start(out=xt[j][:, :, :n], in_=src), writes=[Bx[j]])
        issue_load(0)
        for i, (t0, n, which) in enumerate(n1tiles):
            j = i % 2
            if i + 1 < len(n1tiles):
                issue_load(i + 1)
            dst = g.res[:, t0:t0 + n].rearrange("(kc p) t -> p kc t", p=128)
            S.dma("sp", lambda e, j=j, dst=dst, n=n: e.dma_start(out=dst, in_=xt[j][:, :, :n]), reads=[Bx[j]], writes=tb(g.B_res_t, t0, t0 + n))
            A = g.A1[:, 0, :, which]
            sh = g.modv[:, 0, 0:16, which]
            emit_norm_mod(g, xt[j], Bx[j], n, A, sh, sq, Bsq, tmp, Btmp, rstd, Brstd, ht[j], Bh[j], 1 + j)
            dsth = g.hT[:, t0:t0 + n].rearrange("(kc p) t -> p kc t", p=128)
            S.dma("sp", lambda e, j=j, dsth=dsth, n=n: e.dma_start(out=dsth, in_=ht[j][:, :, :n]), reads=[Bh[j]], writes=tb(g.B_hT_t, t0, t0 + n))
        g.flush()


def hg_pos_view(buf3, i):
    return buf3[:, 4:68, 8 * i:8 * i + 8].rearrange("p c r -> p r c")


def phase_hg(g, l, hd):
    S, nc = g.S, g.nc
    Bc = g.B_const
    NT = 512
    tiles = tok_tiles(NT)
    lbf = [g.lbv[:, l, d * 8 + hd:d * 8 + hd + 1] for d in range(2)]
    omf = [g.omlb[:, l, d * 8 + hd:d * 8 + hd + 1] for d in range(2)]
    with contextlib.ExitStack() as st0:
        def sb0(name, shape, dt):
            return st0.enter_context(nc.sbuf_tensor(_u() + name, shape, dt))
        qs = sb0("hg_qs", [128, NCH, 64], BF16)
        sg = sb0("hg_sg", [128, NCH, 64], BF16)
        vT = sb0("hg_vT", [128, NCH, 64], BF16)
        kk = [sb0("hg_k%d" % d, [128, NCH, 64], BF16) for d in range(2)]
        qt = [sb0("hg_qt%d" % d, [128, NCH, 64], BF16) for d in range(2)]
        sm = sb0("hg_sm", [128, 2, 5, NCH], F32)
        B_qs, B_sg, B_vT = Buf(), Buf(), Buf()
        B_k = [Buf(), Buf()]
        B_qt = [Buf(), Buf()]
        B_sm = Buf()
        with contextlib.ExitStack() as st1:
            def sb1(name, shape, dt):
                return st1.enter_context(nc.sbuf_tensor(_u() + name, shape, dt))
            lf = [sb1("hg_lf%d" % d, [128, NCH, 65], F32) for d in range(2)]
            msk = sb1("hg_msk", [128, NCH, 65], BF16)
            B_lf = [Buf(), Buf()]
            B_msk = Buf()
            S.op("pool", lambda e: e.memset(msk[:], 1.0), writes=[B_msk])
            S.op("pool", lambda e: e.memset(msk[:, :, 0:1], 0.0), writes=[B_msk])
            for d in range(2):
                S.op("pool", lambda e, d=d: e.memset(lf[d][:, :, 0:1], 0.0), writes=[B_lf[d]])
            with contextlib.ExitStack() as st2:
                wt = st2.enter_context(nc.sbuf_tensor(_u() + "hg_w", [128, 16, 640], BF16))
                hts = [st2.enter_context(nc.sbuf_tensor(_u() + "hg_h%d" % i, [128, 16, NT], BF16)) for i in range(2)]
                sgt = st2.enter_context(nc.sbuf_tensor(_u() + "hg_sgt", [128, NT], F32))
                B_w, B_h, B_sgt = Buf(), [Buf(), Buf()], Buf()
                src = g.w_hg[l, hd].rearrange("(kc p) n -> p kc n", p=128)
                for q4 in range(4):
                    S.dma("pool", lambda e, q4=q4: e.dma_start(out=wt[:, q4 * 4:(q4 + 1) * 4, :], in_=src[:, q4 * 4:(q4 + 1) * 4, :]), writes=[B_w])
                for i, (t0, n, which) in enumerate(tiles):
                    j = i % 2
                    hsrc = g.hT[:, t0:t0 + n].rearrange("(kc p) t -> p kc t", p=128)
                    S.dma("sp", lambda e, j=j, hsrc=hsrc, n=n: e.dma_start(out=hts[j][:, :, :n], in_=hsrc), reads=tb(g.B_hT_t, t0, t0 + n), writes=[B_h[j]])

                    def dstv(buf, w0=0):
                        if which == 0:
                            return hg_pos_view(buf[:, :, w0:w0 + 64], i)
                        return buf[:, 0:4, w0:w0 + 64]

                    def srcv(ap):
                        if which == 0:
                            return ap.rearrange("p (r c) -> p r c", c=64)
                        return ap.rearrange("p (c r) -> p c r", r=64)
                    for blk in range(5):
                        pb = blk % 5
                        ps, Bp = g.ps[pb], g.B_ps[pb]
                        for kc in range(16):
                            S.op("pe", lambda e, ps=ps, kc=kc, blk=blk, j=j, n=n: e.matmul(
                                ps[:, :n], lhsT=wt[:, kc, blk * 128:(blk + 1) * 128], rhs=hts[j][:, kc, :n],
                                start=(kc == 0), stop=(kc == 15)), reads=[B_w, B_h[j]], writes=[Bp])
                        pv = srcv(ps[:, :n])
                        if blk == 0:
                            S.op("act", lambda e, pv=pv, o=dstv(qs): e.activation(out=o, in_=pv, func=AF.Silu), reads=[Bp], writes=[B_qs])
                        elif blk in (1, 2):
                            d = blk - 1
                            S.op("act", lambda e, ps=ps, n=n: e.activation(out=sgt[:, :n], in_=ps[:, :n], func=AF.Sigmoid), reads=[Bp], writes=[B_sgt])
                            S.op("dve", lambda e, d=d, sv=srcv(sgt[:, :n]), o=dstv(kk[d]): e.tensor_scalar(
                                out=o, in0=sv, scalar1=omf[d], scalar2=-1.0, op0=ALU.mult, op1=ALU.mult), reads=[B_sgt, Bc], writes=[B_k[d]])
                            S.op("dve", lambda e, d=d, o=dstv(kk[d]): e.tensor_scalar(
                                out=o, in0=o, scalar1=omf[d], scalar2=None, op0=ALU.add), reads=[B_k[d], Bc], writes=[B_k[d]])
                            S.op("act", lambda e, d=d, sv=srcv(sgt[:, :n]), o=dstv(lf[d], 1): e.activation(
                                out=o, in_=sv, func=AF.Ln, bias=lbf[d], scale=omf[d]), reads=[B_sgt, Bc], writes=[B_lf[d]])
                        elif blk == 3:
                            S.op("act", lambda e, pv=pv, o=dstv(vT): e.activation(out=o, in_=pv, func=AF.Copy), reads=[Bp], writes=[B_vT])
                        else:
                            S.op("act", lambda e, pv=pv, o=dstv(sg): e.activation(out=o, in_=pv, func=AF.Silu), reads=[Bp], writes=[B_sg])
                g.flush()
            with nc.sbuf_tensor(_u() + "hg_tmpE", [128, NCH, 64], F32) as tmpE:
                B_tE = Buf()
                for d in range(2):
                    lff = lf[d][:].rearrange("p c w -> p (c w)")
                    mf = msk[:].rearrange("p c w -> p (c w)")
                    S.op("dve", lambda e, lff=lff, mf=mf: e.tensor_tensor_scan(out=lff, data0=mf, data1=lff, initial=0.0, op0=ALU.mult, op1=ALU.add), reads=[B_lf[d], B_msk], writes=[B_lf[d]])
                    if d == 0:
                        Vw = lf[d][:, :, 1:65]
                        refcol = lf[d][:, :, 32]
                    else:
                        Vw = lf[d][:, :, 0:64]
                        refcol = lf[d][:, :, 32]
                    totcol = lf[d][:, :, 64]
                    sref, stot, sr, ss1, sa = [sm[:, d, x, :] for x in range(5)]
                    S.op("dve", lambda e, sref=sref, refcol=refcol: e.tensor_copy(out=sref, in_=refcol), reads=[B_lf[d]], writes=[B_sm])
                    S.op("dve", lambda e, stot=stot, totcol=totcol: e.tensor_copy(out=stot, in_=totcol), reads=[B_lf[d]], writes=[B_sm])
                    S.op("act", lambda e, sa=sa, stot=stot: e.activation(out=sa, in_=stot, func=AF.Exp), reads=[B_sm], writes=[B_sm])
                    e_ref, e_tr = (sr, ss1) if d == 0 else (ss1, sr)
                    S.op("act", lambda e, e_ref=e_ref, sref=sref: e.activation(out=e_ref, in_=sref, func=AF.Exp), reads=[B_sm], writes=[B_sm])
                    S.op("dve", lambda e, e_tr=e_tr, stot=stot, sref=sref: e.tensor_tensor(out=e_tr, in0=stot, in1=sref, op=ALU.subtract), reads=[B_sm], writes=[B_sm])
                    S.op("act", lambda e, e_tr=e_tr: e.activation(out=e_tr, in_=e_tr, func=AF.Exp), reads=[B_sm], writes=[B_sm])
                    S.op("dve", lambda e, Vw=Vw, sref=sref: e.tensor_tensor(out=Vw, in0=Vw, in1=sref.unsqueeze(2).to_broadcast([128, NCH, 64]), op=ALU.subtract), reads=[B_lf[d], B_sm], writes=[B_lf[d]])
                    sq_, sk_ = (1.0, -1.0) if d == 0 else (-1.0, 1.0)
                    S.op("act", lambda e, Vw=Vw, sq_=sq_: e.activation(out=tmpE[:], in_=Vw, func=AF.Exp, scale=sq_), reads=[B_lf[d]], writes=[B_tE])
                    S.op("dve", lambda e, d=d: e.tensor_tensor(out=qt[d][:], in0=qs[:], in1=tmpE[:], op=ALU.mult), reads=[B_qs, B_tE], writes=[B_qt[d]])
                    S.op("act", lambda e, Vw=Vw, sk_=sk_: e.activation(out=tmpE[:], in_=Vw, func=AF.Exp, scale=sk_), reads=[B_lf[d]], writes=[B_tE])
                    S.op("dve", lambda e, d=d: e.tensor_tensor(out=kk[d][:], in0=kk[d][:], in1=tmpE[:], op=ALU.mult), reads=[B_k[d], B_tE], writes=[B_k[d]])
                g.flush()
        with contextlib.ExitStack() as st1:
            def sb1(name, shape, dt):
                return st1.enter_context(nc.sbuf_tensor(_u() + name, shape, dt))
            v_tm = sb1("hg_vtm", [64, NCH, 128], BF16)
            k_tm = sb1("hg_ktm", [64, NCH, 128], BF16)
            o_sb = sb1("hg_o", [128, NCH, 64], F32)
            mk = [sb1("hg_mask%d" % d, [64, 8, 64], I32) for d in range(2)]
            scb = [sb1("hg_scb%d" % i, [64, 8, 64], BF16) for i in range(2)]
            Ur = sb1("hg_U", [128, 8, 128], F32)
            Sst = [sb1("hg_S%d" % i, [128, 128], F32) for i in range(2)]
            Spr = sb1("hg_Sp", [128, 4, 128], BF16)
            B_vtm, B_ktm, B_o = Buf(), Buf(), Buf()
            B_mk = Buf()
            B_scb = [Buf(), Buf()]
            B_U = [Buf() for _ in range(8)]
            B_S = [Buf(), Buf()]
            B_Sp = [Buf() for _ in range(4)]
            S.op("pool", lambda e: e.iota(mk[0][:], pattern=[[0, 8], [1, 64]], base=0, channel_multiplier=-1), writes=[B_mk])
            S.op("pool", lambda e: e.iota(mk[1][:], pattern=[[0, 8], [-1, 64]], base=0, channel_multiplier=1), writes=[B_mk])
            for d in range(2):
                S.op("dve", lambda e, d=d: e.tensor_single_scalar(out=mk[d][:], in_=mk[d][:], scalar=0, op=ALU.is_ge), reads=[B_mk], writes=[B_mk])
            for i in range(2):
                S.op("pool", lambda e, i=i: e.memset(scb[i][:], 0.0), writes=[B_scb[i]])

            def transpose_all(src, Bsrc, dst, Bdst):
                for c0 in range(0, NCH, 8):
                    ncg = min(8, NCH - c0)
                    for cc in range(ncg):
                        S.op("pe", lambda e, c0=c0, cc=cc: e.transpose(g.ps_bf[0:64, cc * 128:(cc + 1) * 128], src[:, c0 + cc, :], g.ident_bf[:]),
                             reads=[Bsrc, Bc], writes=[g.B_psbf])
                    S.op("act", lambda e, c0=c0, ncg=ncg: e.activation(out=dst[:, c0:c0 + ncg, :], in_=g.ps_bf[0:64, 0:ncg * 128].rearrange("p (c x) -> p c x", x=128), func=AF.Copy),
                         reads=[g.B_psbf], writes=[Bdst])
            transpose_all(vT, B_vT, v_tm, B_vtm)
            fwd_chain = list(range(NCH))
            bwd_chain = [3, 2, 1, 0] + list(range(67, 3, -1))
            for pas, d in enumerate((1, 0)):
                chain = bwd_chain if d == 1 else fwd_chain
                transpose_all(kk[d], B_k[d], k_tm, B_ktm)
                for i2 in range(2):
                    S.op("pool", lambda e, i2=i2: e.memset(scb[i2][:], 0.0), writes=[B_scb[i2]])
                sr, ss1, sa = sm[:, d, 2, :], sm[:, d, 3, :], sm[:, d, 4, :]
                groups = [chain[0:4]] + [chain[4 + 8 * i:12 + 8 * i] for i in range(8)]
                step = 0
                prev_sp = None
                for gi, grp in enumerate(groups):
                    cmin = min(grp)
                    ng = len(grp)
                    sj = gi % 2
                    for c in grp:
                        s = c - cmin
                        S.op("pe", lambda e, c=c, s=s, d=d: e.matmul(g.ps[5][0:64, s * 64:(s + 1) * 64], lhsT=kk[d][:, c, :], rhs=qt[d][:, c, :], start=True, stop=True),
                             reads=[B_k[d], B_qt[d]], writes=[g.B_ps[5]])
                    S.op("dve", lambda e, sj=sj, ng=ng, d=d: e.copy_predicated(out=scb[sj][:, 0:ng, :], mask=mk[d][:, 0:ng, :], data=g.ps[5][0:64, 0:ng * 64].rearrange("p (c t) -> p c t", t=64)),
                         reads=[g.B_ps[5], B_mk], writes=[B_scb[sj]])
                    for c in grp:
                        s = c - cmin
                        pb = 3 + (s // 4)
                        S.op("pe", lambda e, c=c, s=s, pb=pb: e.matmul(g.ps[pb][:, (s % 4) * 128:(s % 4 + 1) * 128], lhsT=k_tm[:, c, :], rhs=v_tm[:, c, :], start=True, stop=True),
                             reads=[B_ktm, B_vtm], writes=[g.B_ps[pb]])
                    for c in grp:
                        s = c - cmin
                        pb = 3 + (s // 4)
                        S.op("act", lambda e, c=c, s=s, pb=pb, ss1=ss1: e.activation(out=Ur[:, s, :], in_=g.ps[pb][:, (s % 4) * 128:(s % 4 + 1) * 128], func=AF.Identity, scale=ss1[:, c:c + 1]),
                             reads=[g.B_ps[pb], B_sm], writes=[B_U[s]])
                    for c in grp:
                        s = c - cmin
                        ov = g.ps[6][:, s * 64:(s + 1) * 64]
                        S.op("pe", lambda e, c=c, s=s, sj=sj, ov=ov, last=(prev_sp is None): e.matmul(ov, lhsT=v_tm[:, c, :], rhs=scb[sj][:, s, :], start=True, stop=last),
                             reads=[B_vtm, B_scb[sj]], writes=[g.B_ps[6]])
                        if prev_sp is not None:
                            S.op("pe", lambda e, c=c, ov=ov, psp=prev_sp, d=d: e.matmul(ov, lhsT=Spr[:, psp, :], rhs=qt[d][:, c, :], start=False, stop=True),
                                 reads=[B_Sp[prev_sp], B_qt[d]], writes=[g.B_ps[6]])
                        sn, so = step % 2, (step + 1) % 2
                        if step == 0:
                            S.op("dve", lambda e, s=s, sn=sn: e.tensor_copy(out=Sst[sn][:], in_=Ur[:, s, :]), reads=[B_U[s]], writes=[B_S[sn]])
                        else:
                            S.op("dve", lambda e, s=s, sn=sn, so=so, c=c, sa=sa: e.scalar_tensor_tensor(out=Sst[sn][:], in0=Sst[so][:], scalar=sa[:, c:c + 1], in1=Ur[:, s, :], op0=ALU.mult, op1=ALU.add),
                                 reads=[B_S[so], B_U[s], B_sm], writes=[B_S[sn]])
                        if step + 1 < NCH:
                            cn = chain[step + 1]
                            spi = step % 4
                            S.op("act", lambda e, sn=sn, spi=spi, cn=cn, sr=sr: e.activation(out=Spr[:, spi, :], in_=Sst[sn][:], func=AF.Identity, scale=sr[:, cn:cn + 1]),
                                 reads=[B_S[sn], B_sm], writes=[B_Sp[spi]])
                            prev_sp = spi
                        step += 1
                    osrc = g.ps[6][:, 0:ng * 64].rearrange("p (c t) -> p c t", t=64)
                    odst = o_sb[:, cmin:cmin + ng, :]
                    if pas == 0:
                        S.op("act", lambda e, osrc=osrc, odst=odst: e.activation(out=odst, in_=osrc, func=AF.Copy), reads=[g.B_ps[6]], writes=[B_o])
                    else:
                        S.op("dve", lambda e, osrc=osrc, odst=odst: e.tensor_tensor(out=odst, in0=osrc, in1=odst, op=ALU.add), reads=[g.B_ps[6], B_o], writes=[B_o])
            with contextlib.ExitStack() as st2:
                osq = st2.enter_context(nc.sbuf_tensor(_u() + "hg_osq", [128, 512], BF16))
                rs = st2.enter_context(nc.sbuf_tensor(_u() + "hg_rs", [128, 512], F32))
                tmp = st2.enter_context(nc.sbuf_tensor(_u() + "hg_tmp", [128, NCH, 64], F32))
                outr = st2.enter_context(nc.sbuf_tensor(_u() + "hg_outr", [128, T], BF16))
                B_osq, B_rs, B_tmp, B_outr = Buf(), Buf(), Buf(), Buf()
                of = o_sb[:].rearrange("p c t -> p (c t)")
                tf = tmp[:].rearrange("p c t -> p (c t)")
                hgg = g.vecs[:, VO["hgg"] + l * 8 + hd: VO["hgg"] + l * 8 + hd + 1]
                for t0 in range(0, T, 512):
                    n = min(512, T - t0)
                    S.op("act", lambda e, t0=t0, n=n: e.activation(out=osq[:, :n], in_=of[:, t0:t0 + n], func=AF.Square), reads=[B_o], writes=[B_osq])
                    S.op("pe", lambda e, n=n: e.matmul(g.ps[0][:, :n], lhsT=g.ones_bf[:], rhs=osq[:, :n], start=True, stop=True), reads=[B_osq, Bc], writes=[g.B_ps[0]])
                    S.op("act", lambda e, n=n: e.activation(out=rs[:, :n], in_=g.ps[0][:, :n], func=AF.Sqrt, bias=g.epsb[:, 0:1], scale=1.0 / 128), reads=[g.B_ps[0], Bc], writes=[B_rs])
                    S.op("dve", lambda e, n=n: e.reciprocal(out=rs[:, :n], in_=rs[:, :n]), reads=[B_rs], writes=[B_rs])
                    S.op("dve", lambda e, t0=t0, n=n: e.scalar_tensor_tensor(out=tf[:, t0:t0 + n], in0=of[:, t0:t0 + n], scalar=hgg, in1=rs[:, :n], op0=ALU.mult, op1=ALU.mult), reads=[B_o, B_rs, Bc], writes=[B_tmp])
                S.op("dve", lambda e: e.tensor_tensor(out=outr[:, TL:T].rearrange("p (c t) -> p c t", t=64), in0=tmp[:, 0:4, :], in1=sg[:, 0:4, :], op=ALU.mult), reads=[B_tmp, B_sg], writes=[B_outr])
                S.op("dve", lambda e: e.tensor_tensor(out=outr[:, 0:TL].rearrange("p (r c) -> p c r", c=64), in0=tmp[:, 4:68, :], in1=sg[:, 4:68, :], op=ALU.mult), reads=[B_tmp, B_sg], writes=[B_outr])
                S.dma("sp", lambda e: e.dma_start(out=g.mT[hd * 128:(hd + 1) * 128, :], in_=outr[:]), reads=[B_outr], writes=[g.B_mT])
                if "o" in g.taps and l == 0:
                    S.dma("sp", lambda e: e.dma_start(out=g.taps["o"][hd], in_=o_sb[:].rearrange("p c t -> p (c t)")), reads=[B_o], writes=[g.B_tap])
                g.flush()


def phase_rg(g, l):
    S, nc = g.S, g.nc
    Bc = g.B_const
    NT = 512
    tiles = tok_tiles(NT)
    with contextlib.ExitStack() as st0:
        def sb0(name, shape, dt):
            return st0.enter_context(nc.sbuf_tensor(_u() + name, shape, dt))
        gw = sb0("rg_gw", [128, 2, 2, 8, 128], BF16)
        B_gw = Buf()
        gwf = gw[:].rearrange("p a b n d -> p (a b n d)")
        for q4 in range(4):
            S.dma("pool", lambda e, q4=q4: e.dma_start(out=gwf[:, q4 * 1024:(q4 + 1) * 1024], in_=g.rgw[l, :, q4 * 1024:(q4 + 1) * 1024]), writes=[B_gw])
        wts = [sb0("rg_w%d" % i, [128, 16, 256], BF16) for i in range(2)]
        hts = [sb0("rg_h%d" % i, [128, 16, NT], BF16) for i in range(2)]
        rxl = sb0("rg_rxl", [128, TL + 3], F32)
        rxc = sb0("rg_rxc", [128, TC + 3], F32)
        gg = sb0("rg_gg", [128, T], BF16)
        xc = sb0("rg_xc", [128, T], F32)
        xcb = sb0("rg_xcb", [128, T], BF16)
        av = sb0("rg_a", [128, T], F32)
        bt = sb0("rg_bt", [128, T], F32)
        hs = [sb0("rg_hs%d" % d, [128, T], F32) for d in range(2)]
        t1 = sb0("rg_t1", [128, NT], F32)
        t2 = sb0("rg_t2", [128, NT], F32)
        t3 = sb0("rg_t3", [128, NT], F32)
        outr = sb0("rg_outr", [128, T], BF16)
        B_w, B_h = [Buf(), Buf()], [Buf(), Buf()]
        B_rx, B_gg, B_xc, B_xcb, B_a, B_bt = Buf(), Buf(), Buf(), Buf(), Buf(), Buf()
        B_hs = [Buf(), Buf()]
        B_t1, B_t2, B_t3, B_outr = Buf(), Buf(), Buf(), Buf()
        for n_ in range(8):
            wj = n_ % 2
            src = g.w_rg[l, n_].rearrange("(kc p) n -> p kc n", p=128)
            for q2 in range(2):
                S.dma("pool", lambda e, wj=wj, src=src, q2=q2: e.dma_start(out=wts[wj][:, q2 * 8:(q2 + 1) * 8, :], in_=src[:, q2 * 8:(q2 + 1) * 8, :]), writes=[B_w[wj]])
            S.op("pool", lambda e: e.memset(rxl[:, 0:2], 0.0), writes=[B_rx])
            S.op("pool", lambda e: e.memset(rxl[:, TL + 2:TL + 3], 0.0), writes=[B_rx])
            S.op("pool", lambda e: e.memset(rxc[:, 0:2], 0.0), writes=[B_rx])
            S.op("pool", lambda e: e.memset(rxc[:, TC + 2:TC + 3], 0.0), writes=[B_rx])
            for i, (t0, n, which) in enumerate(tiles):
                j = i % 2
                hsrc = g.hT[:, t0:t0 + n].rearrange("(kc p) t -> p kc t", p=128)
                S.dma("sp", lambda e, j=j, hsrc=hsrc, n=n: e.dma_start(out=hts[j][:, :, :n], in_=hsrc), reads=tb(g.B_hT_t, t0, t0 + n), writes=[B_h[j]])
                for blk in range(2):
                    ps, Bp = g.ps[blk], g.B_ps[blk]
                    for kc in range(16):
                        S.op("pe", lambda e, ps=ps, kc=kc, blk=blk, j=j, n=n, wj=wj: e.matmul(
                            ps[:, :n], lhsT=wts[wj][:, kc, blk * 128:(blk + 1) * 128], rhs=hts[j][:, kc, :n],
                            start=(kc == 0), stop=(kc == 15)), reads=[B_w[wj], B_h[j]], writes=[Bp])
                    if blk == 0:
                        dst = rxl[:, 2 + t0:2 + t0 + n] if which == 0 else rxc[:, 2 + t0 - TL:2 + t0 - TL + n]
                        S.op("act", lambda e, ps=ps, n=n, dst=dst: e.activation(out=dst, in_=ps[:, :n], func=AF.Copy), reads=[Bp], writes=[B_rx])
                    else:
                        S.op("act", lambda e, ps=ps, n=n, t0=t0: e.activation(out=gg[:, t0:t0 + n], in_=ps[:, :n], func=AF.Gelu), reads=[Bp], writes=[B_gg])
            cw = [g.vecs[:, VO["rcw"] + l * 32 + k * 8 + n_: VO["rcw"] + l * 32 + k * 8 + n_ + 1] for k in range(4)]
            cb = g.vecs[:, VO["rcb"] + l * 8 + n_: VO["rcb"] + l * 8 + n_ + 1]
            for (rx, o0, nn) in ((rxl, 0, TL), (rxc, TL, TC)):
                S.op("dve", lambda e, rx=rx, o0=o0, nn=nn, c0_=cw[0], cb=cb: e.tensor_scalar(out=xc[:, o0:o0 + nn], in0=rx[:, 0:nn], scalar1=c0_, scalar2=cb, op0=ALU.mult, op1=ALU.add), reads=[B_rx, Bc], writes=[B_xc])
                for k in range(1, 4):
                    S.op("dve", lambda e, rx=rx, o0=o0, nn=nn, k=k, ck=cw[k]: e.scalar_tensor_tensor(out=xc[:, o0:o0 + nn], in0=rx[:, k:k + nn], scalar=ck, in1=xc[:, o0:o0 + nn], op0=ALU.mult, op1=ALU.add), reads=[B_rx, B_xc, Bc], writes=[B_xc])
            S.op("act", lambda e: e.activation(out=xcb[:], in_=xc[:], func=AF.Copy), reads=[B_xc], writes=[B_xcb])
            for d in range(2):
                ba = g.vecs[:, VO["rba"] + l * 16 + d * 8 + n_: VO["rba"] + l * 16 + d * 8 + n_ + 1]
                bx = g.vecs[:, VO["rbx"] + l * 16 + d * 8 + n_: VO["rbx"] + l * 16 + d * 8 + n_ + 1]
                nsp = g.nsp8[:, l, d * 8 + n_: d * 8 + n_ + 1]
                for (t0, n, which) in tiles:
                    S.op("pe", lambda e, d=d, t0=t0, n=n, n_=n_: e.matmul(g.ps[2][:, :n], lhsT=gw[:, d, 0, n_, :], rhs=xcb[:, t0:t0 + n], start=True, stop=True), reads=[B_gw, B_xcb], writes=[g.B_ps[2]])
                    S.op("pe", lambda e, d=d, t0=t0, n=n, n_=n_: e.matmul(g.ps[3][:, :n], lhsT=gw[:, d, 1, n_, :], rhs=xcb[:, t0:t0 + n], start=True, stop=True), reads=[B_gw, B_xcb], writes=[g.B_ps[3]])
                    S.op("act", lambda e, n=n, ba=ba: e.activation(out=t1[:, :n], in_=g.ps[2][:, :n], func=AF.Sigmoid, bias=ba, scale=1.0), reads=[g.B_ps[2], Bc], writes=[B_t1])
                    S.op("act", lambda e, n=n, t0=t0, nsp=nsp: e.activation(out=av[:, t0:t0 + n], in_=t1[:, :n], func=AF.Exp, scale=nsp), reads=[B_t1, Bc], writes=[B_a])
                    S.op("act", lambda e, n=n, t0=t0: e.activation(out=t2[:, :n], in_=av[:, t0:t0 + n], func=AF.Square), reads=[B_a], writes=[B_t2])
                    S.op("act", lambda e, n=n: e.activation(out=t2[:, :n], in_=t2[:, :n], func=AF.Sqrt, bias=1.0, scale=-1.0), reads=[B_t2], writes=[B_t2])
                    S.op("act", lambda e, n=n, bx=bx: e.activation(out=t3[:, :n], in_=g.ps[3][:, :n], func=AF.Sigmoid, bias=bx, scale=1.0), reads=[g.B_ps[3], Bc], writes=[B_t3])
                    S.op("dve", lambda e, n=n: e.tensor_tensor(out=t2[:, :n], in0=t2[:, :n], in1=t3[:, :n], op=ALU.mult), reads=[B_t2, B_t3], writes=[B_t2])
                    S.op("dve", lambda e, n=n, t0=t0: e.tensor_tensor(out=bt[:, t0:t0 + n], in0=t2[:, :n], in1=xc[:, t0:t0 + n], op=ALU.mult), reads=[B_t2, B_xc], writes=[B_bt])
                if d == 0:
                    S.op("dve", lambda e: e.tensor_tensor_scan(out=hs[0][:, TL:T], data0=av[:, TL:T], data1=bt[:, TL:T], initial=0.0, op0=ALU.mult, op1=ALU.add), reads=[B_a, B_bt], writes=[B_hs[0]])
                    S.op("dve", lambda e: e.tensor_tensor_scan(out=hs[0][:, 0:TL], data0=av[:, 0:TL], data1=bt[:, 0:TL], initial=hs[0][:, T - 1:T], op0=ALU.mult, op1=ALU.add), reads=[B_a, B_bt, B_hs[0]], writes=[B_hs[0]], force_same=True)
                else:
                    S.op("dve", lambda e: e.tensor_tensor_scan(out=hs[1][:, TL:T][:, ::-1], data0=av[:, TL:T][:, ::-1], data1=bt[:, TL:T][:, ::-1], initial=0.0, op0=ALU.mult, op1=ALU.add), reads=[B_a, B_bt], writes=[B_hs[1]])
                    S.op("dve", lambda e: e.tensor_tensor_scan(out=hs[1][:, 0:TL][:, ::-1], data0=av[:, 0:TL][:, ::-1], data1=bt[:, 0:TL][:, ::-1], initial=hs[1][:, TL:TL + 1], op0=ALU.mult, op1=ALU.add), reads=[B_a, B_bt, B_hs[1]], writes=[B_hs[1]], force_same=True)
            S.op("dve", lambda e: e.tensor_tensor(out=hs[0][:], in0=hs[0][:], in1=hs[1][:], op=ALU.add), reads=[B_hs[0], B_hs[1]], writes=[B_hs[0]])
            S.op("dve", lambda e: e.tensor_tensor(out=outr[:], in0=hs[0][:], in1=gg[:], op=ALU.mult), reads=[B_hs[0], B_gg], writes=[B_outr])
            S.dma("sp", lambda e, n_=n_: e.dma_start(out=g.mT[1024 + n_ * 128:1024 + (n_ + 1) * 128, :], in_=outr[:]), reads=[B_outr], writes=[g.B_mT])
        g.flush()


def phase_a(g, l, last):
    S, nc = g.S, g.nc
    Bc = g.B_const
    NT = 256
    with contextlib.ExitStack() as st:
        def sb(name, shape, dt):
            return st.enter_context(nc.sbuf_tensor(_u() + name, shape, dt))
        wo = sb("a_wo", [128, 16, D], BF16)
        mts = [sb("a_m%d" % i, [128, 16, NT], BF16) for i in range(2)]
        xts = [sb("a_x%d" % i, [128, 16, NT], F32) for i in range(2)]
        hto = [sb("a_h%d" % i, [128, 16, NT], BF16) for i in range(2)]
        mix = sb("a_mix", [128, 16, NT], F32)
        sq = sb("a_sq", [128, 16, NT], BF16)
        tmp = sb("a_tmp", [128, NT], F32)
        rstd = sb("a_rstd", [128, NT], F32)
        B_wo, B_m, B_x, B_h = Buf(), [Buf(), Buf()], [Buf(), Buf()], [Buf(), Buf()]
        B_mix, B_sq, B_tmp, B_rstd = Buf(), Buf(), Buf(), Buf()
        src = g.w_out[l].rearrange("(kc p) n -> p kc n", p=128)
        for kc4 in range(0, 16, 4):
            for hh in range(4):
                S.dma("pool", lambda e, kc4=kc4, hh=hh: e.dma_start(out=wo[:, kc4:kc4 + 4, hh * 512:(hh + 1) * 512], in_=src[:, kc4:kc4 + 4, hh * 512:(hh + 1) * 512]), writes=[B_wo])
        for jf in range(44):
            csrc = g.w_up[l, jf].rearrange("(kc p) n -> p kc n", p=128)
            cdst = g.wub[jf].rearrange("p (kc n) -> p kc n", n=256)
            for q2 in range(2):
                S.dma("pool", lambda e, csrc=csrc, cdst=cdst, q2=q2: e.dma_start(out=cdst[:, q2 * 8:(q2 + 1) * 8, :], in_=csrc[:, q2 * 8:(q2 + 1) * 8, :]), writes=[g.B_wub[jf]])
        for ob in range(16):
            csrc = g.w_dn[l, ob].rearrange("(fc p) n -> p fc n", p=128)
            cdst = g.wdb[ob].rearrange("p (fc n) -> p fc n", n=128)
            for q2 in range(4):
                S.dma("pool", lambda e, csrc=csrc, cdst=cdst, q2=q2: e.dma_start(out=cdst[:, q2 * 11:(q2 + 1) * 11, :], in_=csrc[:, q2 * 11:(q2 + 1) * 11, :]), writes=[g.B_wdb[ob]])
        tiles = tok_tiles(NT)
        if last:
            tiles = [t for t in tiles if t[2] == 0]
        import os
        AB = int(os.environ.get("MK_AB", "99"))
        if AB < 99:
            tiles = tiles[:1]
        def issue_load(i):
            t0, n, which = tiles[i]
            j = i % 2
            msrc = g.mT[:, t0:t0 + n].rearrange("(kc p) t -> p kc t", p=128)
            xsrc = g.res[:, t0:t0 + n].rearrange("(kc p) t -> p kc t", p=128)
            S.dma("sp", lambda e, j=j, msrc=msrc: e.dma_start(out=mts[j][:], in_=msrc), reads=[g.B_mT], writes=[B_m[j]])
            S.dma("sp", lambda e, j=j, xsrc=xsrc: e.dma_start(out=xts[j][:], in_=xsrc), reads=tb(g.B_res_t, t0, t0 + n), writes=[B_x[j]])
        issue_load(0)
        for i, (t0, n, which) in enumerate(tiles):
            j = i % 2
            if i + 1 < len(tiles):
                issue_load(i + 1)
            if AB < 1:
                continue
            for ob in range(16):
                pb = ob % 4
                ps, Bp = g.ps[pb], g.B_ps[pb]
                for kc in range(16):
                    S.op("pe", lambda e, ps=ps, kc=kc, ob=ob, j=j: e.matmul(ps[:, :NT], lhsT=wo[:, kc, ob * 128:(ob + 1) * 128], rhs=mts[j][:, kc, :],
                                                                             start=(kc == 0), stop=(kc == 15)), reads=[B_wo, B_m[j]], writes=[Bp])
                S.op("dve", lambda e, ps=ps, ob=ob: e.tensor_copy(out=mix[:, ob, :], in_=ps[:, :NT]), reads=[Bp], writes=[B_mix])
                S.op("act", lambda e, ob=ob: e.activation(out=sq[:, ob, :], in_=mix[:, ob, :], func=AF.Square), reads=[B_mix], writes=[B_sq])
            if AB < 2:
                continue
            ps, Bp = g.ps[4], g.B_ps[4]
            for kc in range(16):
                S.op("pe", lambda e, kc=kc, ps=ps: e.matmul(ps[:, :NT], lhsT=g.ones_bf[:], rhs=sq[:, kc, :], start=(kc == 0), stop=(kc == 15)), reads=[B_sq, Bc], writes=[Bp])
            S.op("act", lambda e, ps=ps: e.activation(out=rstd[:], in_=ps[:, :NT], func=AF.Sqrt, bias=g.epsb[:, 0:1], scale=1.0 / D), reads=[Bp, Bc], writes=[B_rstd])
            S.op("dve", lambda e: e.reciprocal(out=rstd[:], in_=rstd[:]), reads=[B_rstd], writes=[B_rstd])
            if AB < 3:
                continue
            G = g.G1[:, l, :, which]
            for kc in range(16):
                S.op("dve", lambda e, kc=kc, gk=G[:, kc:kc + 1]: e.scalar_tensor_tensor(out=tmp[:], in0=mix[:, kc, :], scalar=gk, in1=rstd[:], op0=ALU.mult, op1=ALU.mult), reads=[B_mix, B_rstd, Bc], writes=[B_tmp])
                S.op("dve", lambda e, kc=kc, j=j: e.tensor_tensor(out=xts[j][:, kc, :], in0=xts[j][:, kc, :], in1=tmp[:], op=ALU.add), reads=[B_tmp, B_x[j]], writes=[B_x[j]])
            if AB < 4:
                continue
            xdst = g.res[:, t0:t0 + n].rearrange("(kc p) t -> p kc t", p=128)
            S.dma("sp", lambda e, j=j, xdst=xdst: e.dma_start(out=xdst, in_=xts[j][:]), reads=[B_x[j]], writes=tb(g.B_res_t, t0, t0 + n))
            if AB < 5:
                continue
            emit_norm_mod(g, xts[j], B_x[j], NT, g.A2[:, l, :, which], g.modv[:, l, 48:64, which], sq, B_sq, tmp, B_tmp, rstd, B_rstd, hto[j], B_h[j], 5)
            hdst = g.h2T[:, t0:t0 + n].rearrange("(kc p) t -> p kc t", p=128)
            S.dma("sp", lambda e, j=j, hdst=hdst: e.dma_start(out=hdst, in_=hto[j][:]), reads=[B_h[j]], writes=tb(g.B_h2T_t, t0, t0 + n))
            if "xa" in g.taps and l == 0:
                S.dma("sp", lambda e, j=j, t0=t0, n=n: e.dma_start(out=g.taps["xa"][:, t0:t0 + n].rearrange("(kc p) t -> p kc t", p=128), in_=xts[j][:]), reads=[B_x[j]], writes=[g.B_tap])
        g.flush()


def phase_b(g, l, last, final):
    S, nc = g.S, g.nc
    Bc = g.B_const
    TS = 512
    NS = 256
    with contextlib.ExitStack() as st:
        def sb(name, shape, dt):
            return st.enter_context(nc.sbuf_tensor(_u() + name, shape, dt))
        h2 = sb("b_h2", [128, 16, TS + 2], BF16)
        ge = sb("b_ge", [128, 44, TS], BF16)
        wup = [sb("b_wu%d" % i, [128, 16, 256], BF16) for i in range(2)]
        wdn = [sb("b_wd%d" % i, [128, 44, 128], BF16) for i in range(2)]
        fl = sb("b_fl", [128, 16, TS], F32)
        sq = sb("b_sq", [128, 16, NS], BF16)
        xt = sb("b_x", [128, 16, NS], F32)
        hn = sb("b_hn", [128, 16, NS], BF16)
        ca2 = [sb("b_ca%d" % i, [128, NS], F32) for i in range(2)]
        cv2 = [sb("b_cv%d" % i, [128, NS], F32) for i in range(2)]
        B_ca2 = [Buf(), Buf()]
        B_cv2 = [Buf(), Buf()]
        cvi = [0]
        tmp = sb("b_tmp", [128, NS], F32)
        rstd = sb("b_rstd", [128, NS], F32)
        B_h2, B_ge, B_wu, B_wd = Buf(), Buf(), [Buf(), Buf()], [Buf(), Buf()]
        B_fl, B_sq, B_x, B_hn, B_tmp, B_rstd = Buf(), Buf(), Buf(), Buf(), Buf(), Buf()
        stiles = [(t0, TS, 0) for t0 in range(0, TL, TS)]
        if not last:
            stiles.append((TL, TC, 1))
        wi = 0
        di = 0
        import os
        if os.environ.get("MK_NST"):
            stiles = stiles[:int(os.environ["MK_NST"])]
        def issue_h2(idx):
            t0, n, which = stiles[idx]
            seq0 = 0 if which == 0 else TL
            seq1 = TL if which == 0 else T
            lo = max(t0 - 1, seq0)
            hi = min(t0 + n + 1, seq1)
            if lo == t0:
                S.op("dve", lambda e: e.memset(h2[:, :, 0:1], 0.0), writes=[B_h2])
            if hi == t0 + n:
                S.op("dve", lambda e, n=n: e.memset(h2[:, :, n + 1:n + 2], 0.0), writes=[B_h2])
            hsrc = g.h2T[:, lo:hi].rearrange("(kc p) t -> p kc t", p=128)
            S.dma("sp", lambda e, hsrc=hsrc, lo=lo, hi=hi, t0=t0: e.dma_start(out=h2[:, :, lo - (t0 - 1):hi - (t0 - 1)], in_=hsrc), reads=tb(g.B_h2T_t, lo, hi), writes=[B_h2])
        issue_h2(0)
        for sidx, (t0, n, which) in enumerate(stiles):
            S.same = "b" not in S.off
            nsub = n // NS
            import os
            BB = int(os.environ.get("MK_BB", "99"))
            if BB < 1:
                g.flush()
                continue
            for jf in range(44):
                wj = wi % 2
                wi += 1
                S.dma("sp", lambda e, wj=wj, jf=jf: e.dma_start(out=wup[wj][:].rearrange("p kc n -> p (kc n)"), in_=g.wub[jf]), reads=[g.B_wub[jf]], writes=[B_wu[wj]])
                fw = [[g.vecs[:, VO["fcw"] + l * 264 + k * 88 + half * 44 + jf: VO["fcw"] + l * 264 + k * 88 + half * 44 + jf + 1] for k in range(3)] for half in range(2)]
                fb = [g.vecs[:, VO["fcb"] + l * 88 + half * 44 + jf: VO["fcb"] + l * 88 + half * 44 + jf + 1] for half in range(2)]
                for s in range(nsub):
                    c0 = s * NS
                    ci = cvi[0] % 2
                    cvi[0] += 1
                    ca, cv, B_ca, B_cv = ca2[ci], cv2[ci], B_ca2[ci], B_cv2[ci]
                    for half in range(2):
                        pb = (s * 2 + half) % 4
                        ps, Bp = g.ps[pb], g.B_ps[pb]
                        HH = int(os.environ.get("MK_HH", "2"))
                        for kc in range(16):
                            S.op("pe", lambda e, ps=ps, kc=kc, wj=wj, half=half, c0=c0: e.matmul(
                                ps[:, :NS + HH], lhsT=wup[wj][:, kc, half * 128:(half + 1) * 128], rhs=h2[:, kc, c0:c0 + NS + HH],
                                start=(kc == 0), stop=(kc == 15)), reads=[B_wu[wj], B_h2], writes=[Bp])
                        dst, Bd = (ca, B_ca) if half == 0 else (cv, B_cv)
                        if os.environ.get("MK_EE", "") == "noact":
                            continue
                        S.op("act", lambda e, ps=ps, dst=dst, half=half, fw=fw, fb=fb: e.activation(out=dst[:], in_=ps[:, HH // 2:NS + HH // 2], func=AF.Identity, bias=fb[half], scale=fw[half][1]), reads=[Bp, Bc], writes=[Bd])
                        EE = os.environ.get("MK_EE", "")
                        if EE == "noact2":
                            continue
                        S.op("dve", lambda e, ps=ps, dst=dst, half=half, fw=fw: e.scalar_tensor_tensor(out=dst[:], in0=ps[:, 0:NS], scalar=fw[half][0], in1=dst[:], op0=ALU.mult, op1=ALU.add), reads=[Bp, Bd, Bc], writes=[Bd])
                        if EE == "one":
                            continue
                        S.op("dve", lambda e, ps=ps, dst=dst, half=half, fw=fw: e.scalar_tensor_tensor(out=dst[:], in0=ps[:, 2:NS + 2], scalar=fw[half][2], in1=dst[:], op0=ALU.mult, op1=ALU.add), reads=[Bp, Bd, Bc], writes=[Bd])
                    if BB < 2:
                        continue
                    S.op("act", lambda e, ca=ca: e.activation(out=ca[:], in_=ca[:], func=AF.Gelu), reads=[B_ca], writes=[B_ca])
                    S.op("dve", lambda e, jf=jf, c0=c0, ca=ca, cv=cv: e.tensor_tensor(out=ge[:, jf, c0:c0 + NS], in0=ca[:], in1=cv[:], op=ALU.mult), reads=[B_ca, B_cv], writes=[B_ge])
            if BB < 3:
                g.flush()
                continue
            for ob in range(16):
                dj = di % 2
                di += 1
                S.dma("sp", lambda e, dj=dj, ob=ob: e.dma_start(out=wdn[dj][:].rearrange("p fc n -> p (fc n)"), in_=g.wdb[ob]), reads=[g.B_wdb[ob]], writes=[B_wd[dj]])
                for s in range(nsub):
                    c0 = s * NS
                    pb = 4 + (s % 2)
                    ps, Bp = g.ps[pb], g.B_ps[pb]
                    for fc in range(44):
                        S.op("pe", lambda e, ps=ps, fc=fc, dj=dj, c0=c0: e.matmul(ps[:, :NS], lhsT=wdn[dj][:, fc, :], rhs=ge[:, fc, c0:c0 + NS], start=(fc == 0), stop=(fc == 43)),
                             reads=[B_wd[dj], B_ge], writes=[Bp])
                    S.op("act", lambda e, ps=ps, ob=ob, c0=c0: e.activation(out=fl[:, ob, c0:c0 + NS], in_=ps[:, :NS], func=AF.Copy), reads=[Bp], writes=[B_fl])
            if sidx + 1 < len(stiles):
                issue_h2(sidx + 1)
            S.same = True
            for s in range(nsub):
                c0 = s * NS
                tt = t0 + c0
                xsrc = g.res[:, tt:tt + NS].rearrange("(kc p) t -> p kc t", p=128)
                S.dma("sp", lambda e, xsrc=xsrc: e.dma_start(out=xt[:], in_=xsrc), reads=tb(g.B_res_t, tt, tt + NS), writes=[B_x])
                for kc in range(16):
                    S.op("act", lambda e, kc=kc, c0=c0: e.activation(out=sq[:, kc, :], in_=fl[:, kc, c0:c0 + NS], func=AF.Square), reads=[B_fl], writes=[B_sq])
                ps, Bp = g.ps[6], g.B_ps[6]
                for kc in range(16):
                    S.op("pe", lambda e, kc=kc, ps=ps: e.matmul(ps[:, :NS], lhsT=g.ones_bf[:], rhs=sq[:, kc, :], start=(kc == 0), stop=(kc == 15)), reads=[B_sq, Bc], writes=[Bp])
                S.op("act", lambda e, ps=ps: e.activation(out=rstd[:], in_=ps[:, :NS], func=AF.Sqrt, bias=g.epsb[:, 0:1], scale=1.0 / D), reads=[Bp, Bc], writes=[B_rstd])
                S.op("dve", lambda e: e.reciprocal(out=rstd[:], in_=rstd[:]), reads=[B_rstd], writes=[B_rstd])
                G = g.G2[:, l, :, which]
                for kc in range(16):
                    S.op("dve", lambda e, kc=kc, c0=c0, gk=G[:, kc:kc + 1]: e.scalar_tensor_tensor(out=tmp[:], in0=fl[:, kc, c0:c0 + NS], scalar=gk, in1=rstd[:], op0=ALU.mult, op1=ALU.mult), reads=[B_fl, B_rstd, Bc], writes=[B_tmp])
                    S.op("dve", lambda e, kc=kc: e.tensor_tensor(out=xt[:, kc, :], in0=xt[:, kc, :], in1=tmp[:], op=ALU.add), reads=[B_tmp, B_x], writes=[B_x])
                if final:
                    if which == 0:
                        ydst = g.y[:, tt:tt + NS].rearrange("(kc p) t -> p kc t", p=128)
                        S.dma("sp", lambda e, ydst=ydst: e.dma_start(out=ydst, in_=xt[:]), reads=[B_x], writes=[g.B_y])
                else:
                    xdst = g.res[:, tt:tt + NS].rearrange("(kc p) t -> p kc t", p=128)
                    S.dma("sp", lambda e, xdst=xdst: e.dma_start(out=xdst, in_=xt[:]), reads=[B_x], writes=tb(g.B_res_t, tt, tt + NS))
                    emit_norm_mod(g, xt, B_x, NS, g.A1[:, l + 1, :, which], g.modv[:, l + 1, 0:16, which], sq, B_sq, tmp, B_tmp, rstd, B_rstd, hn, B_hn, 6)
                    hdst = g.hT[:, tt:tt + NS].rearrange("(kc p) t -> p kc t", p=128)
                    S.dma("sp", lambda e, hdst=hdst: e.dma_start(out=hdst, in_=hn[:]), reads=[B_hn], writes=tb(g.B_hT_t, tt, tt + NS))
        S.same = True
        g.flush()


def fm(a, inner):
    a = np.asarray(a, dtype=np.float32)
    lead = a.shape[:-1]
    x = a.shape[-1] // 128
    a = a.reshape(lead + (x, 128))
    a = np.moveaxis(a, -1, 0)
    return np.ascontiguousarray(a).reshape(128, -1)


def prep_inputs(inp):
    f32 = np.float32
    vec = np.zeros((128, NV), f32)

    def put(name, arr):
        vec[:, VO[name]:VO[name] + arr.shape[1]] = arr
    put("b_mod", fm(inp["b_mod"], 96))
    put("norm_g", fm(inp["norm_g"], 16))
    put("lb", fm(inp["hg_lower_bounds"], 8))
    put("hgg", fm(inp["hg_norm_g"], 8))
    put("rcw", fm(inp["rg_conv_w"], 8))
    put("rcb", fm(inp["rg_conv_b"], 8))
    put("rba", fm(inp["rg_ba"], 8))
    put("rbx", fm(inp["rg_bx"], 8))
    put("rlam", fm(inp["rg_lambda"], 8))
    put("fcw", fm(inp["ffn_conv_w"], 88))
    put("fcb", fm(inp["ffn_conv_b"], 88))
    wa = np.asarray(inp["rg_wa"], f32)
    wx = np.asarray(inp["rg_wx"], f32)
    rgw = np.stack([wa, wx], axis=2)
    rgw = np.ascontiguousarray(rgw.transpose(0, 4, 1, 2, 3, 5)).reshape(NL, 128, 4096)
    w_in = np.asarray(inp["w_in"], f32)
    hgc = w_in[:, :, :5120].reshape(NL, D, 5, 8, 128)
    w_hg = np.ascontiguousarray(hgc.transpose(0, 3, 1, 2, 4)).reshape(NL, 8, D, 640)
    rgc = w_in[:, :, 5120:].reshape(NL, D, 2, 8, 128)
    w_rg = np.ascontiguousarray(rgc.transpose(0, 3, 1, 2, 4)).reshape(NL, 8, D, 256)
    w_up = np.asarray(inp["ffn_w_up"], f32).reshape(NL, D, 2, 44, 128)
    w_up = np.ascontiguousarray(w_up.transpose(0, 3, 1, 2, 4)).reshape(NL, 44, D, 256)
    w_dn = np.asarray(inp["ffn_w_down"], f32).reshape(NL, DFF, 16, 128)
    w_dn = np.ascontiguousarray(w_dn.transpose(0, 2, 1, 3))
    shared = dict(vec=vec, rgw=rgw, w_mod=np.ascontiguousarray(inp["w_mod"], dtype=f32), w_hg=w_hg, w_rg=w_rg,
                  w_out=np.ascontiguousarray(inp["w_out"], dtype=f32), w_up=w_up, w_dn=w_dn)
    x = np.asarray(inp["x"], f32)
    ctx = np.asarray(inp["ctx"], f32)
    c = np.asarray(inp["c"], f32)
    c_ctx = np.asarray(inp["c_ctx"], f32)
    per = []
    for b in range(4):
        res0 = np.ascontiguousarray(np.concatenate([x[b].T, ctx[b].T], axis=1))
        cv = np.stack([c[b], c_ctx], axis=-1).reshape(16, 128, 2).transpose(1, 0, 2)
        per.append(dict(res0=res0, cvec=np.ascontiguousarray(cv)))
    return shared, per


_NC_CACHE = {}


def kernel(**inputs):
    shared, per = prep_inputs(inputs)
    if "nc" not in _NC_CACHE:
        _NC_CACHE["nc"] = build()
    nc = _NC_CACHE["nc"]
    in_maps = []
    for core in range(8):
        m = dict(shared)
        m.update(per[core % 4])
        in_maps.append(m)
    res = run_bass_kernel_spmd(nc, in_maps, core_ids=list(range(8)))
    out = np.stack([res.results[b]["y"].T for b in range(4)], axis=0)
    return np.ascontiguousarray(out.astype(np.float32))
```
